# Optimizing a Trainium2 kernel written in Bass

```python
import math
import jax, jax.numpy as jnp
from jax import lax
import numpy as np

D_MODEL = 4096
BATCH = 8
SEQ = 2048
DEPTH = 2

BLOCK = 128
EPS = 1e-6
NEG = -1e30
A_HEADS = 16
A_Q_LORA = 1536
A_KV_LORA = 512
A_NOPE = 128
A_ROPE = 64
A_V = 128
A_WIDTH = A_HEADS * A_V
ROPE_THETA = 10000.0
B_HEADS = 32
B_KV_HEADS = 4
B_HD = 64
B_WIDTH = B_HEADS * B_HD
WINDOW = 128
C_HEADS = 16
C_HD = 128
C_WIDTH = C_HEADS * C_HD
REL_BUCKETS = 32
REL_MAX_DIST = 128

IN_SIZES = (A_Q_LORA, A_KV_LORA, A_ROPE, A_WIDTH,
            B_WIDTH, B_KV_HEADS * B_HD, B_KV_HEADS * B_HD, B_WIDTH,
            C_WIDTH, C_WIDTH, C_WIDTH, C_HEADS, C_WIDTH)
IN_DIM = sum(IN_SIZES)
N_BRANCH = 3

kernel_name = "hybrid_mla_swa_fox_gated_block"


def rms_norm(x, g):
    xf = x.astype(jnp.float32)
    y = xf * lax.rsqrt(jnp.mean(xf * xf, axis=-1, keepdims=True) + EPS)
    return (y * g.astype(jnp.float32)).astype(x.dtype)


def split_columns(h):
    parts, start = [], 0
    for n in IN_SIZES:
        parts.append(h[..., start:start + n])
        start += n
    return parts


def apply_rope(x, positions):
    half = x.shape[-1] // 2
    inv_freq = ROPE_THETA ** (-jnp.arange(half, dtype=jnp.float32) / half)
    ang = positions.astype(jnp.float32)[..., None] * inv_freq
    cos = jnp.cos(ang)[:, :, None, :]
    sin = jnp.sin(ang)[:, :, None, :]
    xf = x.astype(jnp.float32)
    x1, x2 = xf[..., :half], xf[..., half:]
    return jnp.concatenate([x1 * cos - x2 * sin, x2 * cos + x1 * sin], axis=-1).astype(x.dtype)


def t5_causal_bucket(dist):
    max_exact = REL_BUCKETS // 2
    d = jnp.maximum(dist, 0)
    large = max_exact + (jnp.log(jnp.maximum(d, 1).astype(jnp.float32) / max_exact)
                         / math.log(REL_MAX_DIST / max_exact)
                         * (REL_BUCKETS - max_exact)).astype(jnp.int32)
    large = jnp.minimum(large, REL_BUCKETS - 1)
    return jnp.where(d < max_exact, d, large)


def causal_block_attention(q, k, v, scale, cum_log_f=None):
    b, s, h, dk = q.shape
    dv = v.shape[-1]
    nblk = s // BLOCK
    q_blocks = q.reshape(b, nblk, BLOCK, h, dk).transpose(1, 0, 2, 3, 4)
    key_pos = jnp.arange(s)

    def one_block(i, q_i, cq_i):
        logits = jnp.einsum('bqhd,bshd->bhqs', q_i, k).astype(jnp.float32) * scale
        if cq_i is not None:
            logits = logits + cq_i[..., :, None] - cum_log_f[:, :, None, :]
        q_pos = i * BLOCK + jnp.arange(BLOCK)
        mask = key_pos[None, :] <= q_pos[:, None]
        logits = jnp.where(mask[None, None], logits, NEG)
        p = jax.nn.softmax(logits, axis=-1)
        return jnp.einsum('bhqs,bshd->bqhd', p.astype(v.dtype), v)

    idx = jnp.arange(nblk)
    if cum_log_f is None:
        out = lax.map(lambda a: one_block(a[0], a[1], None), (idx, q_blocks))
    else:
        cq_blocks = cum_log_f.reshape(b, h, nblk, BLOCK).transpose(2, 0, 1, 3)
        out = lax.map(lambda a: one_block(a[0], a[1], a[2]), (idx, q_blocks, cq_blocks))
    return out.transpose(1, 0, 2, 3, 4).reshape(b, s, h, dv)


def sliding_window_attention(q, k, v, sinks, rel_bias):
    b, s, h, d = q.shape
    kvh = k.shape[2]
    g = h // kvh
    nblk = s // BLOCK
    qb = q.reshape(b, nblk, BLOCK, kvh, g, d)
    kb = k.reshape(b, nblk, BLOCK, kvh, d)
    vb = v.reshape(b, nblk, BLOCK, kvh, d)
    pad = ((0, 0), (1, 0), (0, 0), (0, 0), (0, 0))
    k_band = jnp.concatenate([jnp.pad(kb, pad)[:, :-1], kb], axis=2)
    v_band = jnp.concatenate([jnp.pad(vb, pad)[:, :-1], vb], axis=2)
    logits = jnp.einsum('bnqkgd,bnskd->bnkgqs', qb, k_band).astype(jnp.float32) * (d ** -0.5)
    logits = logits + rel_bias.astype(jnp.float32).reshape(kvh, g, BLOCK, 2 * BLOCK)
    dist = jnp.arange(BLOCK)[:, None] + BLOCK - jnp.arange(2 * BLOCK)[None, :]
    key_abs = jnp.arange(nblk)[:, None] * BLOCK - BLOCK + jnp.arange(2 * BLOCK)[None, :]
    valid = ((dist >= 0) & (dist < WINDOW))[None] & (key_abs >= 0)[:, None, :]
    logits = jnp.where(valid[None, :, None, None], logits, NEG)
    sink = jnp.broadcast_to(sinks.astype(jnp.float32).reshape(kvh, g)[None, None, :, :, None, None],
                            logits.shape[:-1] + (1,))
    p = jax.nn.softmax(jnp.concatenate([logits, sink], axis=-1), axis=-1)[..., :-1]
    out = jnp.einsum('bnkgqs,bnskd->bnqkgd', p.astype(v.dtype), v_band)
    return out.reshape(b, s, h * d)


def hybrid_layer(x, positions, rel_bias, pre_g, w_in, q_a_g, kv_a_g, w_uq, w_uk, w_uv,
                 sinks, b_f, w_proj_a, w_proj_b, w_proj_c, w_merge, w_o, post_g):
    b, s, _ = x.shape
    h = rms_norm(x, pre_g)
    proj = h @ w_in
    (a_cq, a_ckv, a_kr, a_z, b_q, b_k, b_v, b_z,
     c_q, c_k, c_v, c_f, c_z) = split_columns(proj)

    cq = rms_norm(a_cq, q_a_g)
    qa = jnp.einsum('bsr,rhd->bshd', cq, w_uq)
    q_a = jnp.concatenate([qa[..., :A_NOPE], apply_rope(qa[..., A_NOPE:], positions)], axis=-1)
    ckv = rms_norm(a_ckv, kv_a_g)
    k_nope = jnp.einsum('bsr,rhd->bshd', ckv, w_uk)
    v_a = jnp.einsum('bsr,rhd->bshd', ckv, w_uv)
    k_rope = apply_rope(a_kr[:, :, None, :], positions)
    k_a = jnp.concatenate([k_nope, jnp.broadcast_to(k_rope, (b, s, A_HEADS, A_ROPE))], axis=-1)
    o_a = causal_block_attention(q_a, k_a, v_a, (A_NOPE + A_ROPE) ** -0.5).reshape(b, s, A_WIDTH)
    o_a = o_a * jax.nn.silu(a_z)

    o_b = sliding_window_attention(b_q.reshape(b, s, B_HEADS, B_HD),
                                   b_k.reshape(b, s, B_KV_HEADS, B_HD),
                                   b_v.reshape(b, s, B_KV_HEADS, B_HD), sinks, rel_bias)
    o_b = o_b * jax.nn.silu(b_z)

    log_f = jax.nn.log_sigmoid((c_f + b_f).astype(jnp.float32))
    cum = jnp.cumsum(log_f, axis=1).transpose(0, 2, 1)
    o_c = causal_block_attention(c_q.reshape(b, s, C_HEADS, C_HD),
                                 c_k.reshape(b, s, C_HEADS, C_HD),
                                 c_v.reshape(b, s, C_HEADS, C_HD), C_HD ** -0.5, cum)
    o_c = o_c.reshape(b, s, C_WIDTH) * jax.nn.silu(c_z)

    gates = jax.nn.sigmoid(h @ w_merge)
    g_a, g_b, g_c = jnp.split(gates, N_BRANCH, axis=-1)
    y = g_a * (o_a @ w_proj_a) + g_b * (o_b @ w_proj_b) + g_c * (o_c @ w_proj_c)
    y = y @ w_o
    return x + rms_norm(y, post_g)


def setup_inputs(seed: int = 0) -> dict:
    key = jax.random.key(seed)
    ks = jax.random.split(key, 20)
    nrm = jax.random.normal
    f32 = jnp.float32
    x = nrm(ks[0], (BATCH, SEQ, D_MODEL), f32)
    positions = jnp.broadcast_to(jnp.arange(SEQ, dtype=jnp.int32)[None, :], (BATCH, SEQ))
    rel_table = 0.5 * nrm(ks[1], (REL_BUCKETS, B_HEADS), f32)
    pre_norm = 1.0 + 0.05 * nrm(ks[2], (DEPTH, D_MODEL), f32)
    w_in = nrm(ks[3], (DEPTH, D_MODEL, IN_DIM), f32) * D_MODEL ** -0.5
    q_a_norm = 1.0 + 0.05 * nrm(ks[4], (DEPTH, A_Q_LORA), f32)
    kv_a_norm = 1.0 + 0.05 * nrm(ks[5], (DEPTH, A_KV_LORA), f32)
    w_uq = nrm(ks[6], (DEPTH, A_Q_LORA, A_HEADS, A_NOPE + A_ROPE), f32) * A_Q_LORA ** -0.5
    w_uk = nrm(ks[7], (DEPTH, A_KV_LORA, A_HEADS, A_NOPE), f32) * A_KV_LORA ** -0.5
    w_uv = nrm(ks[8], (DEPTH, A_KV_LORA, A_HEADS, A_V), f32) * A_KV_LORA ** -0.5
    sinks = 0.5 * nrm(ks[9], (DEPTH, B_HEADS), f32)
    b_f = 1.0 + 0.1 * nrm(ks[10], (DEPTH, C_HEADS), f32)
    w_proj_a = nrm(ks[11], (DEPTH, A_WIDTH, D_MODEL), f32) * A_WIDTH ** -0.5
    w_proj_b = nrm(ks[12], (DEPTH, B_WIDTH, D_MODEL), f32) * B_WIDTH ** -0.5
    w_proj_c = nrm(ks[13], (DEPTH, C_WIDTH, D_MODEL), f32) * C_WIDTH ** -0.5
    w_merge = nrm(ks[14], (DEPTH, D_MODEL, N_BRANCH * D_MODEL), f32) * D_MODEL ** -0.5
    w_o = nrm(ks[15], (DEPTH, D_MODEL, D_MODEL), f32) * D_MODEL ** -0.5
    post_norm = 1.0 + 0.05 * nrm(ks[16], (DEPTH, D_MODEL), f32)
    return {"x": x, "positions": positions, "rel_table": rel_table, "pre_norm": pre_norm,
            "w_in": w_in, "q_a_norm": q_a_norm, "kv_a_norm": kv_a_norm, "w_uq": w_uq,
            "w_uk": w_uk, "w_uv": w_uv, "sinks": sinks, "b_f": b_f, "w_proj_a": w_proj_a,
            "w_proj_b": w_proj_b, "w_proj_c": w_proj_c, "w_merge": w_merge, "w_o": w_o,
            "post_norm": post_norm}


def reference(x, positions, rel_table, pre_norm, w_in, q_a_norm, kv_a_norm, w_uq, w_uk, w_uv,
              sinks, b_f, w_proj_a, w_proj_b, w_proj_c, w_merge, w_o, post_norm):
    dist = jnp.arange(BLOCK)[:, None] + BLOCK - jnp.arange(2 * BLOCK)[None, :]
    rel_bias = rel_table[t5_causal_bucket(dist)].transpose(2, 0, 1)
    for l in range(DEPTH):
        x = hybrid_layer(x, positions, rel_bias, pre_norm[l], w_in[l], q_a_norm[l], kv_a_norm[l],
                         w_uq[l], w_uk[l], w_uv[l], sinks[l], b_f[l], w_proj_a[l], w_proj_b[l],
                         w_proj_c[l], w_merge[l], w_o[l], post_norm[l])
    return x
```

```python
import math
from contextlib import ExitStack
import numpy as np
import ml_dtypes
import concourse.bass as bass
import concourse.mybir as mybir
from concourse.bass_utils import run_bass_kernel_spmd

F32 = mybir.dt.float32
BF16 = mybir.dt.bfloat16
I32 = mybir.dt.int32
AF = mybir.ActivationFunctionType
ALU = mybir.AluOpType
AX = mybir.AxisListType

T = 2048
NT = 16
D = 4096
KC = 32
DEPTH = 2
EPS = 1e-6
IN_SIZES = (1536, 512, 64, 2048, 2048, 256, 256, 2048, 2048, 2048, 2048, 16, 2048)
IN_DIM = sum(IN_SIZES)
OFFS = [0]
for _n in IN_SIZES:
    OFFS.append(OFFS[-1] + _n)
(O_CQ, O_CKV, O_KR, O_AZ, O_BQ, O_BK, O_BV, O_BZ, O_CQ2, O_CK, O_CV, O_CF, O_CZ) = OFFS[:13]
SEM_LIMIT = 30000

ENGS = ["pe", "act", "dve", "pool", "sp"]
BLK = {"pe": "tensor", "act": "scalar", "dve": "vector", "pool": "gpsimd", "sp": "sync"}


class Res:
    __slots__ = ("name", "w", "r", "excl")

    def __init__(self, name, excl=False):
        self.name = name
        self.w = []
        self.r = {}
        self.excl = excl


class Op:
    __slots__ = ("eng", "fn", "deps", "signal", "val", "dma", "ep")


class Prog:
    NSLOT = 8

    def __init__(self, nc, es):
        self.nc = nc
        self.es = es
        self.ops = {e: [] for e in ENGS}
        self.sems = {}
        self.last = {}
        self.dmas = {}

    def _new(self, eng, fn, dma):
        o = Op()
        o.eng = eng
        o.fn = fn
        o.dma = dma
        o.signal = dma is not None
        o.val = None
        o.ep = 0
        o.deps = []
        return o

    def op(self, eng, fn, R=(), W=(), dma=None):
        o = self._new(eng, fn, dma)
        deps = {}
        if dma is not None:
            lst = self.dmas.setdefault(dma, [])
            n = len(lst)
            o.dma = (dma, n % self.NSLOT)
            if n >= self.NSLOT:
                prev = lst[n - self.NSLOT]
                deps[id(prev)] = (prev, True)
            lst.append(o)
        key = o.dma if o.dma else eng
        for r in R:
            for wo in r.w:
                deps[id(wo)] = (wo, True)
            if r.excl:
                for k2, ro in r.r.items():
                    if k2 != key and id(ro) not in deps:
                        deps[id(ro)] = (ro, False)
        appendw = set()
        for w in W:
            if (o.dma is not None and w.w and not w.r
                    and all(x.dma is not None and x.dma[0] == o.dma[0] for x in w.w)):
                appendw.add(id(w))
                continue
            for wo in w.w:
                if id(wo) not in deps:
                    deps[id(wo)] = (wo, False)
            for ro in w.r.values():
                if id(ro) not in deps:
                    deps[id(ro)] = (ro, False)
        for d, raw in deps.values():
            if d is o:
                continue
            if d.dma is None and o.dma is None and d.eng == eng and eng == "pe":
                continue
            d.signal = True
            o.deps.append(d)
        for r in R:
            r.r[key] = o
        for w in W:
            if id(w) in appendw:
                w.w.append(o)
            else:
                w.w = [o]
                w.r = {}
        self.ops[eng].append(o)
        self.last[key] = o
        return o

    def barrier(self, engs=ENGS):
        lasts = list(self.last.values())
        for d in lasts:
            d.signal = True
        for e in engs:
            o = self._new(e, None, None)
            o.deps = list(lasts)
            self.ops[e].append(o)

    def final_wait(self, eng="sp"):
        self.barrier(engs=[eng])

    def emit(self):
        nc = self.nc
        cnt = {}
        for e in ENGS:
            for o in self.ops[e]:
                if not o.signal:
                    continue
                key = o.dma if o.dma else e
                inc = 16 if o.dma else 1
                ep, v = cnt.get(key, (0, 0))
                if v + inc > SEM_LIMIT:
                    ep, v = ep + 1, 0
                v += inc
                cnt[key] = (ep, v)
                o.ep, o.val = ep, v
        for key, (ep, v) in cnt.items():
            nm = key if isinstance(key, str) else f"{key[0]}{key[1]}"
            for k in range(ep + 1):
                self.sems[(key, k)] = self.es.enter_context(nc.semaphore(f"s_{nm}_{k}"))
        block = self.es.enter_context(nc.Block())
        for e in ENGS:
            ops = self.ops[e]

            def body(eng, ops=ops, e=e):
                seen = {}
                for o in ops:
                    need = {}
                    for d in o.deps:
                        k = (d.dma if d.dma else d.eng, d.ep)
                        if d.val > need.get(k, 0):
                            need[k] = d.val
                    for k, v in need.items():
                        if seen.get(k, 0) < v:
                            eng.wait_ge(self.sems[k], v)
                            seen[k] = v
                    if o.fn is None:
                        continue
                    ins = o.fn(eng)
                    if o.signal:
                        key = o.dma if o.dma else e
                        ins.then_inc(self.sems[(key, o.ep)], 16 if o.dma else 1)

            getattr(block, BLK[e])(body)


ARENA_BYTES = 207 * 1024


class Builder:
    def __init__(self, nc, es, io, depth=DEPTH, stop=None, dumps=()):
        self.nc = nc
        self.es = es
        self.io = io
        self.depth = depth
        self.stop = stop
        self.dumps = set(dumps)
        self.P = Prog(nc, es)
        self.arena = es.enter_context(nc.sbuf_tensor("arena", [128, ARENA_BYTES // 2], BF16))
        self.base = 0
        self.off = 0
        self.ps = [es.enter_context(nc.psum_tensor(f"psb{i}", [128, 512], F32)) for i in range(8)]
        self.psr = [Res(f"psb{i}", excl=True) for i in range(8)]
        self.bankc = {}
        self.flip = 0
        self.scr = {}

    def alloc(self, shape, dt, name="t"):
        n = int(np.prod(shape[1:]))
        nb = n * (2 if dt == BF16 else 4)
        o = self.off
        assert o % 4 == 0
        self.off = o + (nb + 63) // 64 * 64
        assert self.off <= ARENA_BYTES, (name, self.off)
        a = self.arena[:, o // 2:(o + nb) // 2]
        if dt != BF16:
            a = a.bitcast(dt)
        if len(shape) == 3:
            a = a.rearrange("p (a b) -> p a b", a=shape[1], b=shape[2])
        elif len(shape) == 4:
            a = a.rearrange("p (a b c) -> p a b c", a=shape[1], b=shape[2], c=shape[3])
        if shape[0] != 128:
            a = a[0:shape[0]]
        return a, Res(name)

    def keep(self):
        self.base = self.off

    def new_phase(self, base=None):
        self.P.barrier()
        if base is not None:
            self.base = base
        self.off = self.base

    def dram(self, name, shape, dt):
        kind = "ExternalOutput" if name in self.dumps else "Internal"
        t = self.nc.dram_tensor(name, list(shape), dt, kind=kind).ap()
        self.scr[name] = t
        return t

    def nb(self, lo=0, hi=8):
        b = self.bankc.get((lo, hi), lo)
        self.bankc[(lo, hi)] = lo + (b + 1 - lo) % (hi - lo)
        return b

    def alt(self):
        self.flip ^= 1
        return "act" if self.flip else "dve"

    def copy(self, eng, out, in_, R, W):
        if eng == "act":
            self.P.op("act", lambda e: e.copy(out=out, in_=in_), R=R, W=W)
        else:
            self.P.op(eng, lambda e: e.tensor_copy(out=out, in_=in_), R=R, W=W)

    def load(self, out, in_, W, q="sp", st="ld"):
        self.P.op(q, lambda e: e.dma_start(out=out, in_=in_), W=W, dma=st)

    def store(self, out, in_, R, q="sp", st="st"):
        self.P.op(q, lambda e: e.dma_start(out=out, in_=in_), R=R, dma=st)

    def transposes(self, src, src_res, cols, dst_fn, dst_res, eng, rows=128, banks=(0, 8)):
        P = self.P
        if isinstance(cols, int):
            cols = [c * rows for c in range(cols)]
        nblk = len(cols)
        g0 = 0
        while g0 < nblk:
            n = min(4, nblk - g0)
            b = self.nb(*banks)
            ps, psr = self.ps[b], self.psr[b]
            for k in range(n):
                c0 = cols[g0 + k]
                P.op("pe", lambda e, ps=ps, k=k, c0=c0: e.matmul(
                    ps[0:rows, k * 128:(k + 1) * 128], lhsT=src[:, c0:c0 + rows], rhs=self.ident,
                    start=True, stop=True), R=[src_res, self.cres], W=[psr])
            dst = dst_fn(g0, n)
            pv = ps[0:rows, 0:n * 128].rearrange("p (a b) -> p a b", a=n, b=128)
            self.copy(eng, dst, pv, [psr], [dst_res])
            g0 += n

    def rstd_from_ss(self, ss, ssr, n, col):
        P = self.P
        P.op("dve", lambda e: e.tensor_scalar(out=ss[:, col + 1:col + 2], in0=ss[:, col:col + 1], scalar1=1.0 / n,
                                              scalar2=EPS, op0=ALU.mult, op1=ALU.add), R=[ssr], W=[ssr])
        P.op("act", lambda e: e.activation(out=ss[:, col + 1:col + 2], in_=ss[:, col + 1:col + 2], func=AF.Sqrt),
             R=[ssr], W=[ssr])
        P.op("dve", lambda e: e.reciprocal(out=ss[:, col + 2:col + 3], in_=ss[:, col + 1:col + 2]), R=[ssr], W=[ssr])

    def gemm(self, AT, ATres, kc, wsrc, ncols, slab_w, evac, tiles=range(NT), banks=(0, 8), wbufs=None):
        P = self.P
        if wbufs is None:
            wbufs = [self.alloc([128, kc, slab_w], BF16, f"wb{i}") for i in range(2)]
        wv = wsrc.rearrange("(c p) n -> p c n", p=128)
        nslab = (ncols + slab_w - 1) // slab_w

        def issue(s):
            c0 = s * slab_w
            w = min(slab_w, ncols - c0)
            wb, wr = wbufs[s % 2]
            step = 8 if kc >= 8 else kc
            for k0 in range(0, kc, step):
                k1 = min(kc, k0 + step)
                P.op("pool", lambda e, wb=wb, k0=k0, k1=k1, c0=c0, w=w: e.dma_start(
                    out=wb[:, k0:k1, 0:w], in_=wv[:, k0:k1, c0:c0 + w]), W=[wr], dma="wld")

        issue(0)
        for s in range(nslab):
            if s + 1 < nslab:
                issue(s + 1)
            c0 = s * slab_w
            w = min(slab_w, ncols - c0)
            wb, wr = wbufs[s % 2]
            for i in tiles:
                b = self.nb(*banks)
                ps, psr = self.ps[b], self.psr[b]
                for c in range(kc):
                    P.op("pe", lambda e, ps=ps, wb=wb, c=c, i=i, w=w: e.matmul(
                        ps[:, 0:w], lhsT=AT[:, c, i * 128:(i + 1) * 128], rhs=wb[:, c, 0:w],
                        start=(c == 0), stop=(c == kc - 1)), R=[wr, ATres[i]], W=[psr])
                evac(i, c0, w, ps, psr)

    def setup(self):
        P, io = self.P, self.io
        self.cres = Res("consts")
        cb, _ = self.alloc([128, 512], BF16, "cstb")
        c32, _ = self.alloc([128, 288], F32, "cst32")
        self.load(cb, io["cstb"], [self.cres])
        self.load(c32, io["cst32"], [self.cres])
        self.ident = cb[:, 0:128]
        self.maskT = cb[:, 128:256]
        self.bmask = cb[:, 256:512]
        self.tri = c32[:, 0:128]
        self.ones32 = c32[:, 128:256]
        invf = c32[:, 256:288]
        self.cos, _ = self.alloc([128, NT, 32], F32, "cos")
        self.sin, _ = self.alloc([128, NT, 32], F32, "sin")
        self.cf, self.cfr = self.alloc([128, NT, 16], F32, "cf")
        self.keep()
        posi, pr = self.alloc([128, NT], I32, "posi")
        posf, pfr = self.alloc([128, NT], F32, "posf")
        u, ur = self.alloc([128, NT, 32], F32, "u")
        ui, uir = self.alloc([128, NT, 32], I32, "ui")
        uk, ukr = self.alloc([128, NT, 32], F32, "uk")
        self.load(posi, io["pos"], [pr])
        P.op("dve", lambda e: e.tensor_copy(out=posf, in_=posi), R=[pr], W=[pfr])
        for i in range(NT):
            P.op("dve", lambda e, i=i: e.tensor_scalar(out=u[:, i, :], in0=invf, scalar1=posf[:, i:i + 1],
                                                      scalar2=1.0 / (2 * math.pi), op0=ALU.mult, op1=ALU.mult),
                 R=[pfr, self.cres], W=[ur])
        uf = u.rearrange("p a b -> p (a b)")
        uif = ui.rearrange("p a b -> p (a b)")
        ukf = uk.rearrange("p a b -> p (a b)")
        for dst, shift in ((self.sin, 0.0), (self.cos, 0.25)):
            df = dst.rearrange("p a b -> p (a b)")
            w, wr = self.alloc([128, NT * 32], F32, "w")
            P.op("dve", lambda e, w=w, shift=shift: e.tensor_scalar(out=w, in0=uf, scalar1=shift, scalar2=None,
                                                                    op0=ALU.add), R=[ur], W=[wr])
            P.op("dve", lambda e, w=w: e.tensor_copy(out=uif, in_=w), R=[wr], W=[uir])
            P.op("dve", lambda e: e.tensor_copy(out=ukf, in_=uif), R=[uir], W=[ukr])
            P.op("dve", lambda e, w=w: e.tensor_tensor(out=w, in0=w, in1=ukf, op=ALU.subtract), R=[wr, ukr], W=[wr])
            P.op("dve", lambda e, w=w: e.tensor_scalar(out=ukf, in0=w, scalar1=0.5, scalar2=None, op0=ALU.is_gt),
                 R=[wr], W=[ukr])
            P.op("dve", lambda e, w=w: e.tensor_tensor(out=w, in0=w, in1=ukf, op=ALU.subtract), R=[wr, ukr], W=[wr])
            P.op("dve", lambda e, w=w: e.tensor_scalar(out=ukf, in0=w, scalar1=-0.5, scalar2=None, op0=ALU.is_lt),
                 R=[wr], W=[ukr])
            P.op("dve", lambda e, w=w: e.tensor_tensor(out=w, in0=w, in1=ukf, op=ALU.add), R=[wr, ukr], W=[wr])
            P.op("act", lambda e, w=w, df=df: e.activation(out=df, in_=w, func=AF.Sin, scale=2 * math.pi),
                 R=[wr], W=[self.cres])
        self.PROJ = self.dram("PROJ", [T, IN_DIM], BF16)
        self.G = self.dram("G", [T, 3 * D], BF16)
        self.QA = self.dram("QA", [T, 3072], BF16)
        self.KN = self.dram("KN", [T, 2048], BF16)
        self.VA = self.dram("VA", [T, 2048], BF16)
        self.OT = self.dram("OT", [3 * 2048, T], BF16)
        self.Y = self.dram("Y", [T, D], BF16)
        self.Z = self.dram("Z", [T, D], F32)
        self.X1 = self.dram("X1", [T, D], F32)

    def prenorm(self, l, xsrc):
        P, io = self.P, self.io
        self.new_phase()
        self.mark = self.base
        hT, _ = self.alloc([128, KC, T], BF16, "hT")
        self.keep()
        hTr = [Res(f"hT{i}") for i in range(NT)]
        gbc, gr = self.alloc([128, D], F32, "gbc")
        self.load(gbc, io["pre_bc"][l * 128:(l + 1) * 128, :], [gr])
        xs = [self.alloc([128, D], F32, f"xs{i}") for i in range(2)]
        hb, hbr = self.alloc([128, D], BF16, "hb")
        ss, ssr = self.alloc([128, 4], F32, "ss")
        for i in range(NT):
            x_, xr = xs[i % 2]
            self.load(x_, xsrc[i * 128:(i + 1) * 128, :], [xr])
            P.op("act", lambda e, x_=x_: e.activation(out=hb, in_=x_, func=AF.Square, accum_out=ss[:, 0:1]),
                 R=[xr], W=[hbr, ssr])
            self.rstd_from_ss(ss, ssr, D, 0)
            P.op("dve", lambda e, x_=x_: e.scalar_tensor_tensor(out=hb, in0=x_, scalar=ss[:, 2:3], in1=gbc,
                                                                op0=ALU.mult, op1=ALU.mult),
                 R=[xr, ssr, gr], W=[hbr])
            self.transposes(hb, hbr, KC, lambda g0, n, i=i: hT[:, g0:g0 + n, i * 128:(i + 1) * 128], hTr[i],
                            "act" if i % 2 else "dve")
        return hT, hTr

    def proj_in(self, l, hT, hTr):
        P, io = self.P, self.io
        self.new_phase()
        stg = [self.alloc([128, 512], BF16, f"stg{i}") for i in range(4)]
        cnt = [0]

        def evac(i, c0, w, ps, psr):
            s_, sr = stg[cnt[0] % 4]
            cnt[0] += 1
            self.copy(self.alt(), s_[:, 0:w], ps[:, 0:w], [psr], [sr])
            self.store(self.PROJ[i * 128:(i + 1) * 128, c0:c0 + w], s_[:, 0:w], [sr])
            if c0 <= O_CF < c0 + w:
                o = O_CF - c0
                P.op("dve", lambda e: e.tensor_copy(out=self.cf[:, i, :], in_=ps[:, o:o + 16]), R=[psr], W=[self.cfr])

        self.gemm(hT, hTr, KC, io["w_in"][l * D:(l + 1) * D, :], IN_DIM, 512, evac)

    def gates(self, l, hT, hTr):
        P, io = self.P, self.io
        self.new_phase()
        stg = [self.alloc([128, 512], BF16, f"gstg{i}") for i in range(4)]
        cnt = [0]

        def evac(i, c0, w, ps, psr):
            s_, sr = stg[cnt[0] % 4]
            cnt[0] += 1
            P.op("act", lambda e: e.activation(out=s_[:, 0:w], in_=ps[:, 0:w], func=AF.Sigmoid), R=[psr], W=[sr])
            self.store(self.G[i * 128:(i + 1) * 128, c0:c0 + w], s_[:, 0:w], [sr])

        self.gemm(hT, hTr, KC, io["w_merge"][l * D:(l + 1) * D, :], 3 * D, 512, evac)

    def evac_store(self, dst, dt=BF16, nbuf=4):
        stg = [self.alloc([128, 512], dt, f"es{i}") for i in range(nbuf)]
        cnt = [0]

        def evac(i, c0, w, ps, psr):
            s_, sr = stg[cnt[0] % nbuf]
            cnt[0] += 1
            self.copy(self.alt(), s_[:, 0:w], ps[:, 0:w], [psr], [sr])
            self.store(dst[i * 128:(i + 1) * 128, c0:c0 + w], s_[:, 0:w], [sr])

        return evac

    def mla(self, l):
        P, io = self.P, self.io
        self.new_phase()
        mark = self.base
        self.KRT, self.KRTr = self.alloc([128, 1, T], BF16, "KRT")
        krt_base = self.off
        cqT, _ = self.alloc([128, 12, T], BF16, "cqT")
        ckvT, _ = self.alloc([128, 4, T], BF16, "ckvT")
        cqTr = [Res(f"cqT{i}") for i in range(NT)]
        ckvTr = [Res(f"ckvT{i}") for i in range(NT)]
        self.keep()
        gq, gqr = self.alloc([128, 1536], F32, "gq")
        gkv, gkvr = self.alloc([128, 512], F32, "gkv")
        self.load(gq, io["qn_bc"][l * 128:(l + 1) * 128, :], [gqr])
        self.load(gkv, io["kvn_bc"][l * 128:(l + 1) * 128, :], [gkvr])
        tin = [self.alloc([128, 2112], BF16, f"tin{i}") for i in range(2)]
        junk, jr = self.alloc([128, 1536], BF16, "junk")
        cqn, cqnr = self.alloc([128, 1536], BF16, "cqn")
        ckvn, ckvnr = self.alloc([128, 512], BF16, "ckvn")
        ss, ssr = self.alloc([128, 8], F32, "ss")
        tt, ttr = self.alloc([128, 4, 32], F32, "tt")
        krr, krrr = self.alloc([128, 64], BF16, "krr")
        for i in range(NT):
            t_, tr = tin[i % 2]
            self.load(t_, self.PROJ[i * 128:(i + 1) * 128, 0:2112], [tr])
            P.op("act", lambda e, t_=t_: e.activation(out=junk, in_=t_[:, 0:1536], func=AF.Square,
                                                      accum_out=ss[:, 0:1]), R=[tr], W=[jr, ssr])
            P.op("act", lambda e, t_=t_: e.activation(out=junk[:, 0:512], in_=t_[:, 1536:2048], func=AF.Square,
                                                      accum_out=ss[:, 3:4]), R=[tr], W=[jr, ssr])
            self.rstd_from_ss(ss, ssr, 1536, 0)
            self.rstd_from_ss(ss, ssr, 512, 3)
            P.op("dve", lambda e, t_=t_: e.scalar_tensor_tensor(out=cqn, in0=t_[:, 0:1536], scalar=ss[:, 2:3], in1=gq,
                                                                op0=ALU.mult, op1=ALU.mult), R=[tr, ssr, gqr], W=[cqnr])
            P.op("dve", lambda e, t_=t_: e.scalar_tensor_tensor(out=ckvn, in0=t_[:, 1536:2048], scalar=ss[:, 5:6],
                                                                in1=gkv, op0=ALU.mult, op1=ALU.mult),
                 R=[tr, ssr, gkvr], W=[ckvnr])
            self.transposes(cqn, cqnr, 12, lambda g0, n, i=i: cqT[:, g0:g0 + n, i * 128:(i + 1) * 128], cqTr[i], "act")
            self.transposes(ckvn, ckvnr, 4, lambda g0, n, i=i: ckvT[:, g0:g0 + n, i * 128:(i + 1) * 128], ckvTr[i], "act")
            x1, x2 = t_[:, 2048:2080], t_[:, 2080:2112]
            cs, sn = self.cos[:, i, :], self.sin[:, i, :]
            for k, (a, b_) in enumerate(((x1, cs), (x2, sn), (x2, cs), (x1, sn))):
                P.op("pool", lambda e, k=k, a=a, b_=b_: e.tensor_tensor(out=tt[:, k, :], in0=a, in1=b_, op=ALU.mult),
                     R=[tr, self.cres], W=[ttr])
            P.op("pool", lambda e: e.tensor_tensor(out=krr[:, 0:32], in0=tt[:, 0, :], in1=tt[:, 1, :], op=ALU.subtract),
                 R=[ttr], W=[krrr])
            P.op("pool", lambda e: e.tensor_tensor(out=krr[:, 32:64], in0=tt[:, 2, :], in1=tt[:, 3, :], op=ALU.add),
                 R=[ttr], W=[krrr])
            self.transposes(krr, krrr, [0], lambda g0, n, i=i: self.KRT[0:64, 0:1, i * 128:(i + 1) * 128], self.KRTr,
                            "dve", rows=64)
        self.new_phase()
        qb = [self.alloc([128, 384], BF16, f"qb{i}") for i in range(3)]
        rt = [self.alloc([128, 4, 2, 32], F32, f"rt{i}") for i in range(2)]
        cnt = [0]

        def evac_q(i, c0, w, ps, psr):
            q_, qr = qb[cnt[0] % 3]
            t4, t4r = rt[cnt[0] % 2]
            cnt[0] += 1
            P.op("act", lambda e: e.copy(out=q_, in_=ps[:, 0:384]), R=[psr], W=[qr])
            pv = ps[:, 0:384].rearrange("p (h d) -> p h d", h=2, d=192)
            qv = q_.rearrange("p (h d) -> p h d", h=2, d=192)
            cs = self.cos[:, i:i + 1, :].broadcast_to([128, 2, 32])
            sn = self.sin[:, i:i + 1, :].broadcast_to([128, 2, 32])
            x1, x2 = pv[:, :, 128:160], pv[:, :, 160:192]
            for k, (a, b_) in enumerate(((x1, cs), (x2, sn), (x2, cs), (x1, sn))):
                P.op("dve", lambda e, k=k, a=a, b_=b_: e.tensor_tensor(out=t4[:, k, :, :], in0=a, in1=b_, op=ALU.mult),
                     R=[psr, self.cres], W=[t4r])
            P.op("dve", lambda e: e.tensor_tensor(out=qv[:, :, 128:160], in0=t4[:, 0, :, :], in1=t4[:, 1, :, :],
                                                  op=ALU.subtract), R=[t4r], W=[qr])
            P.op("dve", lambda e: e.tensor_tensor(out=qv[:, :, 160:192], in0=t4[:, 2, :, :], in1=t4[:, 3, :, :],
                                                  op=ALU.add), R=[t4r], W=[qr])
            self.store(self.QA[i * 128:(i + 1) * 128, c0:c0 + 384], q_, [qr])

        self.gemm(cqT, cqTr, 12, io["w_uq"][l * 1536:(l + 1) * 1536, :], 3072, 384, evac_q)
        self.new_phase()
        wb = [self.alloc([128, 4, 512], BF16, f"wkv{i}") for i in range(2)]
        ev = self.evac_store(self.KN)
        self.gemm(ckvT, ckvTr, 4, io["w_uk"][l * 512:(l + 1) * 512, :], 2048, 512, ev, wbufs=wb)
        ev2 = self.evac_store(self.VA)
        self.gemm(ckvT, ckvTr, 4, io["w_uv"][l * 512:(l + 1) * 512, :], 2048, 512, ev2, wbufs=wb)
        self.new_phase(base=krt_base)
        return mark

    def attn_full(self, l, kind):
        P, io = self.P, self.io
        self.new_phase()
        isA = kind == "A"
        KRT = self.KRT if isA else None
        br = 0 if isA else 2
        scale = (192.0 if isA else 128.0) ** -0.5
        if not isA:
            bfb, bfr = self.alloc([128, 1, 16], F32, "bfb")
            self.load(bfb[:, 0, :], io["bf_bc"][l * 128:(l + 1) * 128, :], [bfr])
            lf, lfr = self.alloc([128, NT, 16], F32, "lf")
            ncum, ncr = self.alloc([128, NT, 16], F32, "ncum")
            nrun, nrr = self.alloc([128, NT + 1, 16], F32, "nrun")
            biasC, bcr = self.alloc([128, 136, 16], F32, "biasC")
            P.op("dve", lambda e: e.tensor_tensor(out=lf, in0=self.cf, in1=bfb.broadcast_to([128, NT, 16]), op=ALU.add),
                 R=[self.cfr, bfr], W=[lfr])
            P.op("act", lambda e: e.activation(out=lf, in_=lf, func=AF.Exp, scale=-1.0), R=[lfr], W=[lfr])
            P.op("act", lambda e: e.activation(out=lf, in_=lf, func=AF.Ln, bias=1.0, scale=1.0), R=[lfr], W=[lfr])
            P.op("pool", lambda e: e.memset(nrun[:, 0, :], 0.0), W=[nrr])
            for i in range(NT):
                b = self.nb(6, 8)
                ps, psr = self.ps[b], self.psr[b]
                P.op("pe", lambda e, ps=ps, i=i: e.matmul(ps[:, 0:16], lhsT=self.tri, rhs=lf[:, i, :], start=True,
                                                          stop=True), R=[lfr, self.cres], W=[psr])
                P.op("pe", lambda e, ps=ps, i=i: e.matmul(ps[:, 16:32], lhsT=self.ones32, rhs=lf[:, i, :], start=True,
                                                          stop=True), R=[lfr, self.cres], W=[psr])
                P.op("dve", lambda e, ps=ps, i=i: e.tensor_tensor(out=ncum[:, i, :], in0=ps[:, 0:16], in1=nrun[:, i, :],
                                                                  op=ALU.add), R=[psr, nrr], W=[ncr])
                P.op("dve", lambda e, ps=ps, i=i: e.tensor_tensor(out=nrun[:, i + 1, :], in0=ps[:, 16:32],
                                                                  in1=nrun[:, i, :], op=ALU.add), R=[psr, nrr], W=[nrr])
            for i in range(NT):
                b0 = i * (i + 1) // 2
                P.op("dve", lambda e, i=i, b0=b0: e.tensor_tensor(
                    out=biasC[:, b0:b0 + i + 1, :], in0=ncum[:, 0:i + 1, :],
                    in1=nrun[:, i:i + 1, :].broadcast_to([128, i + 1, 16]), op=ALU.subtract),
                     R=[ncr, nrr], W=[bcr])
        QT, _ = self.alloc([128, 4, T], BF16, "QT")
        KT, _ = self.alloc([128, 4, T], BF16, "KT")
        QTr = [Res(f"QT{i}") for i in range(NT)]
        KTr = [Res(f"KT{i}") for i in range(NT)]
        if isA:
            QRT, _ = self.alloc([128, 4, T], BF16, "QRT")
            QRTr = [Res(f"QRT{i}") for i in range(NT)]
        Va, _ = self.alloc([128, NT, 4, 130], BF16, "Vaug")
        Var = [Res(f"Va{i}") for i in range(NT)]
        SZ, SZr = self.alloc([128, NT, 512], BF16, "SZ")
        OG, _ = self.alloc([128, NT, 512], BF16, "OG")
        OGr = [Res(f"OG{i}") for i in range(NT)]
        OTs, OTsr = self.alloc([128, 4, T], BF16, "OTs")
        qw = 768 if isA else 512
        qtl = [self.alloc([128, qw], BF16, f"qtl{i}") for i in range(2)]
        ktl = [self.alloc([128, 512], BF16, f"ktl{i}") for i in range(2)]
        vtl = [self.alloc([128, 512], BF16, f"vtl{i}") for i in range(2)]
        PT = [self.alloc([128, 512], BF16, f"PT{i}") for i in range(3)]
        rv = [self.alloc([128, 1], F32, f"rv{i}") for i in range(4)]
        P.op("pool", lambda e: e.memset(Va.rearrange("p a b c -> p (a b c)"), 1.0), W=Var)
        ptc = [0]
        rvc = [0]
        sset = [0]
        for hg in range(4):
            if isA:
                qsrc, ksrc, vsrc, zoff = self.QA, self.KN, self.VA, O_AZ
                qo, ko, vo = hg * 768, hg * 512, hg * 512
            else:
                qsrc = ksrc = vsrc = self.PROJ
                qo, ko, vo, zoff = O_CQ2 + hg * 512, O_CK + hg * 512, O_CV + hg * 512, O_CZ
            self.load(SZ, self.PROJ[:, zoff + hg * 512: zoff + (hg + 1) * 512].rearrange("(n p) d -> p n d", p=128), [SZr])
            P.op("act", lambda e: e.activation(out=SZ, in_=SZ, func=AF.Silu), R=[SZr], W=[SZr])
            for i in range(NT):
                q_, qr = qtl[i % 2]
                k_, kr = ktl[i % 2]
                v_, vr = vtl[i % 2]
                rows = slice(i * 128, (i + 1) * 128)
                self.load(q_, qsrc[rows, qo:qo + qw], [qr])
                self.load(k_, ksrc[rows, ko:ko + 512], [kr])
                self.load(v_, vsrc[rows, vo:vo + 512], [vr])
                tcols = slice(i * 128, (i + 1) * 128)
                if isA:
                    self.transposes(q_, qr, [h * 192 for h in range(4)], lambda g0, n: QT[:, g0:g0 + n, tcols], QTr[i],
                                    "dve", banks=(6, 8))
                    self.transposes(q_, qr, [h * 192 + 128 for h in range(4)], lambda g0, n: QRT[0:64, g0:g0 + n, tcols],
                                    QRTr[i], "dve", rows=64, banks=(6, 8))
                else:
                    self.transposes(q_, qr, 4, lambda g0, n: QT[:, g0:g0 + n, tcols], QTr[i], "dve", banks=(6, 8))
                self.transposes(k_, kr, 4, lambda g0, n: KT[:, g0:g0 + n, tcols], KTr[i], "dve", banks=(6, 8))
                P.op("pool", lambda e, v_=v_, i=i: e.tensor_copy(
                    out=Va[:, i, :, 0:128], in_=v_.rearrange("p (h d) -> p h d", h=4, d=128)), R=[vr], W=[Var[i]])
            for hh in range(4):
                h = hg * 4 + hh
                for Qb in range(4):
                    st = sset[0]
                    sset[0] ^= 1

                    def oacc(t):
                        b = 2 + st * 2 + t // 2
                        return self.ps[b][:, (t % 2) * 256:(t % 2) * 256 + 129], self.psr[b], self.ps[b], (t % 2) * 256

                    for j in range(4 * Qb + 4):
                        r = max(0, j - 4 * Qb)
                        n0 = r * 128
                        ncol = 512 - n0
                        b = self.nb(0, 2)
                        ps, psr = self.ps[b], self.psr[b]
                        q0 = Qb * 512 + n0
                        qres = QTr[4 * Qb + r:4 * Qb + 4]
                        P.op("pe", lambda e, ps=ps, j=j, q0=q0, ncol=ncol, hh=hh: e.matmul(
                            ps[:, 0:ncol], lhsT=KT[:, hh, j * 128:(j + 1) * 128], rhs=QT[:, hh, q0:q0 + ncol],
                            start=True, stop=not isA), R=[KTr[j]] + qres, W=[psr])
                        if isA:
                            P.op("pe", lambda e, ps=ps, j=j, q0=q0, ncol=ncol, hh=hh: e.matmul(
                                ps[:, 0:ncol], lhsT=KRT[0:64, 0, j * 128:(j + 1) * 128],
                                rhs=QRT[0:64, hh, q0:q0 + ncol], start=False, stop=True),
                                 R=[self.KRTr] + QRTr[4 * Qb + r:4 * Qb + 4], W=[psr])
                        p_, pr = PT[ptc[0] % 3]
                        ptc[0] += 1
                        if isA:
                            P.op("act", lambda e, ps=ps, p_=p_, ncol=ncol: e.activation(
                                out=p_[:, 0:ncol], in_=ps[:, 0:ncol], func=AF.Exp, scale=scale), R=[psr], W=[pr])
                        else:
                            for t in range(r, 4):
                                i = 4 * Qb + t
                                bi = i * (i + 1) // 2 + j
                                cc = (t - r) * 128
                                P.op("act", lambda e, ps=ps, p_=p_, cc=cc, bi=bi, h=h: e.activation(
                                    out=p_[:, cc:cc + 128], in_=ps[:, cc:cc + 128], func=AF.Exp, scale=scale,
                                    bias=biasC[:, bi, h:h + 1]), R=[psr, bcr], W=[pr])
                        if j >= 4 * Qb:
                            P.op("pool", lambda e, p_=p_: e.tensor_tensor(out=p_[:, 0:128], in0=p_[:, 0:128],
                                                                          in1=self.maskT, op=ALU.mult),
                                 R=[pr, self.cres], W=[pr])
                        for t in range(r, 4):
                            i = 4 * Qb + t
                            oa, oar, _, _ = oacc(t)
                            cc = (t - r) * 128
                            P.op("pe", lambda e, oa=oa, p_=p_, cc=cc, j=j, hh=hh, i=i, t=t: e.matmul(
                                oa, lhsT=p_[:, cc:cc + 128], rhs=Va[:, j, hh, 0:129], start=(j == 0 and t % 2 == 0),
                                stop=(j == i), skip_group_check=(t % 2 == 1)), R=[pr, Var[j]], W=[oar])
                    for t in range(4):
                        i = 4 * Qb + t
                        oa, oar, bank, c0 = oacc(t)
                        r_, rr = rv[rvc[0] % 4]
                        rvc[0] += 1
                        P.op("dve", lambda e, r_=r_, bank=bank, c0=c0: e.reciprocal(out=r_, in_=bank[:, c0 + 128:c0 + 129]),
                             R=[oar], W=[rr])
                        P.op("dve", lambda e, r_=r_, bank=bank, c0=c0, i=i, hh=hh: e.scalar_tensor_tensor(
                            out=OG[:, i, hh * 128:(hh + 1) * 128], in0=bank[:, c0:c0 + 128], scalar=r_[:, 0:1],
                            in1=SZ[:, i, hh * 128:(hh + 1) * 128], op0=ALU.mult, op1=ALU.mult),
                             R=[oar, rr, SZr], W=[OGr[i]])
            for i in range(NT):
                tcols = slice(i * 128, (i + 1) * 128)
                self.transposes(OG[:, i, :], OGr[i], 4, lambda g0, n: OTs[:, g0:g0 + n, tcols], OTsr, "dve", banks=(6, 8))
            r0 = br * 2048 + hg * 512
            self.store(self.OT[r0:r0 + 512, :].rearrange("(h p) t -> p h t", p=128), OTs, [OTsr])

    def attn_swa(self, l):
        P, io = self.P, self.io
        self.new_phase()
        E, Er = self.alloc([128, 32, 2, 128], BF16, "E")
        esink, esr = self.alloc([128, 32], F32, "esink")
        self.load(esink, io["sink_bc"][l * 128:(l + 1) * 128, :], [esr])
        P.op("act", lambda e: e.activation(out=esink, in_=esink, func=AF.Exp), R=[esr], W=[esr])
        rb = [self.alloc([128, 8, 256], F32, f"rb{i}") for i in range(2)]
        bm = self.bmask.rearrange("p (o c) -> p o c", o=1, c=256).broadcast_to([128, 8, 256])
        for g in range(4):
            r_, rr = rb[g % 2]
            self.load(r_.rearrange("p a b -> p (a b)"), io["relb"][:, g * 2048:(g + 1) * 2048], [rr])
            P.op("act", lambda e, r_=r_: e.activation(out=r_, in_=r_, func=AF.Exp), R=[rr], W=[rr])
            P.op("dve", lambda e, r_=r_, g=g: e.tensor_tensor(
                out=E[:, g * 8:(g + 1) * 8, :, :].rearrange("p h a b -> p h (a b)"), in0=r_, in1=bm, op=ALU.mult),
                 R=[rr, self.cres], W=[Er])
        QT, _ = self.alloc([128, 4, T], BF16, "QTb")
        QTr = [Res(f"QTb{i}") for i in range(NT)]
        KT, _ = self.alloc([128, 1, T], BF16, "KTb")
        KTr = [Res(f"KTb{i}") for i in range(NT)]
        Vb, Vbr = self.alloc([128, NT, 66], BF16, "Vb")
        SZ, SZr = self.alloc([128, NT, 512], BF16, "SZb")
        OG, _ = self.alloc([128, NT, 512], BF16, "OGb")
        OGr = [Res(f"OGb{i}") for i in range(NT)]
        OTs, OTsr = self.alloc([128, 4, T], BF16, "OTsb")
        qtl = [self.alloc([128, 512], BF16, f"bq{i}") for i in range(2)]
        ktl = [self.alloc([128, 128], BF16, f"bk{i}") for i in range(2)]
        PTc = [self.alloc([128, 8, 128], BF16, f"PTc{i}") for i in range(2)]
        PTp = [self.alloc([128, 8, 128], BF16, f"PTp{i}") for i in range(2)]
        den = [self.alloc([128, 4, 1], F32, f"den{i}") for i in range(4)]
        tmp = [self.alloc([128, 4, 64], F32, f"tmpb{i}") for i in range(4)]
        P.op("pool", lambda e: e.memset(Vb.rearrange("p a b -> p (a b)"), 1.0), W=[Vbr])
        fc = [0]
        for kv in range(4):
            self.load(SZ, self.PROJ[:, O_BZ + kv * 512: O_BZ + (kv + 1) * 512].rearrange("(n p) d -> p n d", p=128), [SZr])
            P.op("act", lambda e: e.activation(out=SZ, in_=SZ, func=AF.Silu), R=[SZr], W=[SZr])
            self.load(Vb[:, :, 0:64], self.PROJ[:, O_BV + kv * 64: O_BV + (kv + 1) * 64].rearrange("(n p) d -> p n d", p=128),
                      [Vbr])
            for i in range(NT):
                q_, qr = qtl[i % 2]
                k_, kr = ktl[i % 2]
                rows = slice(i * 128, (i + 1) * 128)
                tcols = slice(i * 128, (i + 1) * 128)
                self.load(q_, self.PROJ[rows, O_BQ + kv * 512: O_BQ + (kv + 1) * 512], [qr])
                self.load(k_[:, 0:64], self.PROJ[rows, O_BK + kv * 64: O_BK + (kv + 1) * 64], [kr])
                self.load(k_[:, 64:128], self.PROJ[rows, O_BK + kv * 64: O_BK + (kv + 1) * 64], [kr])
                self.transposes(q_, qr, 4, lambda g0, n: QT[:, g0:g0 + n, tcols], QTr[i], "dve", banks=(6, 8))
                self.transposes(k_, kr, 1, lambda g0, n: KT[:, g0:g0 + n, tcols], KTr[i], "dve", banks=(6, 8))
            for n in range(NT):
                pc, pcr = PTc[n % 2]
                pp, ppr = PTp[n % 2]
                cols = slice(n * 128, (n + 1) * 128)
                pcols = slice((n - 1) * 128, n * 128)
                for which in ((1, 0) if n > 0 else (1,)):
                    bb = 0 if which == 1 else 2
                    kc_ = cols if which == 1 else pcols
                    kres = KTr[n] if which == 1 else KTr[n - 1]
                    dst, dstr = (pc, pcr) if which == 1 else (pp, ppr)
                    for hh in range(8):
                        m, r = hh // 2, hh % 2
                        ps, psr = self.ps[bb + r], self.psr[bb + r]
                        oc = m * 128
                        P.op("pe", lambda e, ps=ps, oc=oc, r=r, m=m, kc_=kc_, cols=cols: e.matmul(
                            ps[:, oc:oc + 128], lhsT=KT[64 * r:64 * r + 64, 0, kc_], rhs=QT[64 * r:64 * r + 64, m, cols],
                            start=True, stop=True), R=[kres, QTr[n]], W=[psr])
                    for r in range(2):
                        ps, psr = self.ps[bb + r], self.psr[bb + r]
                        P.op("act", lambda e, ps=ps, dst=dst, r=r: e.activation(
                            out=dst[:, r:8:2, :], in_=ps[:, 0:512].rearrange("p (a b) -> p a b", a=4, b=128),
                            func=AF.Exp, scale=0.125), R=[psr], W=[dstr])
                    P.op("dve", lambda e, dst=dst, which=which, kv=kv: e.tensor_tensor(
                        out=dst, in0=dst, in1=E[:, kv * 8:(kv + 1) * 8, which, :], op=ALU.mult), R=[dstr, Er], W=[dstr])
                for hh in range(8):
                    ob, obr = self.ps[4 + hh // 4], self.psr[4 + hh // 4]
                    oc = (hh % 4) * 66
                    if n > 0:
                        P.op("pe", lambda e, ob=ob, oc=oc, hh=hh, pp=pp, n=n: e.matmul(
                            ob[:, oc:oc + 65], lhsT=pp[:, hh, :], rhs=Vb[:, n - 1, 0:65], start=True, stop=False),
                             R=[ppr, Vbr], W=[obr])
                    P.op("pe", lambda e, ob=ob, oc=oc, hh=hh, pc=pc, n=n: e.matmul(
                        ob[:, oc:oc + 65], lhsT=pc[:, hh, :], rhs=Vb[:, n, 0:65], start=(n == 0), stop=True),
                         R=[pcr, Vbr], W=[obr])
                for half in range(2):
                    ob, obr = self.ps[4 + half], self.psr[4 + half]
                    d_, dr = den[fc[0] % 4]
                    t_, tr = tmp[fc[0] % 4]
                    fc[0] += 1
                    ov = ob[:, 0:264].rearrange("p (h d) -> p h d", h=4, d=66)
                    h0 = kv * 8 + half * 4
                    P.op("dve", lambda e, ov=ov, d_=d_, h0=h0: e.tensor_tensor(
                        out=d_, in0=ov[:, :, 64:65], in1=esink[:, h0:h0 + 4].rearrange("p (h o) -> p h o", h=4, o=1),
                        op=ALU.add), R=[obr, esr], W=[dr])
                    P.op("dve", lambda e, d_=d_: e.reciprocal(out=d_, in_=d_), R=[dr], W=[dr])
                    P.op("dve", lambda e, ov=ov, d_=d_, t_=t_: e.tensor_tensor(
                        out=t_, in0=ov[:, :, 0:64], in1=d_.broadcast_to([128, 4, 64]), op=ALU.mult), R=[obr, dr], W=[tr])
                    c0 = half * 256
                    P.op("pool", lambda e, t_=t_, c0=c0, n=n: e.tensor_tensor(
                        out=OG[:, n, c0:c0 + 256], in0=t_.rearrange("p h d -> p (h d)"), in1=SZ[:, n, c0:c0 + 256],
                        op=ALU.mult), R=[tr, SZr], W=[OGr[n]])
            for i in range(NT):
                tcols = slice(i * 128, (i + 1) * 128)
                self.transposes(OG[:, i, :], OGr[i], 4, lambda g0, n: OTs[:, g0:g0 + n, tcols], OTsr, "dve", banks=(6, 8))
            r0 = 2048 + kv * 512
            self.store(self.OT[r0:r0 + 512, :].rearrange("(h p) t -> p h t", p=128), OTs, [OTsr])

    def combine(self, l):
        P, io = self.P, self.io
        for half in range(2):
            self.new_phase()
            OTh, OThr = self.alloc([128, 48, 1024], BF16, "OTh")
            ov = self.OT[:, half * 1024:(half + 1) * 1024].rearrange("(c p) t -> p c t", p=128)
            for k0 in range(0, 48, 8):
                self.load(OTh[:, k0:k0 + 8, :], ov[:, k0:k0 + 8, :], [OThr])
            wb = [self.alloc([128, 48, 256], BF16, f"wc{i}") for i in range(2)]
            gt = [self.alloc([128, 3, 256], BF16, f"gt{i}") for i in range(3)]
            t1 = [self.alloc([128, 256], F32, f"t1_{i}") for i in range(2)]
            t2 = [self.alloc([128, 256], F32, f"t2_{i}") for i in range(2)]
            ys = [self.alloc([128, 256], BF16, f"ys{i}") for i in range(3)]
            wsrc = [io[k][l * 2048:(l + 1) * 2048, :].rearrange("(c p) n -> p c n", p=128) for k in ("w_pa", "w_pb", "w_pc")]

            def issue(s):
                w_, wr = wb[s % 2]
                for b3 in range(3):
                    for k0 in (0, 8):
                        P.op("pool", lambda e, w_=w_, b3=b3, k0=k0, s=s: e.dma_start(
                            out=w_[:, b3 * 16 + k0:b3 * 16 + k0 + 8, :], in_=wsrc[b3][:, k0:k0 + 8, s * 256:(s + 1) * 256]),
                             W=[wr], dma="wld")

            cnt = 0
            issue(0)
            for s in range(16):
                if s + 1 < 16:
                    issue(s + 1)
                w_, wr = wb[s % 2]
                for ti in range(8):
                    i = half * 8 + ti
                    g_, gr = gt[cnt % 3]
                    a_, ar = t1[cnt % 2]
                    b_, br_ = t2[cnt % 2]
                    y_, yr = ys[cnt % 3]
                    cnt += 1
                    self.load(g_, self.G[i * 128:(i + 1) * 128, :].rearrange("p (b f) -> p b f", b=3)[:, :, s * 256:(s + 1) * 256],
                              [gr])
                    pss = []
                    for b3 in range(3):
                        bk = self.nb(0, 6)
                        ps, psr = self.ps[bk], self.psr[bk]
                        pss.append((ps, psr))
                        for c in range(16):
                            P.op("pe", lambda e, ps=ps, w_=w_, b3=b3, c=c, ti=ti: e.matmul(
                                ps[:, 0:256], lhsT=OTh[:, b3 * 16 + c, ti * 128:(ti + 1) * 128], rhs=w_[:, b3 * 16 + c, :],
                                start=(c == 0), stop=(c == 15)), R=[wr, OThr], W=[psr])
                    P.op("dve", lambda e, a_=a_, g_=g_, ps=pss[0][0]: e.tensor_tensor(out=a_, in0=ps[:, 0:256], in1=g_[:, 0, :],
                                                                                      op=ALU.mult), R=[pss[0][1], gr], W=[ar])
                    P.op("dve", lambda e, b_=b_, g_=g_, ps=pss[1][0]: e.tensor_tensor(out=b_, in0=ps[:, 0:256], in1=g_[:, 1, :],
                                                                                      op=ALU.mult), R=[pss[1][1], gr], W=[br_])
                    P.op("pool", lambda e, a_=a_, b_=b_: e.tensor_tensor(out=a_, in0=a_, in1=b_, op=ALU.add), R=[ar, br_], W=[ar])
                    P.op("dve", lambda e, b_=b_, g_=g_, ps=pss[2][0]: e.tensor_tensor(out=b_, in0=ps[:, 0:256], in1=g_[:, 2, :],
                                                                                      op=ALU.mult), R=[pss[2][1], gr], W=[br_])
                    P.op("pool", lambda e, a_=a_, b_=b_, y_=y_: e.tensor_tensor(out=y_, in0=a_, in1=b_, op=ALU.add),
                         R=[ar, br_], W=[yr])
                    self.store(self.Y[i * 128:(i + 1) * 128, s * 256:(s + 1) * 256], y_, [yr])

    def out_proj(self, l):
        P, io = self.P, self.io
        self.new_phase()
        mark = self.base
        yT, _ = self.alloc([128, KC, T], BF16, "yT")
        yTr = [Res(f"yT{i}") for i in range(NT)]
        self.keep()
        yt = [self.alloc([128, D], BF16, f"yt{i}") for i in range(2)]
        for i in range(NT):
            y_, yr = yt[i % 2]
            self.load(y_, self.Y[i * 128:(i + 1) * 128, :], [yr])
            self.transposes(y_, yr, KC, lambda g0, n, i=i: yT[:, g0:g0 + n, i * 128:(i + 1) * 128], yTr[i],
                            "act" if i % 2 else "dve")
        self.new_phase()
        ev = self.evac_store(self.Z, dt=F32, nbuf=2)
        self.gemm(yT, yTr, KC, io["w_o"][l * D:(l + 1) * D, :], D, 512, ev)
        self.new_phase(base=mark)

    def postnorm(self, l, xsrc, dst):
        P, io = self.P, self.io
        gbc, gr = self.alloc([128, D], F32, "gpost")
        self.load(gbc, io["post_bc"][l * 128:(l + 1) * 128, :], [gr])
        zs = [self.alloc([128, D], F32, f"zs{i}") for i in range(2)]
        xs = [self.alloc([128, D], F32, f"xr{i}") for i in range(2)]
        junk, jr = self.alloc([128, D], BF16, "junk")
        ss, ssr = self.alloc([128, 4], F32, "ss")
        for i in range(NT):
            z_, zr = zs[i % 2]
            x_, xr = xs[i % 2]
            rows = slice(i * 128, (i + 1) * 128)
            self.load(z_, self.Z[rows, :], [zr])
            self.load(x_, xsrc[rows, :], [xr])
            P.op("act", lambda e, z_=z_: e.activation(out=junk, in_=z_, func=AF.Square, accum_out=ss[:, 0:1]),
                 R=[zr], W=[jr, ssr])
            self.rstd_from_ss(ss, ssr, D, 0)
            P.op("dve", lambda e, z_=z_: e.scalar_tensor_tensor(out=z_, in0=z_, scalar=ss[:, 2:3], in1=gbc,
                                                                op0=ALU.mult, op1=ALU.mult), R=[zr, ssr, gr], W=[zr])
            P.op("pool", lambda e, z_=z_, x_=x_: e.tensor_tensor(out=z_, in0=z_, in1=x_, op=ALU.add), R=[zr, xr], W=[zr])
            self.store(dst[rows, :], z_, [zr])

    def build(self):
        io = self.io
        self.setup()
        xsrc = io["x"]
        for l in range(self.depth):
            hT, hTr = self.prenorm(l, xsrc)
            self.proj_in(l, hT, hTr)
            if self.stop == "proj":
                return
            self.gates(l, hT, hTr)
            self.new_phase(base=self.mark)
            if self.stop == "gates":
                return
            mark = self.mla(l)
            if self.stop == "mla":
                return
            self.attn_full(l, "A")
            self.new_phase(base=mark)
            if self.stop == "A":
                return
            self.attn_swa(l)
            if self.stop == "B":
                return
            self.attn_full(l, "C")
            if self.stop == "C":
                return
            self.combine(l)
            if self.stop == "Y":
                return
            self.out_proj(l)
            if self.stop == "Z":
                return
            dst = io["out"] if l == self.depth - 1 else self.X1
            self.postnorm(l, xsrc, dst)
            xsrc = dst


IN_SPECS = {
    "x": ([T, D], F32), "pos": ([128, NT], I32),
    "w_in": ([DEPTH * D, IN_DIM], F32), "w_merge": ([DEPTH * D, 3 * D], F32), "w_o": ([DEPTH * D, D], F32),
    "w_uq": ([DEPTH * 1536, 3072], F32), "w_uk": ([DEPTH * 512, 2048], F32), "w_uv": ([DEPTH * 512, 2048], F32),
    "w_pa": ([DEPTH * 2048, D], F32), "w_pb": ([DEPTH * 2048, D], F32), "w_pc": ([DEPTH * 2048, D], F32),
    "pre_bc": ([DEPTH * 128, D], F32), "post_bc": ([DEPTH * 128, D], F32),
    "qn_bc": ([DEPTH * 128, 1536], F32), "kvn_bc": ([DEPTH * 128, 512], F32),
    "sink_bc": ([DEPTH * 128, 32], F32), "bf_bc": ([DEPTH * 128, 16], F32),
    "relb": ([128, 32 * 2 * 128], F32),
    "cstb": ([128, 512], BF16), "cst32": ([128, 288], F32),
}


def build_nc(depth=DEPTH, stop=None, dumps=()):
    nc = bass.Bass("TRN2", target_bir_lowering=False)
    io = {}
    for k, (shp, dt) in IN_SPECS.items():
        io[k] = nc.dram_tensor(k, list(shp), dt, kind="ExternalInput").ap()
    io["out"] = nc.dram_tensor("out", [T, D], F32, kind="ExternalOutput").ap()
    es = ExitStack()
    with es:
        B = Builder(nc, es, io, depth=depth, stop=stop, dumps=dumps)
        B.build()
        B.P.final_wait()
        B.P.emit()
    return nc


def t5_bucket_np(dist):
    d = np.maximum(dist, 0)
    large = 16 + (np.log(np.maximum(d, 1).astype(np.float32) / np.float32(16)) / np.float32(math.log(128 / 16))
                  * np.float32(16)).astype(np.int32)
    large = np.minimum(large, 31)
    return np.where(d < 16, d, large)


def prep_shared(inputs):
    f = lambda a: np.ascontiguousarray(np.asarray(a, dtype=np.float32))
    sh = {}
    sh["w_in"] = f(inputs["w_in"]).reshape(DEPTH * D, IN_DIM)
    sh["w_merge"] = f(inputs["w_merge"]).reshape(DEPTH * D, 3 * D)
    sh["w_o"] = f(inputs["w_o"]).reshape(DEPTH * D, D)
    sh["w_uq"] = f(inputs["w_uq"]).reshape(DEPTH * 1536, 3072)
    sh["w_uk"] = f(inputs["w_uk"]).reshape(DEPTH * 512, 2048)
    sh["w_uv"] = f(inputs["w_uv"]).reshape(DEPTH * 512, 2048)
    sh["w_pa"] = f(inputs["w_proj_a"]).reshape(DEPTH * 2048, D)
    sh["w_pb"] = f(inputs["w_proj_b"]).reshape(DEPTH * 2048, D)
    sh["w_pc"] = f(inputs["w_proj_c"]).reshape(DEPTH * 2048, D)

    def bc(a):
        a = f(a)
        return np.ascontiguousarray(np.broadcast_to(a[:, None, :], (DEPTH, 128, a.shape[1]))).reshape(DEPTH * 128, -1)

    sh["pre_bc"] = bc(inputs["pre_norm"])
    sh["post_bc"] = bc(inputs["post_norm"])
    sh["qn_bc"] = bc(inputs["q_a_norm"])
    sh["kvn_bc"] = bc(inputs["kv_a_norm"])
    sh["sink_bc"] = bc(inputs["sinks"])
    sh["bf_bc"] = bc(inputs["b_f"])
    s_i = np.arange(128)[:, None]
    q_i = np.arange(128)[None, :]
    rt = f(inputs["rel_table"])
    d_cur = q_i - s_i
    d_prev = q_i - s_i + 128
    relb = np.zeros((128, 32, 2, 128), np.float32)
    relb[:, :, 0, :] = rt[t5_bucket_np(d_prev)].transpose(0, 2, 1)
    relb[:, :, 1, :] = rt[t5_bucket_np(d_cur)].transpose(0, 2, 1)
    sh["relb"] = relb.reshape(128, 32 * 2 * 128)
    cstb = np.zeros((128, 512), np.float32)
    cstb[:, 0:128] = np.eye(128)
    cstb[:, 128:256] = (s_i <= q_i)
    cstb[:, 256:384] = (s_i > q_i)
    cstb[:, 384:512] = (s_i <= q_i)
    sh["cstb"] = cstb.astype(ml_dtypes.bfloat16)
    c32 = np.zeros((128, 288), np.float32)
    c32[:, 0:128] = (s_i <= q_i)
    c32[:, 128:256] = 1.0
    half = 32
    c32[:, 256:288] = (np.float32(10000.0) ** (-np.arange(half, dtype=np.float32) / np.float32(half)))[None, :]
    sh["cst32"] = c32
    return sh


def core_inputs(inputs, sh, c):
    m = dict(sh)
    m["x"] = np.ascontiguousarray(np.asarray(inputs["x"][c], dtype=np.float32))
    p = np.asarray(inputs["positions"][c], dtype=np.int32)
    m["pos"] = np.ascontiguousarray(p.reshape(NT, 128).T)
    return m


def kernel(**inputs):
    sh = prep_shared(inputs)
    nc = build_nc()
    n = 8
    in_maps = [core_inputs(inputs, sh, c) for c in range(n)]
    res = run_bass_kernel_spmd(nc, in_maps, core_ids=list(range(n)))
    return np.stack([r["out"] for r in res.results], axis=0).astype(np.float32)
```

```python
import math
from contextlib import ExitStack
import numpy as np
import ml_dtypes
import concourse.bass as bass
import concourse.mybir as mybir
from concourse.bass_utils import run_bass_kernel_spmd

F32 = mybir.dt.float32
BF16 = mybir.dt.bfloat16
I32 = mybir.dt.int32
AF = mybir.ActivationFunctionType
ALU = mybir.AluOpType
AX = mybir.AxisListType

T = 2048
NT = 16
D = 4096
KC = 32
DEPTH = 2
EPS = 1e-6
IN_SIZES = (1536, 512, 64, 2048, 2048, 256, 256, 2048, 2048, 2048, 2048, 16, 2048)
IN_DIM = sum(IN_SIZES)
OFFS = [0]
for _n in IN_SIZES:
    OFFS.append(OFFS[-1] + _n)
(O_CQ, O_CKV, O_KR, O_AZ, O_BQ, O_BK, O_BV, O_BZ, O_CQ2, O_CK, O_CV, O_CF, O_CZ) = OFFS[:13]
SEM_LIMIT = 30000
NSLAB_IN = (IN_DIM + 511) // 512

ENGS = ["pe", "act", "dve", "pool", "sp"]
BLK = {"pe": "tensor", "act": "scalar", "dve": "vector", "pool": "gpsimd", "sp": "sync"}


class Res:
    __slots__ = ("name", "w", "r", "excl")

    def __init__(self, name, excl=False):
        self.name = name
        self.w = []
        self.r = {}
        self.excl = excl


class Op:
    __slots__ = ("eng", "fn", "deps", "signal", "val", "dma", "ep", "ph")


class Prog:
    NSLOT = 8

    def __init__(self, nc, es):
        self.nc = nc
        self.es = es
        self.ops = {e: [] for e in ENGS}
        self.sems = {}
        self.last = {}
        self.dmas = {}
        self.phase = "setup"
        self.scopes = False

    def _new(self, eng, fn, dma):
        o = Op()
        o.eng = eng
        o.fn = fn
        o.dma = dma
        o.signal = dma is not None
        o.val = None
        o.ep = 0
        o.deps = []
        o.ph = self.phase
        return o

    def op(self, eng, fn, R=(), W=(), dma=None):
        o = self._new(eng, fn, dma)
        deps = {}
        if dma is not None:
            lst = self.dmas.setdefault(dma, [])
            n = len(lst)
            o.dma = (dma, n % self.NSLOT)
            if n >= self.NSLOT:
                prev = lst[n - self.NSLOT]
                deps[id(prev)] = (prev, True)
            lst.append(o)
        key = o.dma if o.dma else eng
        for r in R:
            for wo in r.w:
                deps[id(wo)] = (wo, True)
            if r.excl:
                for k2, ro in r.r.items():
                    if k2 != key and id(ro) not in deps:
                        deps[id(ro)] = (ro, False)
        appendw = set()
        for w in W:
            if (o.dma is not None and w.w and not w.r
                    and all(x.dma is not None and x.dma[0] == o.dma[0] for x in w.w)):
                appendw.add(id(w))
                continue
            for wo in w.w:
                if id(wo) not in deps:
                    deps[id(wo)] = (wo, False)
            for ro in w.r.values():
                if id(ro) not in deps:
                    deps[id(ro)] = (ro, False)
        for d, raw in deps.values():
            if d is o:
                continue
            if d.dma is None and o.dma is None and d.eng == eng and eng == "pe":
                continue
            d.signal = True
            o.deps.append(d)
        for r in R:
            r.r[key] = o
        for w in W:
            if id(w) in appendw:
                w.w.append(o)
            else:
                w.w = [o]
                w.r = {}
        self.ops[eng].append(o)
        self.last[key] = o
        return o

    def barrier(self, engs=ENGS):
        lasts = list(self.last.values())
        for d in lasts:
            d.signal = True
        for e in engs:
            o = self._new(e, None, None)
            o.deps = list(lasts)
            self.ops[e].append(o)

    def final_wait(self, eng="sp"):
        self.barrier(engs=[eng])

    def emit(self):
        nc = self.nc
        cnt = {}
        for e in ENGS:
            for o in self.ops[e]:
                if not o.signal:
                    continue
                key = o.dma if o.dma else e
                inc = 16 if o.dma else 1
                ep, v = cnt.get(key, (0, 0))
                if v + inc > SEM_LIMIT:
                    ep, v = ep + 1, 0
                v += inc
                cnt[key] = (ep, v)
                o.ep, o.val = ep, v
        for key, (ep, v) in cnt.items():
            nm = key if isinstance(key, str) else f"{key[0]}{key[1]}"
            for k in range(ep + 1):
                self.sems[(key, k)] = self.es.enter_context(nc.semaphore(f"s_{nm}_{k}"))
        block = self.es.enter_context(nc.Block())
        for e in ENGS:
            ops = self.ops[e]

            def body(eng, ops=ops, e=e):
                seen = {}
                cur = None
                for o in ops:
                    if self.scopes and o.fn is not None and o.ph != cur:
                        if cur is not None:
                            nc.pop_named_scope(cur)
                        cur = o.ph
                        nc.push_named_scope(cur)
                    need = {}
                    for d in o.deps:
                        k = (d.dma if d.dma else d.eng, d.ep)
                        if d.val > need.get(k, 0):
                            need[k] = d.val
                    for k, v in need.items():
                        if seen.get(k, 0) < v:
                            eng.wait_ge(self.sems[k], v)
                            seen[k] = v
                    if o.fn is None:
                        continue
                    ins = o.fn(eng)
                    if o.signal:
                        key = o.dma if o.dma else e
                        ins.then_inc(self.sems[(key, o.ep)], 16 if o.dma else 1)
                if cur is not None:
                    nc.pop_named_scope(cur)

            getattr(block, BLK[e])(body)


ARENA_BYTES = 207 * 1024


class Builder:
    def __init__(self, nc, es, io, depth=DEPTH, stop=None, dumps=()):
        self.nc = nc
        self.es = es
        self.io = io
        self.depth = depth
        self.stop = stop
        self.dumps = set(dumps)
        self.P = Prog(nc, es)
        self.arena = es.enter_context(nc.sbuf_tensor("arena", [128, ARENA_BYTES // 2], BF16))
        self.base = 0
        self.off = 0
        self.ps = [es.enter_context(nc.psum_tensor(f"psb{i}", [128, 512], F32)) for i in range(8)]
        self.psr = [Res(f"psb{i}", excl=True) for i in range(8)]
        self.bankc = {}
        self.flip = 0
        self.scr = {}

    def alloc(self, shape, dt, name="t"):
        n = int(np.prod(shape[1:]))
        nb = n * (2 if dt == BF16 else 4)
        o = self.off
        assert o % 4 == 0
        self.off = o + (nb + 63) // 64 * 64
        assert self.off <= ARENA_BYTES, (name, self.off)
        a = self.arena[:, o // 2:(o + nb) // 2]
        if dt != BF16:
            a = a.bitcast(dt)
        if len(shape) == 3:
            a = a.rearrange("p (a b) -> p a b", a=shape[1], b=shape[2])
        elif len(shape) == 4:
            a = a.rearrange("p (a b c) -> p a b c", a=shape[1], b=shape[2], c=shape[3])
        if shape[0] != 128:
            a = a[0:shape[0]]
        return a, Res(name)

    def keep(self):
        self.base = self.off

    def new_phase(self, base=None, name=None):
        self.P.barrier()
        if name is not None:
            self.P.phase = name
        if base is not None:
            self.base = base
        self.off = self.base

    def dram(self, name, shape, dt):
        kind = "ExternalOutput" if name in self.dumps else "Internal"
        t = self.nc.dram_tensor(name, list(shape), dt, kind=kind).ap()
        self.scr[name] = t
        return t

    def nb(self, lo=0, hi=8):
        b = self.bankc.get((lo, hi), lo)
        self.bankc[(lo, hi)] = lo + (b + 1 - lo) % (hi - lo)
        return b

    def alt(self):
        self.flip ^= 1
        return "act" if self.flip else "dve"

    def copy(self, eng, out, in_, R, W):
        if eng == "act":
            self.P.op("act", lambda e: e.copy(out=out, in_=in_), R=R, W=W)
        else:
            self.P.op(eng, lambda e: e.tensor_copy(out=out, in_=in_), R=R, W=W)

    def load(self, out, in_, W, q="sp", st="ld"):
        self.P.op(q, lambda e: e.dma_start(out=out, in_=in_), W=W, dma=st)

    def store(self, out, in_, R, q="sp", st="st"):
        self.P.op(q, lambda e: e.dma_start(out=out, in_=in_), R=R, dma=st)

    def transposes(self, src, src_res, cols, dst_fn, dst_res, eng, rows=128, banks=(0, 8)):
        P = self.P
        if isinstance(cols, int):
            cols = [c * rows for c in range(cols)]
        nblk = len(cols)
        g0 = 0
        while g0 < nblk:
            n = min(4, nblk - g0)
            b = self.nb(*banks)
            ps, psr = self.ps[b], self.psr[b]
            for k in range(n):
                c0 = cols[g0 + k]
                P.op("pe", lambda e, ps=ps, k=k, c0=c0: e.matmul(
                    ps[0:rows, k * 128:(k + 1) * 128], lhsT=src[:, c0:c0 + rows], rhs=self.ident,
                    start=True, stop=True), R=[src_res, self.cres], W=[psr])
            dst = dst_fn(g0, n)
            pv = ps[0:rows, 0:n * 128].rearrange("p (a b) -> p a b", a=n, b=128)
            self.copy(eng, dst, pv, [psr], [dst_res])
            g0 += n

    def rstd_from_ss(self, ss, ssr, n, col):
        P = self.P
        P.op("dve", lambda e: e.tensor_scalar(out=ss[:, col + 1:col + 2], in0=ss[:, col:col + 1], scalar1=1.0 / n,
                                              scalar2=EPS, op0=ALU.mult, op1=ALU.add), R=[ssr], W=[ssr])
        P.op("act", lambda e: e.activation(out=ss[:, col + 1:col + 2], in_=ss[:, col + 1:col + 2], func=AF.Sqrt),
             R=[ssr], W=[ssr])
        P.op("dve", lambda e: e.reciprocal(out=ss[:, col + 2:col + 3], in_=ss[:, col + 1:col + 2]), R=[ssr], W=[ssr])

    def gemm(self, AT, ATres, kc, wsrc, ncols, slab_w, evac, tiles=range(NT), banks=(0, 8), wbufs=None, slab_src=None):
        P = self.P
        if wbufs is None:
            wbufs = [self.alloc([128, kc, slab_w], BF16, f"wb{i}") for i in range(2)]
        wv = wsrc.rearrange("(c p) n -> p c n", p=128) if slab_src is None else None
        nslab = (ncols + slab_w - 1) // slab_w

        def issue(s):
            c0 = s * slab_w
            w = min(slab_w, ncols - c0)
            wb, wr = wbufs[s % 2]
            step = 8 if kc >= 8 else kc
            for k0 in range(0, kc, step):
                k1 = min(kc, k0 + step)
                src = wv[:, k0:k1, c0:c0 + w] if slab_src is None else slab_src(s)[:, k0:k1, 0:w]
                P.op("pool", lambda e, wb=wb, k0=k0, k1=k1, w=w, src=src: e.dma_start(
                    out=wb[:, k0:k1, 0:w], in_=src), W=[wr], dma="wld")

        issue(0)
        for s in range(nslab):
            if s + 1 < nslab:
                issue(s + 1)
            c0 = s * slab_w
            w = min(slab_w, ncols - c0)
            wb, wr = wbufs[s % 2]
            for i in tiles:
                b = self.nb(*banks)
                ps, psr = self.ps[b], self.psr[b]
                for c in range(kc):
                    P.op("pe", lambda e, ps=ps, wb=wb, c=c, i=i, w=w: e.matmul(
                        ps[:, 0:w], lhsT=AT[:, c, i * 128:(i + 1) * 128], rhs=wb[:, c, 0:w],
                        start=(c == 0), stop=(c == kc - 1)), R=[wr, ATres[i]], W=[psr])
                evac(i, c0, w, ps, psr)

    def setup(self):
        P, io = self.P, self.io
        self.cres = Res("consts")
        cb, _ = self.alloc([128, 512], BF16, "cstb")
        c32, _ = self.alloc([128, 288], F32, "cst32")
        self.load(cb, io["cstb"], [self.cres])
        self.load(c32, io["cst32"], [self.cres])
        self.ident = cb[:, 0:128]
        self.maskT = cb[:, 128:256]
        self.bmask = cb[:, 256:512]
        self.tri = c32[:, 0:128]
        self.ones32 = c32[:, 128:256]
        invf = c32[:, 256:288]
        self.cos, _ = self.alloc([128, NT, 32], F32, "cos")
        self.sin, _ = self.alloc([128, NT, 32], F32, "sin")
        self.cf, self.cfr = self.alloc([128, NT, 16], F32, "cf")
        self.keep()
        posi, pr = self.alloc([128, NT], I32, "posi")
        posf, pfr = self.alloc([128, NT], F32, "posf")
        u, ur = self.alloc([128, NT, 32], F32, "u")
        ui, uir = self.alloc([128, NT, 32], I32, "ui")
        uk, ukr = self.alloc([128, NT, 32], F32, "uk")
        self.load(posi, io["pos"], [pr])
        P.op("dve", lambda e: e.tensor_copy(out=posf, in_=posi), R=[pr], W=[pfr])
        for i in range(NT):
            P.op("dve", lambda e, i=i: e.tensor_scalar(out=u[:, i, :], in0=invf, scalar1=posf[:, i:i + 1],
                                                      scalar2=1.0 / (2 * math.pi), op0=ALU.mult, op1=ALU.mult),
                 R=[pfr, self.cres], W=[ur])
        uf = u.rearrange("p a b -> p (a b)")
        uif = ui.rearrange("p a b -> p (a b)")
        ukf = uk.rearrange("p a b -> p (a b)")
        for dst, shift in ((self.sin, 0.0), (self.cos, 0.25)):
            df = dst.rearrange("p a b -> p (a b)")
            w, wr = self.alloc([128, NT * 32], F32, "w")
            P.op("dve", lambda e, w=w, shift=shift: e.tensor_scalar(out=w, in0=uf, scalar1=shift, scalar2=None,
                                                                    op0=ALU.add), R=[ur], W=[wr])
            P.op("dve", lambda e, w=w: e.tensor_copy(out=uif, in_=w), R=[wr], W=[uir])
            P.op("dve", lambda e: e.tensor_copy(out=ukf, in_=uif), R=[uir], W=[ukr])
            P.op("dve", lambda e, w=w: e.tensor_tensor(out=w, in0=w, in1=ukf, op=ALU.subtract), R=[wr, ukr], W=[wr])
            P.op("dve", lambda e, w=w: e.tensor_scalar(out=ukf, in0=w, scalar1=0.5, scalar2=None, op0=ALU.is_gt),
                 R=[wr], W=[ukr])
            P.op("dve", lambda e, w=w: e.tensor_tensor(out=w, in0=w, in1=ukf, op=ALU.subtract), R=[wr, ukr], W=[wr])
            P.op("dve", lambda e, w=w: e.tensor_scalar(out=ukf, in0=w, scalar1=-0.5, scalar2=None, op0=ALU.is_lt),
                 R=[wr], W=[ukr])
            P.op("dve", lambda e, w=w: e.tensor_tensor(out=w, in0=w, in1=ukf, op=ALU.add), R=[wr, ukr], W=[wr])
            P.op("act", lambda e, w=w, df=df: e.activation(out=df, in_=w, func=AF.Sin, scale=2 * math.pi),
                 R=[wr], W=[self.cres])
        self.PROJ = self.dram("PROJ", [T, 17408], BF16)
        self.G = self.dram("G", [T, 3 * D], BF16)
        self.QA = self.dram("QA", [T, 3072], BF16)
        self.KN = self.dram("KN", [T, 2048], BF16)
        self.VA = self.dram("VA", [T, 2048], BF16)
        self.OT = self.dram("OT", [3 * 2048, T], BF16)
        self.Y = self.dram("Y", [T, D], BF16)
        self.Z = self.dram("Z", [T, D], F32)
        self.X1 = self.dram("X1", [T, D], F32)

    def prenorm(self, l, xsrc):
        P, io = self.P, self.io
        self.new_phase(name=f"L{l}_prenorm")
        self.mark = self.base
        hT, _ = self.alloc([128, KC, T], BF16, "hT")
        self.keep()
        hTr = [Res(f"hT{i}") for i in range(NT)]
        gbc, gr = self.alloc([128, D], F32, "gbc")
        self.load(gbc, io["pre_bc"][l * 128:(l + 1) * 128, :], [gr])
        xs = [self.alloc([128, D], F32, f"xs{i}") for i in range(2)]
        hb, hbr = self.alloc([128, D], BF16, "hb")
        ss, ssr = self.alloc([128, 4], F32, "ss")
        for i in range(NT):
            x_, xr = xs[i % 2]
            self.load(x_, xsrc[i * 128:(i + 1) * 128, :], [xr])
            P.op("act", lambda e, x_=x_: e.activation(out=hb, in_=x_, func=AF.Square, accum_out=ss[:, 0:1]),
                 R=[xr], W=[hbr, ssr])
            self.rstd_from_ss(ss, ssr, D, 0)
            P.op("dve", lambda e, x_=x_: e.scalar_tensor_tensor(out=hb, in0=x_, scalar=ss[:, 2:3], in1=gbc,
                                                                op0=ALU.mult, op1=ALU.mult),
                 R=[xr, ssr, gr], W=[hbr])
            self.transposes(hb, hbr, KC, lambda g0, n, i=i: hT[:, g0:g0 + n, i * 128:(i + 1) * 128], hTr[i],
                            "act" if i % 2 else "dve")
        return hT, hTr

    def proj_in(self, l, hT, hTr):
        P, io = self.P, self.io
        self.new_phase(name=f"L{l}_proj_in")
        stg = [self.alloc([128, 512], BF16, f"stg{i}") for i in range(4)]
        cnt = [0]

        def evac(i, c0, w, ps, psr):
            s_, sr = stg[cnt[0] % 4]
            cnt[0] += 1
            self.copy(self.alt(), s_[:, 0:w], ps[:, 0:w], [psr], [sr])
            self.store(self.PROJ[i * 128:(i + 1) * 128, c0:c0 + w], s_[:, 0:w], [sr])
            if c0 <= O_CF < c0 + w:
                o = O_CF - c0
                P.op("dve", lambda e: e.tensor_copy(out=self.cf[:, i, :], in_=ps[:, o:o + 16]), R=[psr], W=[self.cfr])

        def slab_src(s):
            r0 = (l * NSLAB_IN + s) * 128
            return io["w_in"][r0:r0 + 128, :].rearrange("p (c n) -> p c n", c=KC, n=512)

        self.gemm(hT, hTr, KC, None, IN_DIM, 512, evac, slab_src=slab_src)

    def gates(self, l, hT, hTr):
        P, io = self.P, self.io
        self.new_phase(name=f"L{l}_gates")
        stg = [self.alloc([128, 512], BF16, f"gstg{i}") for i in range(4)]
        cnt = [0]

        def evac(i, c0, w, ps, psr):
            s_, sr = stg[cnt[0] % 4]
            cnt[0] += 1
            P.op("act", lambda e: e.activation(out=s_[:, 0:w], in_=ps[:, 0:w], func=AF.Sigmoid), R=[psr], W=[sr])
            self.store(self.G[i * 128:(i + 1) * 128, c0:c0 + w], s_[:, 0:w], [sr])

        self.gemm(hT, hTr, KC, io["w_merge"][l * D:(l + 1) * D, :], 3 * D, 512, evac)

    def evac_store(self, dst, dt=BF16, nbuf=4):
        stg = [self.alloc([128, 512], dt, f"es{i}") for i in range(nbuf)]
        cnt = [0]

        def evac(i, c0, w, ps, psr):
            s_, sr = stg[cnt[0] % nbuf]
            cnt[0] += 1
            self.copy(self.alt(), s_[:, 0:w], ps[:, 0:w], [psr], [sr])
            self.store(dst[i * 128:(i + 1) * 128, c0:c0 + w], s_[:, 0:w], [sr])

        return evac

    def mla(self, l):
        P, io = self.P, self.io
        self.new_phase(name=f"L{l}_mla_prep")
        mark = self.base
        self.KRT, self.KRTr = self.alloc([128, 1, T], BF16, "KRT")
        krt_base = self.off
        cqT, _ = self.alloc([128, 12, T], BF16, "cqT")
        ckvT, _ = self.alloc([128, 4, T], BF16, "ckvT")
        cqTr = [Res(f"cqT{i}") for i in range(NT)]
        ckvTr = [Res(f"ckvT{i}") for i in range(NT)]
        self.keep()
        gq, gqr = self.alloc([128, 1536], F32, "gq")
        gkv, gkvr = self.alloc([128, 512], F32, "gkv")
        self.load(gq, io["qn_bc"][l * 128:(l + 1) * 128, :], [gqr])
        self.load(gkv, io["kvn_bc"][l * 128:(l + 1) * 128, :], [gkvr])
        tin = [self.alloc([128, 2112], BF16, f"tin{i}") for i in range(2)]
        junk, jr = self.alloc([128, 1536], BF16, "junk")
        cqn, cqnr = self.alloc([128, 1536], BF16, "cqn")
        ckvn, ckvnr = self.alloc([128, 512], BF16, "ckvn")
        ss, ssr = self.alloc([128, 8], F32, "ss")
        tt, ttr = self.alloc([128, 4, 32], F32, "tt")
        krr, krrr = self.alloc([128, 64], BF16, "krr")
        for i in range(NT):
            t_, tr = tin[i % 2]
            self.load(t_, self.PROJ[i * 128:(i + 1) * 128, 0:2112], [tr])
            P.op("act", lambda e, t_=t_: e.activation(out=junk, in_=t_[:, 0:1536], func=AF.Square,
                                                      accum_out=ss[:, 0:1]), R=[tr], W=[jr, ssr])
            P.op("act", lambda e, t_=t_: e.activation(out=junk[:, 0:512], in_=t_[:, 1536:2048], func=AF.Square,
                                                      accum_out=ss[:, 3:4]), R=[tr], W=[jr, ssr])
            self.rstd_from_ss(ss, ssr, 1536, 0)
            self.rstd_from_ss(ss, ssr, 512, 3)
            P.op("dve", lambda e, t_=t_: e.scalar_tensor_tensor(out=cqn, in0=t_[:, 0:1536], scalar=ss[:, 2:3], in1=gq,
                                                                op0=ALU.mult, op1=ALU.mult), R=[tr, ssr, gqr], W=[cqnr])
            P.op("dve", lambda e, t_=t_: e.scalar_tensor_tensor(out=ckvn, in0=t_[:, 1536:2048], scalar=ss[:, 5:6],
                                                                in1=gkv, op0=ALU.mult, op1=ALU.mult),
                 R=[tr, ssr, gkvr], W=[ckvnr])
            self.transposes(cqn, cqnr, 12, lambda g0, n, i=i: cqT[:, g0:g0 + n, i * 128:(i + 1) * 128], cqTr[i], "act")
            self.transposes(ckvn, ckvnr, 4, lambda g0, n, i=i: ckvT[:, g0:g0 + n, i * 128:(i + 1) * 128], ckvTr[i], "act")
            x1, x2 = t_[:, 2048:2080], t_[:, 2080:2112]
            cs, sn = self.cos[:, i, :], self.sin[:, i, :]
            for k, (a, b_) in enumerate(((x1, cs), (x2, sn), (x2, cs), (x1, sn))):
                P.op("pool", lambda e, k=k, a=a, b_=b_: e.tensor_tensor(out=tt[:, k, :], in0=a, in1=b_, op=ALU.mult),
                     R=[tr, self.cres], W=[ttr])
            P.op("pool", lambda e: e.tensor_tensor(out=krr[:, 0:32], in0=tt[:, 0, :], in1=tt[:, 1, :], op=ALU.subtract),
                 R=[ttr], W=[krrr])
            P.op("pool", lambda e: e.tensor_tensor(out=krr[:, 32:64], in0=tt[:, 2, :], in1=tt[:, 3, :], op=ALU.add),
                 R=[ttr], W=[krrr])
            self.transposes(krr, krrr, [0], lambda g0, n, i=i: self.KRT[0:64, 0:1, i * 128:(i + 1) * 128], self.KRTr,
                            "dve", rows=64)
        self.new_phase(name=f"L{l}_uq")
        qb = [self.alloc([128, 384], BF16, f"qb{i}") for i in range(3)]
        rt = [self.alloc([128, 4, 2, 32], F32, f"rt{i}") for i in range(2)]
        cnt = [0]

        def evac_q(i, c0, w, ps, psr):
            q_, qr = qb[cnt[0] % 3]
            t4, t4r = rt[cnt[0] % 2]
            cnt[0] += 1
            P.op("act", lambda e: e.copy(out=q_, in_=ps[:, 0:384]), R=[psr], W=[qr])
            pv = ps[:, 0:384].rearrange("p (h d) -> p h d", h=2, d=192)
            qv = q_.rearrange("p (h d) -> p h d", h=2, d=192)
            cs = self.cos[:, i:i + 1, :].broadcast_to([128, 2, 32])
            sn = self.sin[:, i:i + 1, :].broadcast_to([128, 2, 32])
            x1, x2 = pv[:, :, 128:160], pv[:, :, 160:192]
            for k, (a, b_) in enumerate(((x1, cs), (x2, sn), (x2, cs), (x1, sn))):
                P.op("dve", lambda e, k=k, a=a, b_=b_: e.tensor_tensor(out=t4[:, k, :, :], in0=a, in1=b_, op=ALU.mult),
                     R=[psr, self.cres], W=[t4r])
            P.op("dve", lambda e: e.tensor_tensor(out=qv[:, :, 128:160], in0=t4[:, 0, :, :], in1=t4[:, 1, :, :],
                                                  op=ALU.subtract), R=[t4r], W=[qr])
            P.op("dve", lambda e: e.tensor_tensor(out=qv[:, :, 160:192], in0=t4[:, 2, :, :], in1=t4[:, 3, :, :],
                                                  op=ALU.add), R=[t4r], W=[qr])
            self.store(self.QA[i * 128:(i + 1) * 128, c0:c0 + 384], q_, [qr])

        self.gemm(cqT, cqTr, 12, io["w_uq"][l * 1536:(l + 1) * 1536, :], 3072, 384, evac_q)
        self.new_phase(name=f"L{l}_ukv")
        wb = [self.alloc([128, 4, 512], BF16, f"wkv{i}") for i in range(2)]
        ev = self.evac_store(self.KN)
        self.gemm(ckvT, ckvTr, 4, io["w_uk"][l * 512:(l + 1) * 512, :], 2048, 512, ev, wbufs=wb)
        ev2 = self.evac_store(self.VA)
        self.gemm(ckvT, ckvTr, 4, io["w_uv"][l * 512:(l + 1) * 512, :], 2048, 512, ev2, wbufs=wb)
        self.new_phase(base=krt_base)
        return mark

    def attn_full(self, l, kind):
        P, io = self.P, self.io
        self.new_phase(name=f"L{l}_attn{kind}")
        isA = kind == "A"
        KRT = self.KRT if isA else None
        br = 0 if isA else 2
        scale = (192.0 if isA else 128.0) ** -0.5
        if not isA:
            bfb, bfr = self.alloc([128, 1, 16], F32, "bfb")
            self.load(bfb[:, 0, :], io["bf_bc"][l * 128:(l + 1) * 128, :], [bfr])
            lf, lfr = self.alloc([128, NT, 16], F32, "lf")
            ncum, ncr = self.alloc([128, NT, 16], F32, "ncum")
            nrun, nrr = self.alloc([128, NT + 1, 16], F32, "nrun")
            biasC, bcr = self.alloc([128, 136, 16], F32, "biasC")
            P.op("dve", lambda e: e.tensor_tensor(out=lf, in0=self.cf, in1=bfb.broadcast_to([128, NT, 16]), op=ALU.add),
                 R=[self.cfr, bfr], W=[lfr])
            P.op("act", lambda e: e.activation(out=lf, in_=lf, func=AF.Exp, scale=-1.0), R=[lfr], W=[lfr])
            P.op("act", lambda e: e.activation(out=lf, in_=lf, func=AF.Ln, bias=1.0, scale=1.0), R=[lfr], W=[lfr])
            P.op("pool", lambda e: e.memset(nrun[:, 0, :], 0.0), W=[nrr])
            for i in range(NT):
                b = self.nb(6, 8)
                ps, psr = self.ps[b], self.psr[b]
                P.op("pe", lambda e, ps=ps, i=i: e.matmul(ps[:, 0:16], lhsT=self.tri, rhs=lf[:, i, :], start=True,
                                                          stop=True), R=[lfr, self.cres], W=[psr])
                P.op("pe", lambda e, ps=ps, i=i: e.matmul(ps[:, 16:32], lhsT=self.ones32, rhs=lf[:, i, :], start=True,
                                                          stop=True), R=[lfr, self.cres], W=[psr])
                P.op("dve", lambda e, ps=ps, i=i: e.tensor_tensor(out=ncum[:, i, :], in0=ps[:, 0:16], in1=nrun[:, i, :],
                                                                  op=ALU.add), R=[psr, nrr], W=[ncr])
                P.op("dve", lambda e, ps=ps, i=i: e.tensor_tensor(out=nrun[:, i + 1, :], in0=ps[:, 16:32],
                                                                  in1=nrun[:, i, :], op=ALU.add), R=[psr, nrr], W=[nrr])
            for i in range(NT):
                b0 = i * (i + 1) // 2
                P.op("dve", lambda e, i=i, b0=b0: e.tensor_tensor(
                    out=biasC[:, b0:b0 + i + 1, :], in0=ncum[:, 0:i + 1, :],
                    in1=nrun[:, i:i + 1, :].broadcast_to([128, i + 1, 16]), op=ALU.subtract),
                     R=[ncr, nrr], W=[bcr])
        QT, _ = self.alloc([128, 4, T], BF16, "QT")
        KT, _ = self.alloc([128, 4, T], BF16, "KT")
        QTr = [Res(f"QT{i}") for i in range(NT)]
        KTr = [Res(f"KT{i}") for i in range(NT)]
        if isA:
            QRT, _ = self.alloc([128, 4, T], BF16, "QRT")
            QRTr = [Res(f"QRT{i}") for i in range(NT)]
        Va, _ = self.alloc([128, NT, 4, 130], BF16, "Vaug")
        Var = [Res(f"Va{i}") for i in range(NT)]
        SZ, SZr = self.alloc([128, NT, 512], BF16, "SZ")
        OG, _ = self.alloc([128, NT, 512], BF16, "OG")
        OGr = [Res(f"OG{i}") for i in range(NT)]
        OTs, OTsr = self.alloc([128, 4, T], BF16, "OTs")
        qw = 768 if isA else 512
        qtl = [self.alloc([128, qw], BF16, f"qtl{i}") for i in range(2)]
        ktl = [self.alloc([128, 512], BF16, f"ktl{i}") for i in range(2)]
        vtl = [self.alloc([128, 512], BF16, f"vtl{i}") for i in range(2)]
        PT = [(self.alloc([128, 512], BF16, f"PT{i}")[0], [Res(f"PT{i}_{k}") for k in range(4)]) for i in range(3)]
        rv = [self.alloc([128, 1], F32, f"rv{i}") for i in range(4)]
        P.op("pool", lambda e: e.memset(Va.rearrange("p a b c -> p (a b c)"), 1.0), W=Var)
        ptc = [0]
        rvc = [0]
        sset = [0]
        for hg in range(4):
            if isA:
                qsrc, ksrc, vsrc, zoff = self.QA, self.KN, self.VA, O_AZ
                qo, ko, vo = hg * 768, hg * 512, hg * 512
            else:
                qsrc = ksrc = vsrc = self.PROJ
                qo, ko, vo, zoff = O_CQ2 + hg * 512, O_CK + hg * 512, O_CV + hg * 512, O_CZ
            self.load(SZ, self.PROJ[:, zoff + hg * 512: zoff + (hg + 1) * 512].rearrange("(n p) d -> p n d", p=128), [SZr])
            P.op("act", lambda e: e.activation(out=SZ, in_=SZ, func=AF.Silu), R=[SZr], W=[SZr])
            for i in range(NT):
                q_, qr = qtl[i % 2]
                k_, kr = ktl[i % 2]
                v_, vr = vtl[i % 2]
                rows = slice(i * 128, (i + 1) * 128)
                self.load(q_, qsrc[rows, qo:qo + qw], [qr])
                self.load(k_, ksrc[rows, ko:ko + 512], [kr])
                self.load(v_, vsrc[rows, vo:vo + 512], [vr])
                tcols = slice(i * 128, (i + 1) * 128)
                if isA:
                    self.transposes(q_, qr, [h * 192 for h in range(4)], lambda g0, n: QT[:, g0:g0 + n, tcols], QTr[i],
                                    "dve", banks=(6, 8))
                    self.transposes(q_, qr, [h * 192 + 128 for h in range(4)], lambda g0, n: QRT[0:64, g0:g0 + n, tcols],
                                    QRTr[i], "dve", rows=64, banks=(6, 8))
                else:
                    self.transposes(q_, qr, 4, lambda g0, n: QT[:, g0:g0 + n, tcols], QTr[i], "dve", banks=(6, 8))
                self.transposes(k_, kr, 4, lambda g0, n: KT[:, g0:g0 + n, tcols], KTr[i], "dve", banks=(6, 8))
                P.op("pool", lambda e, v_=v_, i=i: e.tensor_copy(
                    out=Va[:, i, :, 0:128], in_=v_.rearrange("p (h d) -> p h d", h=4, d=128)), R=[vr], W=[Var[i]])
            for hh in range(4):
                h = hg * 4 + hh
                for Qb in range(4):
                    st = sset[0]
                    sset[0] ^= 1

                    def oacc(t):
                        b = 2 + st * 2 + t // 2
                        return self.ps[b][:, (t % 2) * 256:(t % 2) * 256 + 129], self.psr[b], self.ps[b], (t % 2) * 256

                    def emit_scores(j):
                        r = max(0, j - 4 * Qb)
                        n0 = r * 128
                        ncol = 512 - n0
                        b = self.nb(0, 2)
                        ps, psr = self.ps[b], self.psr[b]
                        q0 = Qb * 512 + n0
                        qres = QTr[4 * Qb + r:4 * Qb + 4]
                        P.op("pe", lambda e, ps=ps, j=j, q0=q0, ncol=ncol, hh=hh: e.matmul(
                            ps[:, 0:ncol], lhsT=KT[:, hh, j * 128:(j + 1) * 128], rhs=QT[:, hh, q0:q0 + ncol],
                            start=True, stop=not isA), R=[KTr[j]] + qres, W=[psr])
                        if isA:
                            P.op("pe", lambda e, ps=ps, j=j, q0=q0, ncol=ncol, hh=hh: e.matmul(
                                ps[:, 0:ncol], lhsT=KRT[0:64, 0, j * 128:(j + 1) * 128],
                                rhs=QRT[0:64, hh, q0:q0 + ncol], start=False, stop=True),
                                 R=[self.KRTr] + QRTr[4 * Qb + r:4 * Qb + 4], W=[psr])
                        return ps, psr, r, ncol

                    def emit_rest(j, sc):
                        ps, psr, r, ncol = sc
                        p_, prs = PT[ptc[0] % 3]
                        ptc[0] += 1
                        if isA:
                            P.op("act", lambda e, ps=ps, p_=p_, ncol=ncol: e.activation(
                                out=p_[:, 0:ncol], in_=ps[:, 0:ncol], func=AF.Exp, scale=scale), R=[psr], W=prs[0:4 - r])
                        else:
                            for t in range(r, 4):
                                i = 4 * Qb + t
                                bi = i * (i + 1) // 2 + j
                                cc = (t - r) * 128
                                P.op("act", lambda e, ps=ps, p_=p_, cc=cc, bi=bi, h=h: e.activation(
                                    out=p_[:, cc:cc + 128], in_=ps[:, cc:cc + 128], func=AF.Exp, scale=scale,
                                    bias=biasC[:, bi, h:h + 1]), R=[psr, bcr], W=[prs[t - r]])
                        if j >= 4 * Qb:
                            P.op("pool", lambda e, p_=p_: e.tensor_tensor(out=p_[:, 0:128], in0=p_[:, 0:128],
                                                                          in1=self.maskT, op=ALU.mult),
                                 R=[prs[0], self.cres], W=[prs[0]])
                        for t in range(r, 4):
                            i = 4 * Qb + t
                            oa, oar, _, _ = oacc(t)
                            cc = (t - r) * 128
                            P.op("pe", lambda e, oa=oa, p_=p_, cc=cc, j=j, hh=hh, i=i, t=t: e.matmul(
                                oa, lhsT=p_[:, cc:cc + 128], rhs=Va[:, j, hh, 0:129], start=(j == 0 and t % 2 == 0),
                                stop=(j == i), skip_group_check=(t % 2 == 1)), R=[prs[t - r], Var[j]], W=[oar])

                    nj = 4 * Qb + 4
                    pend = emit_scores(0)
                    for j in range(nj):
                        nxt = emit_scores(j + 1) if j + 1 < nj else None
                        emit_rest(j, pend)
                        pend = nxt
                    for t in range(4):
                        i = 4 * Qb + t
                        oa, oar, bank, c0 = oacc(t)
                        r_, rr = rv[rvc[0] % 4]
                        rvc[0] += 1
                        P.op("dve", lambda e, r_=r_, bank=bank, c0=c0: e.reciprocal(out=r_, in_=bank[:, c0 + 128:c0 + 129]),
                             R=[oar], W=[rr])
                        P.op("dve", lambda e, r_=r_, bank=bank, c0=c0, i=i, hh=hh: e.scalar_tensor_tensor(
                            out=OG[:, i, hh * 128:(hh + 1) * 128], in0=bank[:, c0:c0 + 128], scalar=r_[:, 0:1],
                            in1=SZ[:, i, hh * 128:(hh + 1) * 128], op0=ALU.mult, op1=ALU.mult),
                             R=[oar, rr, SZr], W=[OGr[i]])
            for i in range(NT):
                tcols = slice(i * 128, (i + 1) * 128)
                self.transposes(OG[:, i, :], OGr[i], 4, lambda g0, n: OTs[:, g0:g0 + n, tcols], OTsr, "dve", banks=(6, 8))
            r0 = br * 2048 + hg * 512
            self.store(self.OT[r0:r0 + 512, :].rearrange("(h p) t -> p h t", p=128), OTs, [OTsr])

    def attn_swa(self, l):
        P, io = self.P, self.io
        self.new_phase(name=f"L{l}_swa")
        E, Er = self.alloc([128, 32, 2, 128], BF16, "E")
        esink, esr = self.alloc([128, 32], F32, "esink")
        self.load(esink, io["sink_bc"][l * 128:(l + 1) * 128, :], [esr])
        P.op("act", lambda e: e.activation(out=esink, in_=esink, func=AF.Exp), R=[esr], W=[esr])
        rb = [self.alloc([128, 8, 256], F32, f"rb{i}") for i in range(2)]
        bm = self.bmask.rearrange("p (o c) -> p o c", o=1, c=256).broadcast_to([128, 8, 256])
        for g in range(4):
            r_, rr = rb[g % 2]
            self.load(r_.rearrange("p a b -> p (a b)"), io["relb"][:, g * 2048:(g + 1) * 2048], [rr])
            P.op("act", lambda e, r_=r_: e.activation(out=r_, in_=r_, func=AF.Exp), R=[rr], W=[rr])
            P.op("dve", lambda e, r_=r_, g=g: e.tensor_tensor(
                out=E[:, g * 8:(g + 1) * 8, :, :].rearrange("p h a b -> p h (a b)"), in0=r_, in1=bm, op=ALU.mult),
                 R=[rr, self.cres], W=[Er])
        QT, _ = self.alloc([128, 4, T], BF16, "QTb")
        QTr = [Res(f"QTb{i}") for i in range(NT)]
        KT, _ = self.alloc([128, 1, T], BF16, "KTb")
        KTr = [Res(f"KTb{i}") for i in range(NT)]
        Vb, Vbr = self.alloc([128, NT, 66], BF16, "Vb")
        SZ, SZr = self.alloc([128, NT, 512], BF16, "SZb")
        OG, _ = self.alloc([128, NT, 512], BF16, "OGb")
        OGr = [Res(f"OGb{i}") for i in range(NT)]
        OTs, OTsr = self.alloc([128, 4, T], BF16, "OTsb")
        qtl = [self.alloc([128, 512], BF16, f"bq{i}") for i in range(2)]
        ktl = [self.alloc([128, 128], BF16, f"bk{i}") for i in range(2)]
        PTc = [self.alloc([128, 8, 128], BF16, f"PTc{i}") for i in range(2)]
        PTp = [self.alloc([128, 8, 128], BF16, f"PTp{i}") for i in range(2)]
        den = [self.alloc([128, 4, 1], F32, f"den{i}") for i in range(4)]
        tmp = [self.alloc([128, 4, 64], F32, f"tmpb{i}") for i in range(4)]
        P.op("pool", lambda e: e.memset(Vb.rearrange("p a b -> p (a b)"), 1.0), W=[Vbr])
        fc = [0]
        for kv in range(4):
            self.load(SZ, self.PROJ[:, O_BZ + kv * 512: O_BZ + (kv + 1) * 512].rearrange("(n p) d -> p n d", p=128), [SZr])
            P.op("act", lambda e: e.activation(out=SZ, in_=SZ, func=AF.Silu), R=[SZr], W=[SZr])
            self.load(Vb[:, :, 0:64], self.PROJ[:, O_BV + kv * 64: O_BV + (kv + 1) * 64].rearrange("(n p) d -> p n d", p=128),
                      [Vbr])
            for i in range(NT):
                q_, qr = qtl[i % 2]
                k_, kr = ktl[i % 2]
                rows = slice(i * 128, (i + 1) * 128)
                tcols = slice(i * 128, (i + 1) * 128)
                self.load(q_, self.PROJ[rows, O_BQ + kv * 512: O_BQ + (kv + 1) * 512], [qr])
                self.load(k_[:, 0:64], self.PROJ[rows, O_BK + kv * 64: O_BK + (kv + 1) * 64], [kr])
                self.load(k_[:, 64:128], self.PROJ[rows, O_BK + kv * 64: O_BK + (kv + 1) * 64], [kr])
                self.transposes(q_, qr, 4, lambda g0, n: QT[:, g0:g0 + n, tcols], QTr[i], "dve", banks=(6, 8))
                self.transposes(k_, kr, 1, lambda g0, n: KT[:, g0:g0 + n, tcols], KTr[i], "dve", banks=(6, 8))
            for n in range(NT):
                pc, pcr = PTc[n % 2]
                pp, ppr = PTp[n % 2]
                cols = slice(n * 128, (n + 1) * 128)
                pcols = slice((n - 1) * 128, n * 128)
                for which in ((1, 0) if n > 0 else (1,)):
                    bb = 0 if which == 1 else 2
                    kc_ = cols if which == 1 else pcols
                    kres = KTr[n] if which == 1 else KTr[n - 1]
                    dst, dstr = (pc, pcr) if which == 1 else (pp, ppr)
                    for hh in range(8):
                        m, r = hh // 2, hh % 2
                        ps, psr = self.ps[bb + r], self.psr[bb + r]
                        oc = m * 128
                        P.op("pe", lambda e, ps=ps, oc=oc, r=r, m=m, kc_=kc_, cols=cols: e.matmul(
                            ps[:, oc:oc + 128], lhsT=KT[64 * r:64 * r + 64, 0, kc_], rhs=QT[64 * r:64 * r + 64, m, cols],
                            start=True, stop=True), R=[kres, QTr[n]], W=[psr])
                    for r in range(2):
                        ps, psr = self.ps[bb + r], self.psr[bb + r]
                        P.op("act", lambda e, ps=ps, dst=dst, r=r: e.activation(
                            out=dst[:, r:8:2, :], in_=ps[:, 0:512].rearrange("p (a b) -> p a b", a=4, b=128),
                            func=AF.Exp, scale=0.125), R=[psr], W=[dstr])
                    P.op("dve", lambda e, dst=dst, which=which, kv=kv: e.tensor_tensor(
                        out=dst, in0=dst, in1=E[:, kv * 8:(kv + 1) * 8, which, :], op=ALU.mult), R=[dstr, Er], W=[dstr])
                for hh in range(8):
                    ob, obr = self.ps[4 + hh // 4], self.psr[4 + hh // 4]
                    oc = (hh % 4) * 66
                    if n > 0:
                        P.op("pe", lambda e, ob=ob, oc=oc, hh=hh, pp=pp, n=n: e.matmul(
                            ob[:, oc:oc + 65], lhsT=pp[:, hh, :], rhs=Vb[:, n - 1, 0:65], start=True, stop=False),
                             R=[ppr, Vbr], W=[obr])
                    P.op("pe", lambda e, ob=ob, oc=oc, hh=hh, pc=pc, n=n: e.matmul(
                        ob[:, oc:oc + 65], lhsT=pc[:, hh, :], rhs=Vb[:, n, 0:65], start=(n == 0), stop=True),
                         R=[pcr, Vbr], W=[obr])
                for half in range(2):
                    ob, obr = self.ps[4 + half], self.psr[4 + half]
                    d_, dr = den[fc[0] % 4]
                    t_, tr = tmp[fc[0] % 4]
                    fc[0] += 1
                    ov = ob[:, 0:264].rearrange("p (h d) -> p h d", h=4, d=66)
                    h0 = kv * 8 + half * 4
                    P.op("dve", lambda e, ov=ov, d_=d_, h0=h0: e.tensor_tensor(
                        out=d_, in0=ov[:, :, 64:65], in1=esink[:, h0:h0 + 4].rearrange("p (h o) -> p h o", h=4, o=1),
                        op=ALU.add), R=[obr, esr], W=[dr])
                    P.op("dve", lambda e, d_=d_: e.reciprocal(out=d_, in_=d_), R=[dr], W=[dr])
                    P.op("dve", lambda e, ov=ov, d_=d_, t_=t_: e.tensor_tensor(
                        out=t_, in0=ov[:, :, 0:64], in1=d_.broadcast_to([128, 4, 64]), op=ALU.mult), R=[obr, dr], W=[tr])
                    c0 = half * 256
                    P.op("pool", lambda e, t_=t_, c0=c0, n=n: e.tensor_tensor(
                        out=OG[:, n, c0:c0 + 256], in0=t_.rearrange("p h d -> p (h d)"), in1=SZ[:, n, c0:c0 + 256],
                        op=ALU.mult), R=[tr, SZr], W=[OGr[n]])
            for i in range(NT):
                tcols = slice(i * 128, (i + 1) * 128)
                self.transposes(OG[:, i, :], OGr[i], 4, lambda g0, n: OTs[:, g0:g0 + n, tcols], OTsr, "dve", banks=(6, 8))
            r0 = 2048 + kv * 512
            self.store(self.OT[r0:r0 + 512, :].rearrange("(h p) t -> p h t", p=128), OTs, [OTsr])

    def combine(self, l):
        P, io = self.P, self.io
        for half in range(2):
            self.new_phase(name=f"L{l}_combine")
            OTh, OThr = self.alloc([128, 48, 1024], BF16, "OTh")
            ov = self.OT[:, half * 1024:(half + 1) * 1024].rearrange("(c p) t -> p c t", p=128)
            for k0 in range(0, 48, 8):
                self.load(OTh[:, k0:k0 + 8, :], ov[:, k0:k0 + 8, :], [OThr])
            wb = [self.alloc([128, 48, 256], BF16, f"wc{i}") for i in range(2)]
            gt = [self.alloc([128, 3, 256], BF16, f"gt{i}") for i in range(3)]
            t1 = [self.alloc([128, 256], F32, f"t1_{i}") for i in range(2)]
            t2 = [self.alloc([128, 256], F32, f"t2_{i}") for i in range(2)]
            ys = [self.alloc([128, 256], BF16, f"ys{i}") for i in range(3)]
            wsrc = [io[k][l * 2048:(l + 1) * 2048, :].rearrange("(c p) n -> p c n", p=128) for k in ("w_pa", "w_pb", "w_pc")]

            def issue(s):
                w_, wr = wb[s % 2]
                for b3 in range(3):
                    for k0 in (0, 8):
                        P.op("pool", lambda e, w_=w_, b3=b3, k0=k0, s=s: e.dma_start(
                            out=w_[:, b3 * 16 + k0:b3 * 16 + k0 + 8, :], in_=wsrc[b3][:, k0:k0 + 8, s * 256:(s + 1) * 256]),
                             W=[wr], dma="wld")

            cnt = 0
            issue(0)
            for s in range(16):
                if s + 1 < 16:
                    issue(s + 1)
                w_, wr = wb[s % 2]
                for ti in range(8):
                    i = half * 8 + ti
                    g_, gr = gt[cnt % 3]
                    a_, ar = t1[cnt % 2]
                    b_, br_ = t2[cnt % 2]
                    y_, yr = ys[cnt % 3]
                    cnt += 1
                    self.load(g_, self.G[i * 128:(i + 1) * 128, :].rearrange("p (b f) -> p b f", b=3)[:, :, s * 256:(s + 1) * 256],
                              [gr])
                    pss = []
                    for b3 in range(3):
                        bk = self.nb(0, 6)
                        ps, psr = self.ps[bk], self.psr[bk]
                        pss.append((ps, psr))
                        for c in range(16):
                            P.op("pe", lambda e, ps=ps, w_=w_, b3=b3, c=c, ti=ti: e.matmul(
                                ps[:, 0:256], lhsT=OTh[:, b3 * 16 + c, ti * 128:(ti + 1) * 128], rhs=w_[:, b3 * 16 + c, :],
                                start=(c == 0), stop=(c == 15)), R=[wr, OThr], W=[psr])
                    P.op("dve", lambda e, a_=a_, g_=g_, ps=pss[0][0]: e.tensor_tensor(out=a_, in0=ps[:, 0:256], in1=g_[:, 0, :],
                                                                                      op=ALU.mult), R=[pss[0][1], gr], W=[ar])
                    P.op("dve", lambda e, b_=b_, g_=g_, ps=pss[1][0]: e.tensor_tensor(out=b_, in0=ps[:, 0:256], in1=g_[:, 1, :],
                                                                                      op=ALU.mult), R=[pss[1][1], gr], W=[br_])
                    P.op("dve", lambda e, a_=a_, b_=b_: e.tensor_tensor(out=a_, in0=a_, in1=b_, op=ALU.add), R=[ar, br_], W=[ar])
                    P.op("dve", lambda e, b_=b_, g_=g_, ps=pss[2][0]: e.tensor_tensor(out=b_, in0=ps[:, 0:256], in1=g_[:, 2, :],
                                                                                      op=ALU.mult), R=[pss[2][1], gr], W=[br_])
                    P.op("dve", lambda e, a_=a_, b_=b_, y_=y_: e.tensor_tensor(out=y_, in0=a_, in1=b_, op=ALU.add),
                         R=[ar, br_], W=[yr])
                    self.store(self.Y[i * 128:(i + 1) * 128, s * 256:(s + 1) * 256], y_, [yr])

    def out_proj(self, l):
        P, io = self.P, self.io
        self.new_phase(name=f"L{l}_yT")
        mark = self.base
        yT, _ = self.alloc([128, KC, T], BF16, "yT")
        yTr = [Res(f"yT{i}") for i in range(NT)]
        self.keep()
        yt = [self.alloc([128, D], BF16, f"yt{i}") for i in range(2)]
        for i in range(NT):
            y_, yr = yt[i % 2]
            self.load(y_, self.Y[i * 128:(i + 1) * 128, :], [yr])
            self.transposes(y_, yr, KC, lambda g0, n, i=i: yT[:, g0:g0 + n, i * 128:(i + 1) * 128], yTr[i],
                            "act" if i % 2 else "dve")
        self.new_phase(name=f"L{l}_w_o")
        ev = self.evac_store(self.Z, dt=F32, nbuf=2)
        self.gemm(yT, yTr, KC, io["w_o"][l * D:(l + 1) * D, :], D, 512, ev)
        self.new_phase(base=mark, name=f"L{l}_postnorm")

    def postnorm(self, l, xsrc, dst):
        P, io = self.P, self.io
        gbc, gr = self.alloc([128, D], F32, "gpost")
        self.load(gbc, io["post_bc"][l * 128:(l + 1) * 128, :], [gr])
        zs = [self.alloc([128, D], F32, f"zs{i}") for i in range(2)]
        xs = [self.alloc([128, D], F32, f"xr{i}") for i in range(2)]
        junk, jr = self.alloc([128, D], BF16, "junk")
        ss, ssr = self.alloc([128, 4], F32, "ss")
        for i in range(NT):
            z_, zr = zs[i % 2]
            x_, xr = xs[i % 2]
            rows = slice(i * 128, (i + 1) * 128)
            self.load(z_, self.Z[rows, :], [zr])
            self.load(x_, xsrc[rows, :], [xr])
            P.op("act", lambda e, z_=z_: e.activation(out=junk, in_=z_, func=AF.Square, accum_out=ss[:, 0:1]),
                 R=[zr], W=[jr, ssr])
            self.rstd_from_ss(ss, ssr, D, 0)
            P.op("dve", lambda e, z_=z_: e.scalar_tensor_tensor(out=z_, in0=z_, scalar=ss[:, 2:3], in1=gbc,
                                                                op0=ALU.mult, op1=ALU.mult), R=[zr, ssr, gr], W=[zr])
            P.op("pool", lambda e, z_=z_, x_=x_: e.tensor_tensor(out=z_, in0=z_, in1=x_, op=ALU.add), R=[zr, xr], W=[zr])
            self.store(dst[rows, :], z_, [zr])

    def build(self):
        io = self.io
        self.setup()
        xsrc = io["x"]
        for l in range(self.depth):
            hT, hTr = self.prenorm(l, xsrc)
            self.proj_in(l, hT, hTr)
            if self.stop == "proj":
                return
            self.gates(l, hT, hTr)
            self.new_phase(base=self.mark)
            if self.stop == "gates":
                return
            mark = self.mla(l)
            if self.stop == "mla":
                return
            self.attn_full(l, "A")
            self.new_phase(base=mark)
            if self.stop == "A":
                return
            self.attn_swa(l)
            if self.stop == "B":
                return
            self.attn_full(l, "C")
            if self.stop == "C":
                return
            self.combine(l)
            if self.stop == "Y":
                return
            self.out_proj(l)
            if self.stop == "Z":
                return
            dst = io["out"] if l == self.depth - 1 else self.X1
            self.postnorm(l, xsrc, dst)
            xsrc = dst


IN_SPECS = {
    "x": ([T, D], F32), "pos": ([128, NT], I32),
    "w_in": ([DEPTH * NSLAB_IN * 128, KC * 512], F32), "w_merge": ([DEPTH * D, 3 * D], F32), "w_o": ([DEPTH * D, D], F32),
    "w_uq": ([DEPTH * 1536, 3072], F32), "w_uk": ([DEPTH * 512, 2048], F32), "w_uv": ([DEPTH * 512, 2048], F32),
    "w_pa": ([DEPTH * 2048, D], F32), "w_pb": ([DEPTH * 2048, D], F32), "w_pc": ([DEPTH * 2048, D], F32),
    "pre_bc": ([DEPTH * 128, D], F32), "post_bc": ([DEPTH * 128, D], F32),
    "qn_bc": ([DEPTH * 128, 1536], F32), "kvn_bc": ([DEPTH * 128, 512], F32),
    "sink_bc": ([DEPTH * 128, 32], F32), "bf_bc": ([DEPTH * 128, 16], F32),
    "relb": ([128, 32 * 2 * 128], F32),
    "cstb": ([128, 512], BF16), "cst32": ([128, 288], F32),
}


def build_nc(depth=DEPTH, stop=None, dumps=(), scopes=False):
    nc = bass.Bass("TRN2", target_bir_lowering=False)
    io = {}
    for k, (shp, dt) in IN_SPECS.items():
        io[k] = nc.dram_tensor(k, list(shp), dt, kind="ExternalInput").ap()
    io["out"] = nc.dram_tensor("out", [T, D], F32, kind="ExternalOutput").ap()
    es = ExitStack()
    with es:
        B = Builder(nc, es, io, depth=depth, stop=stop, dumps=dumps)
        B.P.scopes = scopes
        B.build()
        B.P.final_wait()
        B.P.emit()
    return nc


def t5_bucket_np(dist):
    d = np.maximum(dist, 0)
    large = 16 + (np.log(np.maximum(d, 1).astype(np.float32) / np.float32(16)) / np.float32(math.log(128 / 16))
                  * np.float32(16)).astype(np.int32)
    large = np.minimum(large, 31)
    return np.where(d < 16, d, large)


def prep_shared(inputs):
    f = lambda a: np.ascontiguousarray(np.asarray(a, dtype=np.float32))
    sh = {}
    wi = np.zeros((DEPTH, KC, 128, NSLAB_IN * 512), np.float32)
    wi[:, :, :, :IN_DIM] = f(inputs["w_in"]).reshape(DEPTH, KC, 128, IN_DIM)
    sh["w_in"] = np.ascontiguousarray(
        wi.reshape(DEPTH, KC, 128, NSLAB_IN, 512).transpose(0, 3, 2, 1, 4)).reshape(DEPTH * NSLAB_IN * 128, KC * 512)
    del wi
    sh["w_merge"] = f(inputs["w_merge"]).reshape(DEPTH * D, 3 * D)
    sh["w_o"] = f(inputs["w_o"]).reshape(DEPTH * D, D)
    sh["w_uq"] = f(inputs["w_uq"]).reshape(DEPTH * 1536, 3072)
    sh["w_uk"] = f(inputs["w_uk"]).reshape(DEPTH * 512, 2048)
    sh["w_uv"] = f(inputs["w_uv"]).reshape(DEPTH * 512, 2048)
    sh["w_pa"] = f(inputs["w_proj_a"]).reshape(DEPTH * 2048, D)
    sh["w_pb"] = f(inputs["w_proj_b"]).reshape(DEPTH * 2048, D)
    sh["w_pc"] = f(inputs["w_proj_c"]).reshape(DEPTH * 2048, D)

    def bc(a):
        a = f(a)
        return np.ascontiguousarray(np.broadcast_to(a[:, None, :], (DEPTH, 128, a.shape[1]))).reshape(DEPTH * 128, -1)

    sh["pre_bc"] = bc(inputs["pre_norm"])
    sh["post_bc"] = bc(inputs["post_norm"])
    sh["qn_bc"] = bc(inputs["q_a_norm"])
    sh["kvn_bc"] = bc(inputs["kv_a_norm"])
    sh["sink_bc"] = bc(inputs["sinks"])
    sh["bf_bc"] = bc(inputs["b_f"])
    s_i = np.arange(128)[:, None]
    q_i = np.arange(128)[None, :]
    rt = f(inputs["rel_table"])
    d_cur = q_i - s_i
    d_prev = q_i - s_i + 128
    relb = np.zeros((128, 32, 2, 128), np.float32)
    relb[:, :, 0, :] = rt[t5_bucket_np(d_prev)].transpose(0, 2, 1)
    relb[:, :, 1, :] = rt[t5_bucket_np(d_cur)].transpose(0, 2, 1)
    sh["relb"] = relb.reshape(128, 32 * 2 * 128)
    cstb = np.zeros((128, 512), np.float32)
    cstb[:, 0:128] = np.eye(128)
    cstb[:, 128:256] = (s_i <= q_i)
    cstb[:, 256:384] = (s_i > q_i)
    cstb[:, 384:512] = (s_i <= q_i)
    sh["cstb"] = cstb.astype(ml_dtypes.bfloat16)
    c32 = np.zeros((128, 288), np.float32)
    c32[:, 0:128] = (s_i <= q_i)
    c32[:, 128:256] = 1.0
    half = 32
    c32[:, 256:288] = (np.float32(10000.0) ** (-np.arange(half, dtype=np.float32) / np.float32(half)))[None, :]
    sh["cst32"] = c32
    return sh


def core_inputs(inputs, sh, c):
    m = dict(sh)
    m["x"] = np.ascontiguousarray(np.asarray(inputs["x"][c], dtype=np.float32))
    p = np.asarray(inputs["positions"][c], dtype=np.int32)
    m["pos"] = np.ascontiguousarray(p.reshape(NT, 128).T)
    return m


def kernel(**inputs):
    sh = prep_shared(inputs)
    nc = build_nc()
    n = 8
    in_maps = [core_inputs(inputs, sh, c) for c in range(n)]
    res = run_bass_kernel_spmd(nc, in_maps, core_ids=list(range(n)))
    return np.stack([r["out"] for r in res.results], axis=0).astype(np.float32)
```

```python
import math
from contextlib import ExitStack
import numpy as np
import ml_dtypes
import concourse.bass as bass
import concourse.mybir as mybir
from concourse.bass_utils import run_bass_kernel_spmd

F32 = mybir.dt.float32
BF16 = mybir.dt.bfloat16
I32 = mybir.dt.int32
AF = mybir.ActivationFunctionType
ALU = mybir.AluOpType
AX = mybir.AxisListType

T = 2048
NT = 16
D = 4096
KC = 32
DEPTH = 2
EPS = 1e-6
IN_SIZES = (1536, 512, 64, 2048, 2048, 256, 256, 2048, 2048, 2048, 2048, 16, 2048)
IN_DIM = sum(IN_SIZES)
OFFS = [0]
for _n in IN_SIZES:
    OFFS.append(OFFS[-1] + _n)
(O_CQ, O_CKV, O_KR, O_AZ, O_BQ, O_BK, O_BV, O_BZ, O_CQ2, O_CK, O_CV, O_CF, O_CZ) = OFFS[:13]
SEM_LIMIT = 30000
NSLAB_IN = (IN_DIM + 511) // 512

ENGS = ["pe", "act", "dve", "pool", "sp"]
BLK = {"pe": "tensor", "act": "scalar", "dve": "vector", "pool": "gpsimd", "sp": "sync"}


class Res:
    __slots__ = ("name", "w", "r", "excl")

    def __init__(self, name, excl=False):
        self.name = name
        self.w = []
        self.r = {}
        self.excl = excl


class Op:
    __slots__ = ("eng", "fn", "deps", "signal", "val", "dma", "ep", "ph")


class Prog:
    NSLOT = 8

    def __init__(self, nc, es):
        self.nc = nc
        self.es = es
        self.ops = {e: [] for e in ENGS}
        self.sems = {}
        self.last = {}
        self.dmas = {}
        self.phase = "setup"
        self.scopes = False

    def _new(self, eng, fn, dma):
        o = Op()
        o.eng = eng
        o.fn = fn
        o.dma = dma
        o.signal = dma is not None
        o.val = None
        o.ep = 0
        o.deps = []
        o.ph = self.phase
        return o

    def op(self, eng, fn, R=(), W=(), dma=None):
        o = self._new(eng, fn, dma)
        deps = {}
        if dma is not None:
            lst = self.dmas.setdefault(dma, [])
            n = len(lst)
            o.dma = (dma, n % self.NSLOT)
            if n >= self.NSLOT:
                prev = lst[n - self.NSLOT]
                deps[id(prev)] = (prev, True)
            lst.append(o)
        key = o.dma if o.dma else eng
        for r in R:
            for wo in r.w:
                deps[id(wo)] = (wo, True)
            if r.excl:
                for k2, ro in r.r.items():
                    if k2 != key and id(ro) not in deps:
                        deps[id(ro)] = (ro, False)
        appendw = set()
        for w in W:
            if (o.dma is not None and w.w and not w.r
                    and all(x.dma is not None and x.dma[0] == o.dma[0] for x in w.w)):
                appendw.add(id(w))
                continue
            for wo in w.w:
                if id(wo) not in deps:
                    deps[id(wo)] = (wo, False)
            for ro in w.r.values():
                if id(ro) not in deps:
                    deps[id(ro)] = (ro, False)
        for d, raw in deps.values():
            if d is o:
                continue
            if d.dma is None and o.dma is None and d.eng == eng and eng == "pe":
                continue
            d.signal = True
            o.deps.append(d)
        for r in R:
            r.r[key] = o
        for w in W:
            if id(w) in appendw:
                w.w.append(o)
            else:
                w.w = [o]
                w.r = {}
        self.ops[eng].append(o)
        self.last[key] = o
        return o

    def barrier(self, engs=ENGS):
        lasts = list(self.last.values())
        for d in lasts:
            d.signal = True
        for e in engs:
            o = self._new(e, None, None)
            o.deps = list(lasts)
            self.ops[e].append(o)

    def final_wait(self, eng="sp"):
        self.barrier(engs=[eng])

    def emit(self):
        nc = self.nc
        cnt = {}
        for e in ENGS:
            for o in self.ops[e]:
                if not o.signal:
                    continue
                key = o.dma if o.dma else e
                inc = 16 if o.dma else 1
                ep, v = cnt.get(key, (0, 0))
                if v + inc > SEM_LIMIT:
                    ep, v = ep + 1, 0
                v += inc
                cnt[key] = (ep, v)
                o.ep, o.val = ep, v
        for key, (ep, v) in cnt.items():
            nm = key if isinstance(key, str) else f"{key[0]}{key[1]}"
            for k in range(ep + 1):
                self.sems[(key, k)] = self.es.enter_context(nc.semaphore(f"s_{nm}_{k}"))
        block = self.es.enter_context(nc.Block())
        for e in ENGS:
            ops = self.ops[e]

            def body(eng, ops=ops, e=e):
                seen = {}
                cur = None
                for o in ops:
                    need = {}
                    for d in o.deps:
                        k = (d.dma if d.dma else d.eng, d.ep)
                        if d.val > need.get(k, 0):
                            need[k] = d.val
                    for k, v in need.items():
                        if seen.get(k, 0) < v:
                            eng.wait_ge(self.sems[k], v)
                            seen[k] = v
                    if self.scopes and o.fn is not None and o.ph != cur:
                        if cur is not None:
                            nc.pop_named_scope(cur)
                        cur = o.ph
                        nc.push_named_scope(cur)
                    if o.fn is None:
                        continue
                    ins = o.fn(eng)
                    if o.signal:
                        key = o.dma if o.dma else e
                        ins.then_inc(self.sems[(key, o.ep)], 16 if o.dma else 1)
                if cur is not None:
                    nc.pop_named_scope(cur)

            getattr(block, BLK[e])(body)


ARENA_BYTES = 207 * 1024


class Builder:
    def __init__(self, nc, es, io, depth=DEPTH, stop=None, dumps=()):
        self.nc = nc
        self.es = es
        self.io = io
        self.depth = depth
        self.stop = stop
        self.dumps = set(dumps)
        self.P = Prog(nc, es)
        self.arena = es.enter_context(nc.sbuf_tensor("arena", [128, ARENA_BYTES // 2], BF16))
        self.base = 0
        self.off = 0
        self.ps = [es.enter_context(nc.psum_tensor(f"psb{i}", [128, 512], F32)) for i in range(8)]
        self.psr = [Res(f"psb{i}", excl=True) for i in range(8)]
        self.bankc = {}
        self.flip = 0
        self.scr = {}

    def alloc(self, shape, dt, name="t"):
        n = int(np.prod(shape[1:]))
        nb = n * (2 if dt == BF16 else 4)
        o = self.off
        assert o % 4 == 0
        self.off = o + (nb + 63) // 64 * 64
        assert self.off <= ARENA_BYTES, (name, self.off)
        a = self.arena[:, o // 2:(o + nb) // 2]
        if dt != BF16:
            a = a.bitcast(dt)
        if len(shape) == 3:
            a = a.rearrange("p (a b) -> p a b", a=shape[1], b=shape[2])
        elif len(shape) == 4:
            a = a.rearrange("p (a b c) -> p a b c", a=shape[1], b=shape[2], c=shape[3])
        if shape[0] != 128:
            a = a[0:shape[0]]
        return a, Res(name)

    def keep(self):
        self.base = self.off

    def new_phase(self, base=None, name=None):
        self.P.barrier()
        if name is not None:
            self.P.phase = name
        if base is not None:
            self.base = base
        self.off = self.base

    def dram(self, name, shape, dt):
        kind = "ExternalOutput" if name in self.dumps else "Internal"
        t = self.nc.dram_tensor(name, list(shape), dt, kind=kind).ap()
        self.scr[name] = t
        return t

    def nb(self, lo=0, hi=8):
        b = self.bankc.get((lo, hi), lo)
        self.bankc[(lo, hi)] = lo + (b + 1 - lo) % (hi - lo)
        return b

    def alt(self):
        self.flip ^= 1
        return "act" if self.flip else "dve"

    def copy(self, eng, out, in_, R, W):
        if eng == "act":
            self.P.op("act", lambda e: e.copy(out=out, in_=in_), R=R, W=W)
        else:
            self.P.op(eng, lambda e: e.tensor_copy(out=out, in_=in_), R=R, W=W)

    def load(self, out, in_, W, q="sp", st="ld"):
        self.P.op(q, lambda e: e.dma_start(out=out, in_=in_), W=W, dma=st)

    def store(self, out, in_, R, q="sp", st="st"):
        self.P.op(q, lambda e: e.dma_start(out=out, in_=in_), R=R, dma=st)

    def transposes(self, src, src_res, cols, dst_fn, dst_res, eng, rows=128, banks=(0, 8)):
        P = self.P
        if isinstance(cols, int):
            cols = [c * rows for c in range(cols)]
        nblk = len(cols)
        g0 = 0
        while g0 < nblk:
            n = min(4, nblk - g0)
            b = self.nb(*banks)
            ps, psr = self.ps[b], self.psr[b]
            for k in range(n):
                c0 = cols[g0 + k]
                P.op("pe", lambda e, ps=ps, k=k, c0=c0: e.matmul(
                    ps[0:rows, k * 128:(k + 1) * 128], lhsT=src[:, c0:c0 + rows], rhs=self.ident,
                    start=True, stop=True), R=[src_res, self.cres], W=[psr])
            dst = dst_fn(g0, n)
            pv = ps[0:rows, 0:n * 128].rearrange("p (a b) -> p a b", a=n, b=128)
            self.copy(eng, dst, pv, [psr], [dst_res])
            g0 += n

    def rstd_from_ss(self, ss, ssr, n, col):
        P = self.P
        P.op("dve", lambda e: e.tensor_scalar(out=ss[:, col + 1:col + 2], in0=ss[:, col:col + 1], scalar1=1.0 / n,
                                              scalar2=EPS, op0=ALU.mult, op1=ALU.add), R=[ssr], W=[ssr])
        P.op("act", lambda e: e.activation(out=ss[:, col + 1:col + 2], in_=ss[:, col + 1:col + 2], func=AF.Sqrt),
             R=[ssr], W=[ssr])
        P.op("dve", lambda e: e.reciprocal(out=ss[:, col + 2:col + 3], in_=ss[:, col + 1:col + 2]), R=[ssr], W=[ssr])

    def gemm(self, AT, ATres, kc, wsrc, ncols, slab_w, evac, tiles=range(NT), banks=(0, 8), wbufs=None, slab_src=None):
        P = self.P
        if wbufs is None:
            wbufs = [self.alloc([128, kc, slab_w], BF16, f"wb{i}") for i in range(2)]
        wv = wsrc.rearrange("(c p) n -> p c n", p=128) if slab_src is None else None
        nslab = (ncols + slab_w - 1) // slab_w

        def issue(s):
            c0 = s * slab_w
            w = min(slab_w, ncols - c0)
            wb, wr = wbufs[s % 2]
            step = 8 if kc >= 8 else kc
            for k0 in range(0, kc, step):
                k1 = min(kc, k0 + step)
                src = wv[:, k0:k1, c0:c0 + w] if slab_src is None else slab_src(s)[:, k0:k1, 0:w]
                P.op("pool", lambda e, wb=wb, k0=k0, k1=k1, w=w, src=src: e.dma_start(
                    out=wb[:, k0:k1, 0:w], in_=src), W=[wr], dma="wld")

        issue(0)
        for s in range(nslab):
            if s + 1 < nslab:
                issue(s + 1)
            c0 = s * slab_w
            w = min(slab_w, ncols - c0)
            wb, wr = wbufs[s % 2]
            for i in tiles:
                b = self.nb(*banks)
                ps, psr = self.ps[b], self.psr[b]
                for c in range(kc):
                    P.op("pe", lambda e, ps=ps, wb=wb, c=c, i=i, w=w: e.matmul(
                        ps[:, 0:w], lhsT=AT[:, c, i * 128:(i + 1) * 128], rhs=wb[:, c, 0:w],
                        start=(c == 0), stop=(c == kc - 1)), R=[wr, ATres[i]], W=[psr])
                evac(i, c0, w, ps, psr)

    def setup(self):
        P, io = self.P, self.io
        self.cres = Res("consts")
        cb, _ = self.alloc([128, 512], BF16, "cstb")
        c32, _ = self.alloc([128, 288], F32, "cst32")
        self.load(cb, io["cstb"], [self.cres])
        self.load(c32, io["cst32"], [self.cres])
        self.ident = cb[:, 0:128]
        self.maskT = cb[:, 128:256]
        self.bmask = cb[:, 256:512]
        self.tri = c32[:, 0:128]
        self.ones32 = c32[:, 128:256]
        invf = c32[:, 256:288]
        self.cos, _ = self.alloc([128, NT, 32], F32, "cos")
        self.sin, _ = self.alloc([128, NT, 32], F32, "sin")
        self.cf, self.cfr = self.alloc([128, NT, 16], F32, "cf")
        self.keep()
        posi, pr = self.alloc([128, NT], I32, "posi")
        posf, pfr = self.alloc([128, NT], F32, "posf")
        u, ur = self.alloc([128, NT, 32], F32, "u")
        ui, uir = self.alloc([128, NT, 32], I32, "ui")
        uk, ukr = self.alloc([128, NT, 32], F32, "uk")
        self.load(posi, io["pos"], [pr])
        P.op("dve", lambda e: e.tensor_copy(out=posf, in_=posi), R=[pr], W=[pfr])
        for i in range(NT):
            P.op("dve", lambda e, i=i: e.tensor_scalar(out=u[:, i, :], in0=invf, scalar1=posf[:, i:i + 1],
                                                      scalar2=1.0 / (2 * math.pi), op0=ALU.mult, op1=ALU.mult),
                 R=[pfr, self.cres], W=[ur])
        uf = u.rearrange("p a b -> p (a b)")
        uif = ui.rearrange("p a b -> p (a b)")
        ukf = uk.rearrange("p a b -> p (a b)")
        for dst, shift in ((self.sin, 0.0), (self.cos, 0.25)):
            df = dst.rearrange("p a b -> p (a b)")
            w, wr = self.alloc([128, NT * 32], F32, "w")
            P.op("dve", lambda e, w=w, shift=shift: e.tensor_scalar(out=w, in0=uf, scalar1=shift, scalar2=None,
                                                                    op0=ALU.add), R=[ur], W=[wr])
            P.op("dve", lambda e, w=w: e.tensor_copy(out=uif, in_=w), R=[wr], W=[uir])
            P.op("dve", lambda e: e.tensor_copy(out=ukf, in_=uif), R=[uir], W=[ukr])
            P.op("dve", lambda e, w=w: e.tensor_tensor(out=w, in0=w, in1=ukf, op=ALU.subtract), R=[wr, ukr], W=[wr])
            P.op("dve", lambda e, w=w: e.tensor_scalar(out=ukf, in0=w, scalar1=0.5, scalar2=None, op0=ALU.is_gt),
                 R=[wr], W=[ukr])
            P.op("dve", lambda e, w=w: e.tensor_tensor(out=w, in0=w, in1=ukf, op=ALU.subtract), R=[wr, ukr], W=[wr])
            P.op("dve", lambda e, w=w: e.tensor_scalar(out=ukf, in0=w, scalar1=-0.5, scalar2=None, op0=ALU.is_lt),
                 R=[wr], W=[ukr])
            P.op("dve", lambda e, w=w: e.tensor_tensor(out=w, in0=w, in1=ukf, op=ALU.add), R=[wr, ukr], W=[wr])
            P.op("act", lambda e, w=w, df=df: e.activation(out=df, in_=w, func=AF.Sin, scale=2 * math.pi),
                 R=[wr], W=[self.cres])
        self.PROJ = self.dram("PROJ", [T, 17408], BF16)
        self.G = self.dram("G", [T, 3 * D], BF16)
        self.QA = self.dram("QA", [T, 3072], BF16)
        self.KN = self.dram("KN", [T, 2048], BF16)
        self.VA = self.dram("VA", [T, 2048], BF16)
        self.OT = self.dram("OT", [3 * 2048, T], BF16)
        self.Y = self.dram("Y", [T, D], BF16)
        self.Z = self.dram("Z", [T, D], F32)
        self.X1 = self.dram("X1", [T, D], F32)

    def prenorm(self, l, xsrc):
        P, io = self.P, self.io
        self.new_phase(name=f"L{l}_prenorm")
        self.mark = self.base
        hT, _ = self.alloc([128, KC, T], BF16, "hT")
        self.keep()
        hTr = [Res(f"hT{i}") for i in range(NT)]
        gbc, gr = self.alloc([128, D], F32, "gbc")
        self.load(gbc, io["pre_bc"][l * 128:(l + 1) * 128, :], [gr])
        xs = [self.alloc([128, D], F32, f"xs{i}") for i in range(2)]
        hb, hbr = self.alloc([128, D], BF16, "hb")
        ss, ssr = self.alloc([128, 4], F32, "ss")
        for i in range(NT):
            x_, xr = xs[i % 2]
            self.load(x_, xsrc[i * 128:(i + 1) * 128, :], [xr])
            P.op("act", lambda e, x_=x_: e.activation(out=hb, in_=x_, func=AF.Square, accum_out=ss[:, 0:1]),
                 R=[xr], W=[hbr, ssr])
            self.rstd_from_ss(ss, ssr, D, 0)
            P.op("dve", lambda e, x_=x_: e.scalar_tensor_tensor(out=hb, in0=x_, scalar=ss[:, 2:3], in1=gbc,
                                                                op0=ALU.mult, op1=ALU.mult),
                 R=[xr, ssr, gr], W=[hbr])
            self.transposes(hb, hbr, KC, lambda g0, n, i=i: hT[:, g0:g0 + n, i * 128:(i + 1) * 128], hTr[i],
                            "act" if i % 2 else "dve")
        return hT, hTr

    def proj_in(self, l, hT, hTr):
        P, io = self.P, self.io
        self.new_phase(name=f"L{l}_proj_in")
        stg = [self.alloc([128, 512], BF16, f"stg{i}") for i in range(4)]
        cnt = [0]

        def evac(i, c0, w, ps, psr):
            s_, sr = stg[cnt[0] % 4]
            cnt[0] += 1
            self.copy("act", s_[:, 0:w], ps[:, 0:w], [psr], [sr])
            self.store(self.PROJ[i * 128:(i + 1) * 128, c0:c0 + w], s_[:, 0:w], [sr])
            if c0 <= O_CF < c0 + w:
                o = O_CF - c0
                P.op("act", lambda e: e.copy(out=self.cf[:, i, :], in_=ps[:, o:o + 16]), R=[psr], W=[self.cfr])

        def slab_src(s):
            r0 = (l * NSLAB_IN + s) * 128
            return io["w_in"][r0:r0 + 128, :].rearrange("p (c n) -> p c n", c=KC, n=512)

        self.gemm(hT, hTr, KC, None, IN_DIM, 512, evac, slab_src=slab_src)

    def gates(self, l, hT, hTr):
        P, io = self.P, self.io
        self.new_phase(name=f"L{l}_gates")
        stg = [self.alloc([128, 512], BF16, f"gstg{i}") for i in range(4)]
        cnt = [0]

        def evac(i, c0, w, ps, psr):
            s_, sr = stg[cnt[0] % 4]
            cnt[0] += 1
            P.op("act", lambda e: e.activation(out=s_[:, 0:w], in_=ps[:, 0:w], func=AF.Sigmoid), R=[psr], W=[sr])
            self.store(self.G[i * 128:(i + 1) * 128, c0:c0 + w], s_[:, 0:w], [sr])

        self.gemm(hT, hTr, KC, io["w_merge"][l * D:(l + 1) * D, :], 3 * D, 512, evac)

    def evac_store(self, dst, dt=BF16, nbuf=4):
        stg = [self.alloc([128, 512], dt, f"es{i}") for i in range(nbuf)]
        cnt = [0]

        def evac(i, c0, w, ps, psr):
            s_, sr = stg[cnt[0] % nbuf]
            cnt[0] += 1
            self.copy("act", s_[:, 0:w], ps[:, 0:w], [psr], [sr])
            self.store(dst[i * 128:(i + 1) * 128, c0:c0 + w], s_[:, 0:w], [sr])

        return evac

    def mla(self, l):
        P, io = self.P, self.io
        self.new_phase(name=f"L{l}_mla_prep")
        mark = self.base
        self.KRT, self.KRTr = self.alloc([128, 1, T], BF16, "KRT")
        krt_base = self.off
        cqT, _ = self.alloc([128, 12, T], BF16, "cqT")
        ckvT, _ = self.alloc([128, 4, T], BF16, "ckvT")
        cqTr = [Res(f"cqT{i}") for i in range(NT)]
        ckvTr = [Res(f"ckvT{i}") for i in range(NT)]
        self.keep()
        gq, gqr = self.alloc([128, 1536], F32, "gq")
        gkv, gkvr = self.alloc([128, 512], F32, "gkv")
        self.load(gq, io["qn_bc"][l * 128:(l + 1) * 128, :], [gqr])
        self.load(gkv, io["kvn_bc"][l * 128:(l + 1) * 128, :], [gkvr])
        tin = [self.alloc([128, 2112], BF16, f"tin{i}") for i in range(2)]
        junk, jr = self.alloc([128, 1536], BF16, "junk")
        cqn, cqnr = self.alloc([128, 1536], BF16, "cqn")
        ckvn, ckvnr = self.alloc([128, 512], BF16, "ckvn")
        ss, ssr = self.alloc([128, 8], F32, "ss")
        tt, ttr = self.alloc([128, 4, 32], F32, "tt")
        krr, krrr = self.alloc([128, 64], BF16, "krr")
        for i in range(NT):
            t_, tr = tin[i % 2]
            self.load(t_, self.PROJ[i * 128:(i + 1) * 128, 0:2112], [tr])
            P.op("act", lambda e, t_=t_: e.activation(out=junk, in_=t_[:, 0:1536], func=AF.Square,
                                                      accum_out=ss[:, 0:1]), R=[tr], W=[jr, ssr])
            P.op("act", lambda e, t_=t_: e.activation(out=junk[:, 0:512], in_=t_[:, 1536:2048], func=AF.Square,
                                                      accum_out=ss[:, 3:4]), R=[tr], W=[jr, ssr])
            self.rstd_from_ss(ss, ssr, 1536, 0)
            self.rstd_from_ss(ss, ssr, 512, 3)
            P.op("dve", lambda e, t_=t_: e.scalar_tensor_tensor(out=cqn, in0=t_[:, 0:1536], scalar=ss[:, 2:3], in1=gq,
                                                                op0=ALU.mult, op1=ALU.mult), R=[tr, ssr, gqr], W=[cqnr])
            P.op("dve", lambda e, t_=t_: e.scalar_tensor_tensor(out=ckvn, in0=t_[:, 1536:2048], scalar=ss[:, 5:6],
                                                                in1=gkv, op0=ALU.mult, op1=ALU.mult),
                 R=[tr, ssr, gkvr], W=[ckvnr])
            self.transposes(cqn, cqnr, 12, lambda g0, n, i=i: cqT[:, g0:g0 + n, i * 128:(i + 1) * 128], cqTr[i], "act")
            self.transposes(ckvn, ckvnr, 4, lambda g0, n, i=i: ckvT[:, g0:g0 + n, i * 128:(i + 1) * 128], ckvTr[i], "act")
            x1, x2 = t_[:, 2048:2080], t_[:, 2080:2112]
            cs, sn = self.cos[:, i, :], self.sin[:, i, :]
            for k, (a, b_) in enumerate(((x1, cs), (x2, sn), (x2, cs), (x1, sn))):
                P.op("pool", lambda e, k=k, a=a, b_=b_: e.tensor_tensor(out=tt[:, k, :], in0=a, in1=b_, op=ALU.mult),
                     R=[tr, self.cres], W=[ttr])
            P.op("pool", lambda e: e.tensor_tensor(out=krr[:, 0:32], in0=tt[:, 0, :], in1=tt[:, 1, :], op=ALU.subtract),
                 R=[ttr], W=[krrr])
            P.op("pool", lambda e: e.tensor_tensor(out=krr[:, 32:64], in0=tt[:, 2, :], in1=tt[:, 3, :], op=ALU.add),
                 R=[ttr], W=[krrr])
            self.transposes(krr, krrr, [0], lambda g0, n, i=i: self.KRT[0:64, 0:1, i * 128:(i + 1) * 128], self.KRTr,
                            "dve", rows=64)
        self.new_phase(name=f"L{l}_uq")
        qb = [self.alloc([128, 384], BF16, f"qb{i}") for i in range(3)]
        rt = [self.alloc([128, 4, 2, 32], F32, f"rt{i}") for i in range(2)]
        cnt = [0]

        def evac_q(i, c0, w, ps, psr):
            q_, qr = qb[cnt[0] % 3]
            t4, t4r = rt[cnt[0] % 2]
            cnt[0] += 1
            P.op("act", lambda e: e.copy(out=q_, in_=ps[:, 0:384]), R=[psr], W=[qr])
            pv = ps[:, 0:384].rearrange("p (h d) -> p h d", h=2, d=192)
            qv = q_.rearrange("p (h d) -> p h d", h=2, d=192)
            cs = self.cos[:, i:i + 1, :].broadcast_to([128, 2, 32])
            sn = self.sin[:, i:i + 1, :].broadcast_to([128, 2, 32])
            x1, x2 = pv[:, :, 128:160], pv[:, :, 160:192]
            for k, (a, b_) in enumerate(((x1, cs), (x2, sn), (x2, cs), (x1, sn))):
                P.op("dve", lambda e, k=k, a=a, b_=b_: e.tensor_tensor(out=t4[:, k, :, :], in0=a, in1=b_, op=ALU.mult),
                     R=[psr, self.cres], W=[t4r])
            P.op("dve", lambda e: e.tensor_tensor(out=qv[:, :, 128:160], in0=t4[:, 0, :, :], in1=t4[:, 1, :, :],
                                                  op=ALU.subtract), R=[t4r], W=[qr])
            P.op("dve", lambda e: e.tensor_tensor(out=qv[:, :, 160:192], in0=t4[:, 2, :, :], in1=t4[:, 3, :, :],
                                                  op=ALU.add), R=[t4r], W=[qr])
            self.store(self.QA[i * 128:(i + 1) * 128, c0:c0 + 384], q_, [qr])

        self.gemm(cqT, cqTr, 12, io["w_uq"][l * 1536:(l + 1) * 1536, :], 3072, 384, evac_q)
        self.new_phase(name=f"L{l}_ukv")
        wb = [self.alloc([128, 4, 512], BF16, f"wkv{i}") for i in range(2)]
        ev = self.evac_store(self.KN)
        self.gemm(ckvT, ckvTr, 4, io["w_uk"][l * 512:(l + 1) * 512, :], 2048, 512, ev, wbufs=wb)
        ev2 = self.evac_store(self.VA)
        self.gemm(ckvT, ckvTr, 4, io["w_uv"][l * 512:(l + 1) * 512, :], 2048, 512, ev2, wbufs=wb)
        self.new_phase(base=krt_base)
        return mark

    def attn_full(self, l, kind):
        P, io = self.P, self.io
        self.new_phase(name=f"L{l}_attn{kind}")
        isA = kind == "A"
        KRT = self.KRT if isA else None
        br = 0 if isA else 2
        scale = (192.0 if isA else 128.0) ** -0.5
        if not isA:
            bfb, bfr = self.alloc([128, 1, 16], F32, "bfb")
            self.load(bfb[:, 0, :], io["bf_bc"][l * 128:(l + 1) * 128, :], [bfr])
            lf, lfr = self.alloc([128, NT, 16], F32, "lf")
            ncum, ncr = self.alloc([128, NT, 16], F32, "ncum")
            nrun, nrr = self.alloc([128, NT + 1, 16], F32, "nrun")
            biasC, bcr = self.alloc([128, 136, 16], F32, "biasC")
            P.op("dve", lambda e: e.tensor_tensor(out=lf, in0=self.cf, in1=bfb.broadcast_to([128, NT, 16]), op=ALU.add),
                 R=[self.cfr, bfr], W=[lfr])
            P.op("act", lambda e: e.activation(out=lf, in_=lf, func=AF.Exp, scale=-1.0), R=[lfr], W=[lfr])
            P.op("act", lambda e: e.activation(out=lf, in_=lf, func=AF.Ln, bias=1.0, scale=1.0), R=[lfr], W=[lfr])
            P.op("pool", lambda e: e.memset(nrun[:, 0, :], 0.0), W=[nrr])
            for i in range(NT):
                b = self.nb(0, 4)
                ps, psr = self.ps[b], self.psr[b]
                P.op("pe", lambda e, ps=ps, i=i: e.matmul(ps[:, 0:16], lhsT=self.tri, rhs=lf[:, i, :], start=True,
                                                          stop=True), R=[lfr, self.cres], W=[psr])
                P.op("pe", lambda e, ps=ps, i=i: e.matmul(ps[:, 16:32], lhsT=self.ones32, rhs=lf[:, i, :], start=True,
                                                          stop=True), R=[lfr, self.cres], W=[psr])
                P.op("dve", lambda e, ps=ps, i=i: e.tensor_tensor(out=ncum[:, i, :], in0=ps[:, 0:16], in1=nrun[:, i, :],
                                                                  op=ALU.add), R=[psr, nrr], W=[ncr])
                P.op("dve", lambda e, ps=ps, i=i: e.tensor_tensor(out=nrun[:, i + 1, :], in0=ps[:, 16:32],
                                                                  in1=nrun[:, i, :], op=ALU.add), R=[psr, nrr], W=[nrr])
            for i in range(NT):
                b0 = i * (i + 1) // 2
                P.op("dve", lambda e, i=i, b0=b0: e.tensor_tensor(
                    out=biasC[:, b0:b0 + i + 1, :], in0=ncum[:, 0:i + 1, :],
                    in1=nrun[:, i:i + 1, :].broadcast_to([128, i + 1, 16]), op=ALU.subtract),
                     R=[ncr, nrr], W=[bcr])
        QT, _ = self.alloc([128, 4, T], BF16, "QT")
        KT, _ = self.alloc([128, 4, T], BF16, "KT")
        QTr = [Res(f"QT{i}") for i in range(NT)]
        KTr = [Res(f"KT{i}") for i in range(NT)]
        if isA:
            QRT, _ = self.alloc([128, 4, T], BF16, "QRT")
            QRTr = [Res(f"QRT{i}") for i in range(NT)]
        Va, _ = self.alloc([128, NT, 4, 130], BF16, "Vaug")
        Var = [Res(f"Va{i}") for i in range(NT)]
        SZ, SZr = self.alloc([128, NT, 512], BF16, "SZ")
        OG, _ = self.alloc([128, NT, 512], BF16, "OG")
        OGr = [Res(f"OG{i}") for i in range(NT)]
        OTs, OTsr = self.alloc([128, 4, T], BF16, "OTs")
        qw = 768 if isA else 512
        qtl = [self.alloc([128, qw], BF16, f"qtl{i}") for i in range(2)]
        ktl = [self.alloc([128, 512], BF16, f"ktl{i}") for i in range(2)]
        vtl = [self.alloc([128, 512], BF16, f"vtl{i}") for i in range(2)]
        PT = [(self.alloc([128, 512], BF16, f"PT{i}")[0], [Res(f"PT{i}_{k}") for k in range(4)]) for i in range(5)]
        rv = [self.alloc([128, 1], F32, f"rv{i}") for i in range(4)]
        P.op("pool", lambda e: e.memset(Va.rearrange("p a b c -> p (a b c)"), 1.0), W=Var)
        ptc = [0]
        rvc = [0]
        sset = [0]
        for hg in range(4):
            if isA:
                qsrc, ksrc, vsrc, zoff = self.QA, self.KN, self.VA, O_AZ
                qo, ko, vo = hg * 768, hg * 512, hg * 512
            else:
                qsrc = ksrc = vsrc = self.PROJ
                qo, ko, vo, zoff = O_CQ2 + hg * 512, O_CK + hg * 512, O_CV + hg * 512, O_CZ
            self.load(SZ, self.PROJ[:, zoff + hg * 512: zoff + (hg + 1) * 512].rearrange("(n p) d -> p n d", p=128), [SZr])
            P.op("act", lambda e: e.activation(out=SZ, in_=SZ, func=AF.Silu), R=[SZr], W=[SZr])
            for i in range(NT):
                q_, qr = qtl[i % 2]
                k_, kr = ktl[i % 2]
                v_, vr = vtl[i % 2]
                rows = slice(i * 128, (i + 1) * 128)
                self.load(q_, qsrc[rows, qo:qo + qw], [qr])
                self.load(k_, ksrc[rows, ko:ko + 512], [kr])
                self.load(v_, vsrc[rows, vo:vo + 512], [vr])
                tcols = slice(i * 128, (i + 1) * 128)
                if isA:
                    self.transposes(q_, qr, [h * 192 for h in range(4)], lambda g0, n: QT[:, g0:g0 + n, tcols], QTr[i],
                                    "dve", banks=(0, 4))
                    self.transposes(q_, qr, [h * 192 + 128 for h in range(4)], lambda g0, n: QRT[0:64, g0:g0 + n, tcols],
                                    QRTr[i], "dve", rows=64, banks=(0, 4))
                else:
                    self.transposes(q_, qr, 4, lambda g0, n: QT[:, g0:g0 + n, tcols], QTr[i], "dve", banks=(0, 4))
                self.transposes(k_, kr, 4, lambda g0, n: KT[:, g0:g0 + n, tcols], KTr[i], "dve", banks=(0, 4))
                P.op("pool", lambda e, v_=v_, i=i: e.tensor_copy(
                    out=Va[:, i, :, 0:128], in_=v_.rearrange("p (h d) -> p h d", h=4, d=128)), R=[vr], W=[Var[i]])
            for hh in range(4):
                h = hg * 4 + hh
                for Qb in range(4):
                    st = sset[0]
                    sset[0] ^= 1

                    def oacc(t):
                        b = 4 + st * 2 + t // 2
                        return self.ps[b][:, (t % 2) * 256:(t % 2) * 256 + 129], self.psr[b], self.ps[b], (t % 2) * 256

                    def emit_scores(j):
                        r = max(0, j - 4 * Qb)
                        n0 = r * 128
                        ncol = 512 - n0
                        b = self.nb(0, 4)
                        ps, psr = self.ps[b], self.psr[b]
                        q0 = Qb * 512 + n0
                        qres = QTr[4 * Qb + r:4 * Qb + 4]
                        P.op("pe", lambda e, ps=ps, j=j, q0=q0, ncol=ncol, hh=hh: e.matmul(
                            ps[:, 0:ncol], lhsT=KT[:, hh, j * 128:(j + 1) * 128], rhs=QT[:, hh, q0:q0 + ncol],
                            start=True, stop=not isA), R=[KTr[j]] + qres, W=[psr])
                        if isA:
                            P.op("pe", lambda e, ps=ps, j=j, q0=q0, ncol=ncol, hh=hh: e.matmul(
                                ps[:, 0:ncol], lhsT=KRT[0:64, 0, j * 128:(j + 1) * 128],
                                rhs=QRT[0:64, hh, q0:q0 + ncol], start=False, stop=True),
                                 R=[self.KRTr] + QRTr[4 * Qb + r:4 * Qb + 4], W=[psr])
                        return ps, psr, r, ncol

                    def emit_rest(j, sc):
                        ps, psr, r, ncol = sc
                        p_, prs = PT[ptc[0] % 5]
                        ptc[0] += 1
                        if isA:
                            P.op("act", lambda e, ps=ps, p_=p_, ncol=ncol: e.activation(
                                out=p_[:, 0:ncol], in_=ps[:, 0:ncol], func=AF.Exp, scale=scale), R=[psr], W=prs[0:4 - r])
                        else:
                            for t in range(r, 4):
                                i = 4 * Qb + t
                                bi = i * (i + 1) // 2 + j
                                cc = (t - r) * 128
                                P.op("act", lambda e, ps=ps, p_=p_, cc=cc, bi=bi, h=h: e.activation(
                                    out=p_[:, cc:cc + 128], in_=ps[:, cc:cc + 128], func=AF.Exp, scale=scale,
                                    bias=biasC[:, bi, h:h + 1]), R=[psr, bcr], W=[prs[t - r]])
                        if j >= 4 * Qb:
                            P.op("pool", lambda e, p_=p_: e.tensor_tensor(out=p_[:, 0:128], in0=p_[:, 0:128],
                                                                          in1=self.maskT, op=ALU.mult),
                                 R=[prs[0], self.cres], W=[prs[0]])
                        for t in range(r, 4):
                            i = 4 * Qb + t
                            oa, oar, _, _ = oacc(t)
                            cc = (t - r) * 128
                            P.op("pe", lambda e, oa=oa, p_=p_, cc=cc, j=j, hh=hh, i=i, t=t: e.matmul(
                                oa, lhsT=p_[:, cc:cc + 128], rhs=Va[:, j, hh, 0:129], start=(j == 0 and t % 2 == 0),
                                stop=(j == i), skip_group_check=(t % 2 == 1)), R=[prs[t - r], Var[j]], W=[oar])

                    nj = 4 * Qb + 4
                    pend = [emit_scores(0), emit_scores(1)]
                    for j in range(nj):
                        if j + 2 < nj:
                            pend.append(emit_scores(j + 2))
                        emit_rest(j, pend.pop(0))
                    for t in range(4):
                        i = 4 * Qb + t
                        oa, oar, bank, c0 = oacc(t)
                        r_, rr = rv[rvc[0] % 4]
                        rvc[0] += 1
                        P.op("dve", lambda e, r_=r_, bank=bank, c0=c0: e.reciprocal(out=r_, in_=bank[:, c0 + 128:c0 + 129]),
                             R=[oar], W=[rr])
                        P.op("dve", lambda e, r_=r_, bank=bank, c0=c0, i=i, hh=hh: e.scalar_tensor_tensor(
                            out=OG[:, i, hh * 128:(hh + 1) * 128], in0=bank[:, c0:c0 + 128], scalar=r_[:, 0:1],
                            in1=SZ[:, i, hh * 128:(hh + 1) * 128], op0=ALU.mult, op1=ALU.mult),
                             R=[oar, rr, SZr], W=[OGr[i]])
            for i in range(NT):
                tcols = slice(i * 128, (i + 1) * 128)
                self.transposes(OG[:, i, :], OGr[i], 4, lambda g0, n: OTs[:, g0:g0 + n, tcols], OTsr, "dve", banks=(0, 4))
            r0 = br * 2048 + hg * 512
            self.store(self.OT[r0:r0 + 512, :].rearrange("(h p) t -> p h t", p=128), OTs, [OTsr])

    def attn_swa(self, l):
        P, io = self.P, self.io
        self.new_phase(name=f"L{l}_swa")
        E, Er = self.alloc([128, 32, 2, 128], BF16, "E")
        esink, esr = self.alloc([128, 32], F32, "esink")
        self.load(esink, io["sink_bc"][l * 128:(l + 1) * 128, :], [esr])
        P.op("act", lambda e: e.activation(out=esink, in_=esink, func=AF.Exp), R=[esr], W=[esr])
        rb = [self.alloc([128, 8, 256], F32, f"rb{i}") for i in range(2)]
        bm = self.bmask.rearrange("p (o c) -> p o c", o=1, c=256).broadcast_to([128, 8, 256])
        for g in range(4):
            r_, rr = rb[g % 2]
            self.load(r_.rearrange("p a b -> p (a b)"), io["relb"][:, g * 2048:(g + 1) * 2048], [rr])
            P.op("act", lambda e, r_=r_: e.activation(out=r_, in_=r_, func=AF.Exp), R=[rr], W=[rr])
            P.op("dve", lambda e, r_=r_, g=g: e.tensor_tensor(
                out=E[:, g * 8:(g + 1) * 8, :, :].rearrange("p h a b -> p h (a b)"), in0=r_, in1=bm, op=ALU.mult),
                 R=[rr, self.cres], W=[Er])
        QT, _ = self.alloc([128, 4, T], BF16, "QTb")
        QTr = [Res(f"QTb{i}") for i in range(NT)]
        KT, _ = self.alloc([128, 1, T], BF16, "KTb")
        KTr = [Res(f"KTb{i}") for i in range(NT)]
        Vb, Vbr = self.alloc([128, NT, 66], BF16, "Vb")
        SZ, SZr = self.alloc([128, NT, 512], BF16, "SZb")
        OG, _ = self.alloc([128, NT, 512], BF16, "OGb")
        OGr = [Res(f"OGb{i}") for i in range(NT)]
        OTs, OTsr = self.alloc([128, 4, T], BF16, "OTsb")
        qtl = [self.alloc([128, 512], BF16, f"bq{i}") for i in range(2)]
        ktl = [self.alloc([128, 128], BF16, f"bk{i}") for i in range(2)]
        PTc = [self.alloc([128, 8, 128], BF16, f"PTc{i}") for i in range(2)]
        PTp = [self.alloc([128, 8, 128], BF16, f"PTp{i}") for i in range(2)]
        den = [self.alloc([128, 4, 1], F32, f"den{i}") for i in range(4)]
        tmp = [self.alloc([128, 4, 64], F32, f"tmpb{i}") for i in range(4)]
        P.op("pool", lambda e: e.memset(Vb.rearrange("p a b -> p (a b)"), 1.0), W=[Vbr])
        fc = [0]
        for kv in range(4):
            self.load(SZ, self.PROJ[:, O_BZ + kv * 512: O_BZ + (kv + 1) * 512].rearrange("(n p) d -> p n d", p=128), [SZr])
            P.op("act", lambda e: e.activation(out=SZ, in_=SZ, func=AF.Silu), R=[SZr], W=[SZr])
            self.load(Vb[:, :, 0:64], self.PROJ[:, O_BV + kv * 64: O_BV + (kv + 1) * 64].rearrange("(n p) d -> p n d", p=128),
                      [Vbr])
            for i in range(NT):
                q_, qr = qtl[i % 2]
                k_, kr = ktl[i % 2]
                rows = slice(i * 128, (i + 1) * 128)
                tcols = slice(i * 128, (i + 1) * 128)
                self.load(q_, self.PROJ[rows, O_BQ + kv * 512: O_BQ + (kv + 1) * 512], [qr])
                self.load(k_[:, 0:64], self.PROJ[rows, O_BK + kv * 64: O_BK + (kv + 1) * 64], [kr])
                self.load(k_[:, 64:128], self.PROJ[rows, O_BK + kv * 64: O_BK + (kv + 1) * 64], [kr])
                self.transposes(q_, qr, 4, lambda g0, n: QT[:, g0:g0 + n, tcols], QTr[i], "dve", banks=(6, 8))
                self.transposes(k_, kr, 1, lambda g0, n: KT[:, g0:g0 + n, tcols], KTr[i], "dve", banks=(6, 8))
            for n in range(NT):
                pc, pcr = PTc[n % 2]
                pp, ppr = PTp[n % 2]
                cols = slice(n * 128, (n + 1) * 128)
                pcols = slice((n - 1) * 128, n * 128)
                for which in ((1, 0) if n > 0 else (1,)):
                    bb = 0 if which == 1 else 2
                    kc_ = cols if which == 1 else pcols
                    kres = KTr[n] if which == 1 else KTr[n - 1]
                    dst, dstr = (pc, pcr) if which == 1 else (pp, ppr)
                    for hh in range(8):
                        m, r = hh // 2, hh % 2
                        ps, psr = self.ps[bb + r], self.psr[bb + r]
                        oc = m * 128
                        P.op("pe", lambda e, ps=ps, oc=oc, r=r, m=m, kc_=kc_, cols=cols: e.matmul(
                            ps[:, oc:oc + 128], lhsT=KT[64 * r:64 * r + 64, 0, kc_], rhs=QT[64 * r:64 * r + 64, m, cols],
                            start=True, stop=True), R=[kres, QTr[n]], W=[psr])
                    for r in range(2):
                        ps, psr = self.ps[bb + r], self.psr[bb + r]
                        P.op("act", lambda e, ps=ps, dst=dst, r=r: e.activation(
                            out=dst[:, r:8:2, :], in_=ps[:, 0:512].rearrange("p (a b) -> p a b", a=4, b=128),
                            func=AF.Exp, scale=0.125), R=[psr], W=[dstr])
                    P.op("dve", lambda e, dst=dst, which=which, kv=kv: e.tensor_tensor(
                        out=dst, in0=dst, in1=E[:, kv * 8:(kv + 1) * 8, which, :], op=ALU.mult), R=[dstr, Er], W=[dstr])
                for hh in range(8):
                    ob, obr = self.ps[4 + hh // 4], self.psr[4 + hh // 4]
                    oc = (hh % 4) * 66
                    if n > 0:
                        P.op("pe", lambda e, ob=ob, oc=oc, hh=hh, pp=pp, n=n: e.matmul(
                            ob[:, oc:oc + 65], lhsT=pp[:, hh, :], rhs=Vb[:, n - 1, 0:65], start=True, stop=False),
                             R=[ppr, Vbr], W=[obr])
                    P.op("pe", lambda e, ob=ob, oc=oc, hh=hh, pc=pc, n=n: e.matmul(
                        ob[:, oc:oc + 65], lhsT=pc[:, hh, :], rhs=Vb[:, n, 0:65], start=(n == 0), stop=True),
                         R=[pcr, Vbr], W=[obr])
                for half in range(2):
                    ob, obr = self.ps[4 + half], self.psr[4 + half]
                    d_, dr = den[fc[0] % 4]
                    t_, tr = tmp[fc[0] % 4]
                    fc[0] += 1
                    ov = ob[:, 0:264].rearrange("p (h d) -> p h d", h=4, d=66)
                    h0 = kv * 8 + half * 4
                    P.op("dve", lambda e, ov=ov, d_=d_, h0=h0: e.tensor_tensor(
                        out=d_, in0=ov[:, :, 64:65], in1=esink[:, h0:h0 + 4].rearrange("p (h o) -> p h o", h=4, o=1),
                        op=ALU.add), R=[obr, esr], W=[dr])
                    P.op("dve", lambda e, d_=d_: e.reciprocal(out=d_, in_=d_), R=[dr], W=[dr])
                    P.op("dve", lambda e, ov=ov, d_=d_, t_=t_: e.tensor_tensor(
                        out=t_, in0=ov[:, :, 0:64], in1=d_.broadcast_to([128, 4, 64]), op=ALU.mult), R=[obr, dr], W=[tr])
                    c0 = half * 256
                    P.op("pool", lambda e, t_=t_, c0=c0, n=n: e.tensor_tensor(
                        out=OG[:, n, c0:c0 + 256], in0=t_.rearrange("p h d -> p (h d)"), in1=SZ[:, n, c0:c0 + 256],
                        op=ALU.mult), R=[tr, SZr], W=[OGr[n]])
            for i in range(NT):
                tcols = slice(i * 128, (i + 1) * 128)
                self.transposes(OG[:, i, :], OGr[i], 4, lambda g0, n: OTs[:, g0:g0 + n, tcols], OTsr, "dve", banks=(6, 8))
            r0 = 2048 + kv * 512
            self.store(self.OT[r0:r0 + 512, :].rearrange("(h p) t -> p h t", p=128), OTs, [OTsr])

    def combine(self, l):
        P, io = self.P, self.io
        for half in range(2):
            self.new_phase(name=f"L{l}_combine")
            OTh, OThr = self.alloc([128, 48, 1024], BF16, "OTh")
            ov = self.OT[:, half * 1024:(half + 1) * 1024].rearrange("(c p) t -> p c t", p=128)
            for k0 in range(0, 48, 8):
                self.load(OTh[:, k0:k0 + 8, :], ov[:, k0:k0 + 8, :], [OThr])
            CW = 384
            NS = (D + CW - 1) // CW
            wb = [self.alloc([128, 48, CW], BF16, f"wc{i}") for i in range(2)]
            gt = [self.alloc([128, 3, CW], BF16, f"gt{i}") for i in range(3)]
            t1 = [self.alloc([128, CW], F32, f"t1_{i}") for i in range(2)]
            t2 = [self.alloc([128, CW], F32, f"t2_{i}") for i in range(2)]
            ys = [self.alloc([128, CW], BF16, f"ys{i}") for i in range(3)]
            pcs = [(self.alloc([128, 3, CW], F32, f"pc{i}")[0], [Res(f"pc{i}_{k}") for k in range(3)]) for i in range(3)]
            wsrc = [io[k][l * 2048:(l + 1) * 2048, :].rearrange("(c p) n -> p c n", p=128) for k in ("w_pa", "w_pb", "w_pc")]

            def issue(s):
                w_, wr = wb[s % 2]
                c0 = s * CW
                w = min(CW, D - c0)
                for b3 in range(3):
                    for k0 in (0, 8):
                        P.op("pool", lambda e, w_=w_, b3=b3, k0=k0, c0=c0, w=w: e.dma_start(
                            out=w_[:, b3 * 16 + k0:b3 * 16 + k0 + 8, 0:w], in_=wsrc[b3][:, k0:k0 + 8, c0:c0 + w]),
                             W=[wr], dma="wld")

            cnt = 0
            issue(0)
            for s in range(NS):
                if s + 1 < NS:
                    issue(s + 1)
                w_, wr = wb[s % 2]
                c0 = s * CW
                w = min(CW, D - c0)
                for ti in range(8):
                    i = half * 8 + ti
                    g_, gr = gt[cnt % 3]
                    a_, ar = t1[cnt % 2]
                    b_, br_ = t2[cnt % 2]
                    y_, yr = ys[cnt % 3]
                    cnt += 1
                    self.load(g_[:, :, 0:w], self.G[i * 128:(i + 1) * 128, :].rearrange("p (b f) -> p b f", b=3)[:, :, c0:c0 + w],
                              [gr])
                    pss = []
                    for b3 in range(3):
                        bk = self.nb(0, 6)
                        ps, psr = self.ps[bk], self.psr[bk]
                        pss.append((ps, psr))
                        for c in range(16):
                            P.op("pe", lambda e, ps=ps, w_=w_, b3=b3, c=c, ti=ti, w=w: e.matmul(
                                ps[:, 0:w], lhsT=OTh[:, b3 * 16 + c, ti * 128:(ti + 1) * 128], rhs=w_[:, b3 * 16 + c, 0:w],
                                start=(c == 0), stop=(c == 15)), R=[wr, OThr], W=[psr])
                    pc_, pcr = pcs[cnt % 3]
                    for b3 in range(3):
                        P.op("act", lambda e, pc_=pc_, b3=b3, ps=pss[b3][0], w=w: e.copy(out=pc_[:, b3, 0:w], in_=ps[:, 0:w]),
                             R=[pss[b3][1]], W=[pcr[b3]])
                    P.op("dve", lambda e, a_=a_, g_=g_, pc_=pc_, w=w: e.tensor_tensor(
                        out=a_[:, 0:w], in0=pc_[:, 0, 0:w], in1=g_[:, 0, 0:w], op=ALU.mult), R=[pcr[0], gr], W=[ar])
                    P.op("pool", lambda e, b_=b_, g_=g_, pc_=pc_, w=w: e.tensor_tensor(
                        out=b_[:, 0:w], in0=pc_[:, 1, 0:w], in1=g_[:, 1, 0:w], op=ALU.mult), R=[pcr[1], gr], W=[br_])
                    P.op("dve", lambda e, a_=a_, b_=b_, w=w: e.tensor_tensor(out=a_[:, 0:w], in0=a_[:, 0:w], in1=b_[:, 0:w],
                                                                           op=ALU.add), R=[ar, br_], W=[ar])
                    P.op("pool", lambda e, b_=b_, g_=g_, pc_=pc_, w=w: e.tensor_tensor(
                        out=b_[:, 0:w], in0=pc_[:, 2, 0:w], in1=g_[:, 2, 0:w], op=ALU.mult), R=[pcr[2], gr], W=[br_])
                    P.op("dve", lambda e, a_=a_, b_=b_, y_=y_, w=w: e.tensor_tensor(out=y_[:, 0:w], in0=a_[:, 0:w], in1=b_[:, 0:w],
                                                                                  op=ALU.add), R=[ar, br_], W=[yr])
                    self.store(self.Y[i * 128:(i + 1) * 128, c0:c0 + w], y_[:, 0:w], [yr])

    def out_proj(self, l):
        P, io = self.P, self.io
        self.new_phase(name=f"L{l}_yT")
        mark = self.base
        yT, _ = self.alloc([128, KC, T], BF16, "yT")
        yTr = [Res(f"yT{i}") for i in range(NT)]
        self.keep()
        yt = [self.alloc([128, D], BF16, f"yt{i}") for i in range(2)]
        for i in range(NT):
            y_, yr = yt[i % 2]
            self.load(y_, self.Y[i * 128:(i + 1) * 128, :], [yr])
            self.transposes(y_, yr, KC, lambda g0, n, i=i: yT[:, g0:g0 + n, i * 128:(i + 1) * 128], yTr[i],
                            "act" if i % 2 else "dve")
        self.new_phase(name=f"L{l}_w_o")
        ev = self.evac_store(self.Z, dt=F32, nbuf=2)
        self.gemm(yT, yTr, KC, io["w_o"][l * D:(l + 1) * D, :], D, 512, ev)
        self.new_phase(base=mark, name=f"L{l}_postnorm")

    def postnorm(self, l, xsrc, dst):
        P, io = self.P, self.io
        gbc, gr = self.alloc([128, D], F32, "gpost")
        self.load(gbc, io["post_bc"][l * 128:(l + 1) * 128, :], [gr])
        zs = [self.alloc([128, D], F32, f"zs{i}") for i in range(2)]
        xs = [self.alloc([128, D], F32, f"xr{i}") for i in range(2)]
        junk, jr = self.alloc([128, D], BF16, "junk")
        ss, ssr = self.alloc([128, 4], F32, "ss")
        for i in range(NT):
            z_, zr = zs[i % 2]
            x_, xr = xs[i % 2]
            rows = slice(i * 128, (i + 1) * 128)
            self.load(z_, self.Z[rows, :], [zr])
            self.load(x_, xsrc[rows, :], [xr])
            P.op("act", lambda e, z_=z_: e.activation(out=junk, in_=z_, func=AF.Square, accum_out=ss[:, 0:1]),
                 R=[zr], W=[jr, ssr])
            self.rstd_from_ss(ss, ssr, D, 0)
            P.op("dve", lambda e, z_=z_: e.scalar_tensor_tensor(out=z_, in0=z_, scalar=ss[:, 2:3], in1=gbc,
                                                                op0=ALU.mult, op1=ALU.mult), R=[zr, ssr, gr], W=[zr])
            P.op("pool", lambda e, z_=z_, x_=x_: e.tensor_tensor(out=z_, in0=z_, in1=x_, op=ALU.add), R=[zr, xr], W=[zr])
            self.store(dst[rows, :], z_, [zr])

    def build(self):
        io = self.io
        self.setup()
        xsrc = io["x"]
        for l in range(self.depth):
            hT, hTr = self.prenorm(l, xsrc)
            self.proj_in(l, hT, hTr)
            if self.stop == "proj":
                return
            self.gates(l, hT, hTr)
            self.new_phase(base=self.mark)
            if self.stop == "gates":
                return
            mark = self.mla(l)
            if self.stop == "mla":
                return
            self.attn_full(l, "A")
            self.new_phase(base=mark)
            if self.stop == "A":
                return
            self.attn_swa(l)
            if self.stop == "B":
                return
            self.attn_full(l, "C")
            if self.stop == "C":
                return
            self.combine(l)
            if self.stop == "Y":
                return
            self.out_proj(l)
            if self.stop == "Z":
                return
            dst = io["out"] if l == self.depth - 1 else self.X1
            self.postnorm(l, xsrc, dst)
            xsrc = dst


IN_SPECS = {
    "x": ([T, D], F32), "pos": ([128, NT], I32),
    "w_in": ([DEPTH * NSLAB_IN * 128, KC * 512], F32), "w_merge": ([DEPTH * D, 3 * D], F32), "w_o": ([DEPTH * D, D], F32),
    "w_uq": ([DEPTH * 1536, 3072], F32), "w_uk": ([DEPTH * 512, 2048], F32), "w_uv": ([DEPTH * 512, 2048], F32),
    "w_pa": ([DEPTH * 2048, D], F32), "w_pb": ([DEPTH * 2048, D], F32), "w_pc": ([DEPTH * 2048, D], F32),
    "pre_bc": ([DEPTH * 128, D], F32), "post_bc": ([DEPTH * 128, D], F32),
    "qn_bc": ([DEPTH * 128, 1536], F32), "kvn_bc": ([DEPTH * 128, 512], F32),
    "sink_bc": ([DEPTH * 128, 32], F32), "bf_bc": ([DEPTH * 128, 16], F32),
    "relb": ([128, 32 * 2 * 128], F32),
    "cstb": ([128, 512], BF16), "cst32": ([128, 288], F32),
}


def build_nc(depth=DEPTH, stop=None, dumps=(), scopes=False):
    nc = bass.Bass("TRN2", target_bir_lowering=False)
    io = {}
    for k, (shp, dt) in IN_SPECS.items():
        io[k] = nc.dram_tensor(k, list(shp), dt, kind="ExternalInput").ap()
    io["out"] = nc.dram_tensor("out", [T, D], F32, kind="ExternalOutput").ap()
    es = ExitStack()
    with es:
        B = Builder(nc, es, io, depth=depth, stop=stop, dumps=dumps)
        B.P.scopes = scopes
        B.build()
        B.P.final_wait()
        B.P.emit()
    return nc


def t5_bucket_np(dist):
    d = np.maximum(dist, 0)
    large = 16 + (np.log(np.maximum(d, 1).astype(np.float32) / np.float32(16)) / np.float32(math.log(128 / 16))
                  * np.float32(16)).astype(np.int32)
    large = np.minimum(large, 31)
    return np.where(d < 16, d, large)


def prep_shared(inputs):
    f = lambda a: np.ascontiguousarray(np.asarray(a, dtype=np.float32))
    sh = {}
    wi = np.zeros((DEPTH, KC, 128, NSLAB_IN * 512), np.float32)
    wi[:, :, :, :IN_DIM] = f(inputs["w_in"]).reshape(DEPTH, KC, 128, IN_DIM)
    sh["w_in"] = np.ascontiguousarray(
        wi.reshape(DEPTH, KC, 128, NSLAB_IN, 512).transpose(0, 3, 2, 1, 4)).reshape(DEPTH * NSLAB_IN * 128, KC * 512)
    del wi
    sh["w_merge"] = f(inputs["w_merge"]).reshape(DEPTH * D, 3 * D)
    sh["w_o"] = f(inputs["w_o"]).reshape(DEPTH * D, D)
    sh["w_uq"] = f(inputs["w_uq"]).reshape(DEPTH * 1536, 3072)
    sh["w_uk"] = f(inputs["w_uk"]).reshape(DEPTH * 512, 2048)
    sh["w_uv"] = f(inputs["w_uv"]).reshape(DEPTH * 512, 2048)
    sh["w_pa"] = f(inputs["w_proj_a"]).reshape(DEPTH * 2048, D)
    sh["w_pb"] = f(inputs["w_proj_b"]).reshape(DEPTH * 2048, D)
    sh["w_pc"] = f(inputs["w_proj_c"]).reshape(DEPTH * 2048, D)

    def bc(a):
        a = f(a)
        return np.ascontiguousarray(np.broadcast_to(a[:, None, :], (DEPTH, 128, a.shape[1]))).reshape(DEPTH * 128, -1)

    sh["pre_bc"] = bc(inputs["pre_norm"])
    sh["post_bc"] = bc(inputs["post_norm"])
    sh["qn_bc"] = bc(inputs["q_a_norm"])
    sh["kvn_bc"] = bc(inputs["kv_a_norm"])
    sh["sink_bc"] = bc(inputs["sinks"])
    sh["bf_bc"] = bc(inputs["b_f"])
    s_i = np.arange(128)[:, None]
    q_i = np.arange(128)[None, :]
    rt = f(inputs["rel_table"])
    d_cur = q_i - s_i
    d_prev = q_i - s_i + 128
    relb = np.zeros((128, 32, 2, 128), np.float32)
    relb[:, :, 0, :] = rt[t5_bucket_np(d_prev)].transpose(0, 2, 1)
    relb[:, :, 1, :] = rt[t5_bucket_np(d_cur)].transpose(0, 2, 1)
    sh["relb"] = relb.reshape(128, 32 * 2 * 128)
    cstb = np.zeros((128, 512), np.float32)
    cstb[:, 0:128] = np.eye(128)
    cstb[:, 128:256] = (s_i <= q_i)
    cstb[:, 256:384] = (s_i > q_i)
    cstb[:, 384:512] = (s_i <= q_i)
    sh["cstb"] = cstb.astype(ml_dtypes.bfloat16)
    c32 = np.zeros((128, 288), np.float32)
    c32[:, 0:128] = (s_i <= q_i)
    c32[:, 128:256] = 1.0
    half = 32
    c32[:, 256:288] = (np.float32(10000.0) ** (-np.arange(half, dtype=np.float32) / np.float32(half)))[None, :]
    sh["cst32"] = c32
    return sh


def core_inputs(inputs, sh, c):
    m = dict(sh)
    m["x"] = np.ascontiguousarray(np.asarray(inputs["x"][c], dtype=np.float32))
    p = np.asarray(inputs["positions"][c], dtype=np.int32)
    m["pos"] = np.ascontiguousarray(p.reshape(NT, 128).T)
    return m


def kernel(**inputs):
    sh = prep_shared(inputs)
    nc = build_nc()
    n = 8
    in_maps = [core_inputs(inputs, sh, c) for c in range(n)]
    res = run_bass_kernel_spmd(nc, in_maps, core_ids=list(range(n)))
    return np.stack([r["out"] for r in res.results], axis=0).astype(np.float32)
```

```python
import math
from contextlib import ExitStack
import numpy as np
import ml_dtypes
import concourse.bass as bass
import concourse.mybir as mybir
from concourse.bass_utils import run_bass_kernel_spmd

F32 = mybir.dt.float32
BF16 = mybir.dt.bfloat16
I32 = mybir.dt.int32
AF = mybir.ActivationFunctionType
ALU = mybir.AluOpType
AX = mybir.AxisListType

T = 2048
NT = 16
D = 4096
KC = 32
DEPTH = 2
EPS = 1e-6
IN_SIZES = (1536, 512, 64, 2048, 2048, 256, 256, 2048, 2048, 2048, 2048, 16, 2048)
IN_DIM = sum(IN_SIZES)
OFFS = [0]
for _n in IN_SIZES:
    OFFS.append(OFFS[-1] + _n)
(O_CQ, O_CKV, O_KR, O_AZ, O_BQ, O_BK, O_BV, O_BZ, O_CQ2, O_CK, O_CV, O_CF, O_CZ) = OFFS[:13]
SEM_LIMIT = 30000
NSLAB_IN = (IN_DIM + 511) // 512

ENGS = ["pe", "act", "dve", "pool", "sp"]
BLK = {"pe": "tensor", "act": "scalar", "dve": "vector", "pool": "gpsimd", "sp": "sync"}


class Res:
    __slots__ = ("name", "w", "r", "excl")

    def __init__(self, name, excl=False):
        self.name = name
        self.w = []
        self.r = {}
        self.excl = excl


class Op:
    __slots__ = ("eng", "fn", "deps", "signal", "val", "dma", "ep", "ph")


class Prog:
    NSLOT = 8

    def __init__(self, nc, es):
        self.nc = nc
        self.es = es
        self.ops = {e: [] for e in ENGS}
        self.sems = {}
        self.last = {}
        self.dmas = {}
        self.phase = "setup"
        self.scopes = False

    def _new(self, eng, fn, dma):
        o = Op()
        o.eng = eng
        o.fn = fn
        o.dma = dma
        o.signal = dma is not None
        o.val = None
        o.ep = 0
        o.deps = []
        o.ph = self.phase
        return o

    def op(self, eng, fn, R=(), W=(), dma=None):
        o = self._new(eng, fn, dma)
        deps = {}
        if dma is not None:
            lst = self.dmas.setdefault(dma, [])
            n = len(lst)
            o.dma = (dma, n % self.NSLOT)
            if n >= self.NSLOT:
                prev = lst[n - self.NSLOT]
                deps[id(prev)] = (prev, True)
            lst.append(o)
        key = o.dma if o.dma else eng
        for r in R:
            for wo in r.w:
                deps[id(wo)] = (wo, True)
            if r.excl:
                for k2, ro in r.r.items():
                    if k2 != key and id(ro) not in deps:
                        deps[id(ro)] = (ro, False)
        appendw = set()
        for w in W:
            if (o.dma is not None and w.w and not w.r
                    and all(x.dma is not None and x.dma[0] == o.dma[0] for x in w.w)):
                appendw.add(id(w))
                continue
            for wo in w.w:
                if id(wo) not in deps:
                    deps[id(wo)] = (wo, False)
            for ro in w.r.values():
                if id(ro) not in deps:
                    deps[id(ro)] = (ro, False)
        for d, raw in deps.values():
            if d is o:
                continue
            if d.dma is None and o.dma is None and d.eng == eng and eng == "pe":
                continue
            d.signal = True
            o.deps.append(d)
        for r in R:
            r.r[key] = o
        for w in W:
            if id(w) in appendw:
                w.w.append(o)
            else:
                w.w = [o]
                w.r = {}
        self.ops[eng].append(o)
        self.last[key] = o
        return o

    def barrier(self, engs=ENGS):
        lasts = list(self.last.values())
        for d in lasts:
            d.signal = True
        for e in engs:
            o = self._new(e, None, None)
            o.deps = list(lasts)
            self.ops[e].append(o)

    def final_wait(self, eng="sp"):
        self.barrier(engs=[eng])

    def emit(self):
        nc = self.nc
        cnt = {}
        for e in ENGS:
            for o in self.ops[e]:
                if not o.signal:
                    continue
                key = o.dma if o.dma else e
                inc = 16 if o.dma else 1
                ep, v = cnt.get(key, (0, 0))
                if v + inc > SEM_LIMIT:
                    ep, v = ep + 1, 0
                v += inc
                cnt[key] = (ep, v)
                o.ep, o.val = ep, v
        for key, (ep, v) in cnt.items():
            nm = key if isinstance(key, str) else f"{key[0]}{key[1]}"
            for k in range(ep + 1):
                self.sems[(key, k)] = self.es.enter_context(nc.semaphore(f"s_{nm}_{k}"))
        block = self.es.enter_context(nc.Block())
        for e in ENGS:
            ops = self.ops[e]

            def body(eng, ops=ops, e=e):
                seen = {}
                cur = None
                for o in ops:
                    need = {}
                    for d in o.deps:
                        k = (d.dma if d.dma else d.eng, d.ep)
                        if d.val > need.get(k, 0):
                            need[k] = d.val
                    for k, v in need.items():
                        if seen.get(k, 0) < v:
                            eng.wait_ge(self.sems[k], v)
                            seen[k] = v
                    if self.scopes and o.fn is not None and o.ph != cur:
                        if cur is not None:
                            nc.pop_named_scope(cur)
                        cur = o.ph
                        nc.push_named_scope(cur)
                    if o.fn is None:
                        continue
                    ins = o.fn(eng)
                    if o.signal:
                        key = o.dma if o.dma else e
                        ins.then_inc(self.sems[(key, o.ep)], 16 if o.dma else 1)
                if cur is not None:
                    nc.pop_named_scope(cur)

            getattr(block, BLK[e])(body)


ARENA_BYTES = 207 * 1024


class Builder:
    def __init__(self, nc, es, io, depth=DEPTH, stop=None, dumps=()):
        self.nc = nc
        self.es = es
        self.io = io
        self.depth = depth
        self.stop = stop
        self.dumps = set(dumps)
        self.P = Prog(nc, es)
        self.arena = es.enter_context(nc.sbuf_tensor("arena", [128, ARENA_BYTES // 2], BF16))
        self.base = 0
        self.off = 0
        self.ps = [es.enter_context(nc.psum_tensor(f"psb{i}", [128, 512], F32)) for i in range(8)]
        self.psr = [Res(f"psb{i}", excl=True) for i in range(8)]
        self.bankc = {}
        self.flip = 0
        self.scr = {}

    def alloc(self, shape, dt, name="t"):
        n = int(np.prod(shape[1:]))
        nb = n * (2 if dt == BF16 else 4)
        o = self.off
        assert o % 4 == 0
        self.off = o + (nb + 63) // 64 * 64
        assert self.off <= ARENA_BYTES, (name, self.off)
        a = self.arena[:, o // 2:(o + nb) // 2]
        if dt != BF16:
            a = a.bitcast(dt)
        if len(shape) == 3:
            a = a.rearrange("p (a b) -> p a b", a=shape[1], b=shape[2])
        elif len(shape) == 4:
            a = a.rearrange("p (a b c) -> p a b c", a=shape[1], b=shape[2], c=shape[3])
        if shape[0] != 128:
            a = a[0:shape[0]]
        return a, Res(name)

    def keep(self):
        self.base = self.off

    def new_phase(self, base=None, name=None):
        self.P.barrier()
        if name is not None:
            self.P.phase = name
        if base is not None:
            self.base = base
        self.off = self.base

    def dram(self, name, shape, dt):
        kind = "ExternalOutput" if name in self.dumps else "Internal"
        t = self.nc.dram_tensor(name, list(shape), dt, kind=kind).ap()
        self.scr[name] = t
        return t

    def nb(self, lo=0, hi=8):
        b = self.bankc.get((lo, hi), lo)
        self.bankc[(lo, hi)] = lo + (b + 1 - lo) % (hi - lo)
        return b

    def alt(self):
        self.flip ^= 1
        return "act" if self.flip else "dve"

    def copy(self, eng, out, in_, R, W):
        if eng == "act":
            self.P.op("act", lambda e: e.copy(out=out, in_=in_), R=R, W=W)
        else:
            self.P.op(eng, lambda e: e.tensor_copy(out=out, in_=in_), R=R, W=W)

    def load(self, out, in_, W, q="sp", st="ld"):
        self.P.op(q, lambda e: e.dma_start(out=out, in_=in_), W=W, dma=st)

    def store(self, out, in_, R, q="sp", st="st"):
        self.P.op(q, lambda e: e.dma_start(out=out, in_=in_), R=R, dma=st)

    def transposes(self, src, src_res, cols, dst_fn, dst_res, eng, rows=128, banks=(0, 8)):
        P = self.P
        if isinstance(cols, int):
            cols = [c * rows for c in range(cols)]
        nblk = len(cols)
        g0 = 0
        while g0 < nblk:
            n = min(4, nblk - g0)
            b = self.nb(*banks)
            ps, psr = self.ps[b], self.psr[b]
            for k in range(n):
                c0 = cols[g0 + k]
                P.op("pe", lambda e, ps=ps, k=k, c0=c0: e.matmul(
                    ps[0:rows, k * 128:(k + 1) * 128], lhsT=src[:, c0:c0 + rows], rhs=self.ident,
                    start=True, stop=True), R=[src_res, self.cres], W=[psr])
            dst = dst_fn(g0, n)
            pv = ps[0:rows, 0:n * 128].rearrange("p (a b) -> p a b", a=n, b=128)
            self.copy(eng, dst, pv, [psr], [dst_res])
            g0 += n

    def rstd_from_ss(self, ss, ssr, n, col):
        P = self.P
        P.op("dve", lambda e: e.tensor_scalar(out=ss[:, col + 1:col + 2], in0=ss[:, col:col + 1], scalar1=1.0 / n,
                                              scalar2=EPS, op0=ALU.mult, op1=ALU.add), R=[ssr], W=[ssr])
        P.op("act", lambda e: e.activation(out=ss[:, col + 1:col + 2], in_=ss[:, col + 1:col + 2], func=AF.Sqrt),
             R=[ssr], W=[ssr])
        P.op("dve", lambda e: e.reciprocal(out=ss[:, col + 2:col + 3], in_=ss[:, col + 1:col + 2]), R=[ssr], W=[ssr])

    def gemm(self, AT, ATres, kc, wsrc, ncols, slab_w, evac, tiles=range(NT), banks=(0, 8), wbufs=None, slab_src=None):
        P = self.P
        if wbufs is None:
            wbufs = [self.alloc([128, kc, slab_w], BF16, f"wb{i}") for i in range(2)]
        wv = wsrc.rearrange("(c p) n -> p c n", p=128) if slab_src is None else None
        nslab = (ncols + slab_w - 1) // slab_w

        def issue(s):
            c0 = s * slab_w
            w = min(slab_w, ncols - c0)
            wb, wr = wbufs[s % 2]
            step = 8 if kc >= 8 else kc
            for k0 in range(0, kc, step):
                k1 = min(kc, k0 + step)
                src = wv[:, k0:k1, c0:c0 + w] if slab_src is None else slab_src(s)[:, k0:k1, 0:w]
                P.op("pool", lambda e, wb=wb, k0=k0, k1=k1, w=w, src=src: e.dma_start(
                    out=wb[:, k0:k1, 0:w], in_=src), W=[wr], dma="wld")

        issue(0)
        for s in range(nslab):
            if s + 1 < nslab:
                issue(s + 1)
            c0 = s * slab_w
            w = min(slab_w, ncols - c0)
            wb, wr = wbufs[s % 2]
            for i in tiles:
                b = self.nb(*banks)
                ps, psr = self.ps[b], self.psr[b]
                for c in range(kc):
                    P.op("pe", lambda e, ps=ps, wb=wb, c=c, i=i, w=w: e.matmul(
                        ps[:, 0:w], lhsT=AT[:, c, i * 128:(i + 1) * 128], rhs=wb[:, c, 0:w],
                        start=(c == 0), stop=(c == kc - 1)), R=[wr, ATres[i]], W=[psr])
                evac(i, c0, w, ps, psr)

    def setup(self):
        P, io = self.P, self.io
        self.cres = Res("consts")
        cb, _ = self.alloc([128, 512], BF16, "cstb")
        c32, _ = self.alloc([128, 288], F32, "cst32")
        self.load(cb, io["cstb"], [self.cres])
        self.load(c32, io["cst32"], [self.cres])
        self.ident = cb[:, 0:128]
        self.maskT = cb[:, 128:256]
        self.bmask = cb[:, 256:512]
        self.tri = c32[:, 0:128]
        self.ones32 = c32[:, 128:256]
        invf = c32[:, 256:288]
        self.cos, _ = self.alloc([128, NT, 32], F32, "cos")
        self.sin, _ = self.alloc([128, NT, 32], F32, "sin")
        self.cf, self.cfr = self.alloc([128, NT, 16], F32, "cf")
        self.keep()
        posi, pr = self.alloc([128, NT], I32, "posi")
        posf, pfr = self.alloc([128, NT], F32, "posf")
        u, ur = self.alloc([128, NT, 32], F32, "u")
        ui, uir = self.alloc([128, NT, 32], I32, "ui")
        uk, ukr = self.alloc([128, NT, 32], F32, "uk")
        self.load(posi, io["pos"], [pr])
        P.op("dve", lambda e: e.tensor_copy(out=posf, in_=posi), R=[pr], W=[pfr])
        for i in range(NT):
            P.op("dve", lambda e, i=i: e.tensor_scalar(out=u[:, i, :], in0=invf, scalar1=posf[:, i:i + 1],
                                                      scalar2=1.0 / (2 * math.pi), op0=ALU.mult, op1=ALU.mult),
                 R=[pfr, self.cres], W=[ur])
        uf = u.rearrange("p a b -> p (a b)")
        uif = ui.rearrange("p a b -> p (a b)")
        ukf = uk.rearrange("p a b -> p (a b)")
        for dst, shift in ((self.sin, 0.0), (self.cos, 0.25)):
            df = dst.rearrange("p a b -> p (a b)")
            w, wr = self.alloc([128, NT * 32], F32, "w")
            P.op("dve", lambda e, w=w, shift=shift: e.tensor_scalar(out=w, in0=uf, scalar1=shift, scalar2=None,
                                                                    op0=ALU.add), R=[ur], W=[wr])
            P.op("dve", lambda e, w=w: e.tensor_copy(out=uif, in_=w), R=[wr], W=[uir])
            P.op("dve", lambda e: e.tensor_copy(out=ukf, in_=uif), R=[uir], W=[ukr])
            P.op("dve", lambda e, w=w: e.tensor_tensor(out=w, in0=w, in1=ukf, op=ALU.subtract), R=[wr, ukr], W=[wr])
            P.op("dve", lambda e, w=w: e.tensor_scalar(out=ukf, in0=w, scalar1=0.5, scalar2=None, op0=ALU.is_gt),
                 R=[wr], W=[ukr])
            P.op("dve", lambda e, w=w: e.tensor_tensor(out=w, in0=w, in1=ukf, op=ALU.subtract), R=[wr, ukr], W=[wr])
            P.op("dve", lambda e, w=w: e.tensor_scalar(out=ukf, in0=w, scalar1=-0.5, scalar2=None, op0=ALU.is_lt),
                 R=[wr], W=[ukr])
            P.op("dve", lambda e, w=w: e.tensor_tensor(out=w, in0=w, in1=ukf, op=ALU.add), R=[wr, ukr], W=[wr])
            P.op("act", lambda e, w=w, df=df: e.activation(out=df, in_=w, func=AF.Sin, scale=2 * math.pi),
                 R=[wr], W=[self.cres])
        self.PROJ = self.dram("PROJ", [T, 17408], BF16)
        self.G = self.dram("G", [T, 3 * D], BF16)
        self.QA = self.dram("QA", [T, 3072], BF16)
        self.KN = self.dram("KN", [T, 2048], BF16)
        self.VA = self.dram("VA", [T, 2048], BF16)
        self.OT = self.dram("OT", [3 * 2048, T], BF16)
        self.Y = self.dram("Y", [T, D], BF16)
        self.Z = self.dram("Z", [T, D], F32)
        self.X1 = self.dram("X1", [T, D], F32)

    def prenorm(self, l, xsrc):
        P, io = self.P, self.io
        self.new_phase(name=f"L{l}_prenorm")
        self.mark = self.base
        hT, _ = self.alloc([128, KC, T], BF16, "hT")
        self.keep()
        hTr = [Res(f"hT{i}") for i in range(NT)]
        gbc, gr = self.alloc([128, D], F32, "gbc")
        self.load(gbc, io["pre_bc"][l * 128:(l + 1) * 128, :], [gr])
        xs = [self.alloc([128, D], F32, f"xs{i}") for i in range(2)]
        hb, hbr = self.alloc([128, D], BF16, "hb")
        ss, ssr = self.alloc([128, 4], F32, "ss")
        for i in range(NT):
            x_, xr = xs[i % 2]
            self.load(x_, xsrc[i * 128:(i + 1) * 128, :], [xr])
            P.op("act", lambda e, x_=x_: e.activation(out=hb, in_=x_, func=AF.Square, accum_out=ss[:, 0:1]),
                 R=[xr], W=[hbr, ssr])
            self.rstd_from_ss(ss, ssr, D, 0)
            P.op("dve", lambda e, x_=x_: e.scalar_tensor_tensor(out=hb, in0=x_, scalar=ss[:, 2:3], in1=gbc,
                                                                op0=ALU.mult, op1=ALU.mult),
                 R=[xr, ssr, gr], W=[hbr])
            self.transposes(hb, hbr, KC, lambda g0, n, i=i: hT[:, g0:g0 + n, i * 128:(i + 1) * 128], hTr[i],
                            "act" if i % 2 else "dve")
        return hT, hTr

    def proj_in(self, l, hT, hTr):
        P, io = self.P, self.io
        self.new_phase(name=f"L{l}_proj_in")
        stg = [self.alloc([128, 512], BF16, f"stg{i}") for i in range(4)]
        cnt = [0]

        def evac(i, c0, w, ps, psr):
            s_, sr = stg[cnt[0] % 4]
            cnt[0] += 1
            self.copy("act", s_[:, 0:w], ps[:, 0:w], [psr], [sr])
            self.store(self.PROJ[i * 128:(i + 1) * 128, c0:c0 + w], s_[:, 0:w], [sr])
            if c0 <= O_CF < c0 + w:
                o = O_CF - c0
                P.op("act", lambda e: e.copy(out=self.cf[:, i, :], in_=ps[:, o:o + 16]), R=[psr], W=[self.cfr])

        def slab_src(s):
            r0 = (l * NSLAB_IN + s) * 128
            return io["w_in"][r0:r0 + 128, :].rearrange("p (c n) -> p c n", c=KC, n=512)

        self.gemm(hT, hTr, KC, None, IN_DIM, 512, evac, slab_src=slab_src)

    def gates(self, l, hT, hTr):
        P, io = self.P, self.io
        self.new_phase(name=f"L{l}_gates")
        stg = [self.alloc([128, 512], BF16, f"gstg{i}") for i in range(4)]
        cnt = [0]

        def evac(i, c0, w, ps, psr):
            s_, sr = stg[cnt[0] % 4]
            cnt[0] += 1
            P.op("act", lambda e: e.activation(out=s_[:, 0:w], in_=ps[:, 0:w], func=AF.Sigmoid), R=[psr], W=[sr])
            self.store(self.G[i * 128:(i + 1) * 128, c0:c0 + w], s_[:, 0:w], [sr])

        self.gemm(hT, hTr, KC, io["w_merge"][l * D:(l + 1) * D, :], 3 * D, 512, evac)

    def evac_store(self, dst, dt=BF16, nbuf=4):
        stg = [self.alloc([128, 512], dt, f"es{i}") for i in range(nbuf)]
        cnt = [0]

        def evac(i, c0, w, ps, psr):
            s_, sr = stg[cnt[0] % nbuf]
            cnt[0] += 1
            self.copy("act", s_[:, 0:w], ps[:, 0:w], [psr], [sr])
            self.store(dst[i * 128:(i + 1) * 128, c0:c0 + w], s_[:, 0:w], [sr])

        return evac

    def mla(self, l):
        P, io = self.P, self.io
        self.new_phase(name=f"L{l}_mla_prep")
        mark = self.base
        self.KRT, self.KRTr = self.alloc([128, 1, T], BF16, "KRT")
        krt_base = self.off
        cqT, _ = self.alloc([128, 12, T], BF16, "cqT")
        ckvT, _ = self.alloc([128, 4, T], BF16, "ckvT")
        cqTr = [Res(f"cqT{i}") for i in range(NT)]
        ckvTr = [Res(f"ckvT{i}") for i in range(NT)]
        self.keep()
        gq, gqr = self.alloc([128, 1536], F32, "gq")
        gkv, gkvr = self.alloc([128, 512], F32, "gkv")
        self.load(gq, io["qn_bc"][l * 128:(l + 1) * 128, :], [gqr])
        self.load(gkv, io["kvn_bc"][l * 128:(l + 1) * 128, :], [gkvr])
        tin = [self.alloc([128, 2112], BF16, f"tin{i}") for i in range(2)]
        junk, jr = self.alloc([128, 1536], BF16, "junk")
        cqn, cqnr = self.alloc([128, 1536], BF16, "cqn")
        ckvn, ckvnr = self.alloc([128, 512], BF16, "ckvn")
        ss, ssr = self.alloc([128, 8], F32, "ss")
        tt, ttr = self.alloc([128, 4, 32], F32, "tt")
        krr, krrr = self.alloc([128, 64], BF16, "krr")
        for i in range(NT):
            t_, tr = tin[i % 2]
            self.load(t_, self.PROJ[i * 128:(i + 1) * 128, 0:2112], [tr])
            P.op("act", lambda e, t_=t_: e.activation(out=junk, in_=t_[:, 0:1536], func=AF.Square,
                                                      accum_out=ss[:, 0:1]), R=[tr], W=[jr, ssr])
            P.op("act", lambda e, t_=t_: e.activation(out=junk[:, 0:512], in_=t_[:, 1536:2048], func=AF.Square,
                                                      accum_out=ss[:, 3:4]), R=[tr], W=[jr, ssr])
            self.rstd_from_ss(ss, ssr, 1536, 0)
            self.rstd_from_ss(ss, ssr, 512, 3)
            P.op("dve", lambda e, t_=t_: e.scalar_tensor_tensor(out=cqn, in0=t_[:, 0:1536], scalar=ss[:, 2:3], in1=gq,
                                                                op0=ALU.mult, op1=ALU.mult), R=[tr, ssr, gqr], W=[cqnr])
            P.op("dve", lambda e, t_=t_: e.scalar_tensor_tensor(out=ckvn, in0=t_[:, 1536:2048], scalar=ss[:, 5:6],
                                                                in1=gkv, op0=ALU.mult, op1=ALU.mult),
                 R=[tr, ssr, gkvr], W=[ckvnr])
            self.transposes(cqn, cqnr, 12, lambda g0, n, i=i: cqT[:, g0:g0 + n, i * 128:(i + 1) * 128], cqTr[i], "act")
            self.transposes(ckvn, ckvnr, 4, lambda g0, n, i=i: ckvT[:, g0:g0 + n, i * 128:(i + 1) * 128], ckvTr[i], "act")
            x1, x2 = t_[:, 2048:2080], t_[:, 2080:2112]
            cs, sn = self.cos[:, i, :], self.sin[:, i, :]
            for k, (a, b_) in enumerate(((x1, cs), (x2, sn), (x2, cs), (x1, sn))):
                P.op("pool", lambda e, k=k, a=a, b_=b_: e.tensor_tensor(out=tt[:, k, :], in0=a, in1=b_, op=ALU.mult),
                     R=[tr, self.cres], W=[ttr])
            P.op("pool", lambda e: e.tensor_tensor(out=krr[:, 0:32], in0=tt[:, 0, :], in1=tt[:, 1, :], op=ALU.subtract),
                 R=[ttr], W=[krrr])
            P.op("pool", lambda e: e.tensor_tensor(out=krr[:, 32:64], in0=tt[:, 2, :], in1=tt[:, 3, :], op=ALU.add),
                 R=[ttr], W=[krrr])
            self.transposes(krr, krrr, [0], lambda g0, n, i=i: self.KRT[0:64, 0:1, i * 128:(i + 1) * 128], self.KRTr,
                            "dve", rows=64)
        self.new_phase(name=f"L{l}_uq")
        qb = [self.alloc([128, 384], BF16, f"qb{i}") for i in range(3)]
        rt = [self.alloc([128, 4, 2, 32], F32, f"rt{i}") for i in range(2)]
        cnt = [0]

        def evac_q(i, c0, w, ps, psr):
            q_, qr = qb[cnt[0] % 3]
            t4, t4r = rt[cnt[0] % 2]
            cnt[0] += 1
            P.op("act", lambda e: e.copy(out=q_, in_=ps[:, 0:384]), R=[psr], W=[qr])
            pv = ps[:, 0:384].rearrange("p (h d) -> p h d", h=2, d=192)
            qv = q_.rearrange("p (h d) -> p h d", h=2, d=192)
            cs = self.cos[:, i:i + 1, :].broadcast_to([128, 2, 32])
            sn = self.sin[:, i:i + 1, :].broadcast_to([128, 2, 32])
            x1, x2 = pv[:, :, 128:160], pv[:, :, 160:192]
            for k, (a, b_) in enumerate(((x1, cs), (x2, sn), (x2, cs), (x1, sn))):
                P.op("dve", lambda e, k=k, a=a, b_=b_: e.tensor_tensor(out=t4[:, k, :, :], in0=a, in1=b_, op=ALU.mult),
                     R=[psr, self.cres], W=[t4r])
            P.op("dve", lambda e: e.tensor_tensor(out=qv[:, :, 128:160], in0=t4[:, 0, :, :], in1=t4[:, 1, :, :],
                                                  op=ALU.subtract), R=[t4r], W=[qr])
            P.op("dve", lambda e: e.tensor_tensor(out=qv[:, :, 160:192], in0=t4[:, 2, :, :], in1=t4[:, 3, :, :],
                                                  op=ALU.add), R=[t4r], W=[qr])
            self.store(self.QA[i * 128:(i + 1) * 128, c0:c0 + 384], q_, [qr])

        self.gemm(cqT, cqTr, 12, io["w_uq"][l * 1536:(l + 1) * 1536, :], 3072, 384, evac_q)
        self.new_phase(name=f"L{l}_ukv")
        wb = [self.alloc([128, 4, 512], BF16, f"wkv{i}") for i in range(2)]
        ev = self.evac_store(self.KN)
        self.gemm(ckvT, ckvTr, 4, io["w_uk"][l * 512:(l + 1) * 512, :], 2048, 512, ev, wbufs=wb)
        ev2 = self.evac_store(self.VA)
        self.gemm(ckvT, ckvTr, 4, io["w_uv"][l * 512:(l + 1) * 512, :], 2048, 512, ev2, wbufs=wb)
        self.new_phase(base=krt_base)
        return mark

    def attn_full(self, l, kind):
        P, io = self.P, self.io
        self.new_phase(name=f"L{l}_attn{kind}")
        isA = kind == "A"
        KRT = self.KRT if isA else None
        br = 0 if isA else 2
        scale = (192.0 if isA else 128.0) ** -0.5
        if not isA:
            bfb, bfr = self.alloc([128, 1, 16], F32, "bfb")
            self.load(bfb[:, 0, :], io["bf_bc"][l * 128:(l + 1) * 128, :], [bfr])
            lf, lfr = self.alloc([128, NT, 16], F32, "lf")
            ncum, ncr = self.alloc([128, NT, 16], F32, "ncum")
            nrun, nrr = self.alloc([128, NT + 1, 16], F32, "nrun")
            biasC, bcr = self.alloc([128, 136, 16], F32, "biasC")
            P.op("dve", lambda e: e.tensor_tensor(out=lf, in0=self.cf, in1=bfb.broadcast_to([128, NT, 16]), op=ALU.add),
                 R=[self.cfr, bfr], W=[lfr])
            P.op("act", lambda e: e.activation(out=lf, in_=lf, func=AF.Exp, scale=-1.0), R=[lfr], W=[lfr])
            P.op("act", lambda e: e.activation(out=lf, in_=lf, func=AF.Ln, bias=1.0, scale=1.0), R=[lfr], W=[lfr])
            P.op("pool", lambda e: e.memset(nrun[:, 0, :], 0.0), W=[nrr])
            for i in range(NT):
                b = self.nb(0, 4)
                ps, psr = self.ps[b], self.psr[b]
                P.op("pe", lambda e, ps=ps, i=i: e.matmul(ps[:, 0:16], lhsT=self.tri, rhs=lf[:, i, :], start=True,
                                                          stop=True), R=[lfr, self.cres], W=[psr])
                P.op("pe", lambda e, ps=ps, i=i: e.matmul(ps[:, 16:32], lhsT=self.ones32, rhs=lf[:, i, :], start=True,
                                                          stop=True), R=[lfr, self.cres], W=[psr])
                P.op("dve", lambda e, ps=ps, i=i: e.tensor_tensor(out=ncum[:, i, :], in0=ps[:, 0:16], in1=nrun[:, i, :],
                                                                  op=ALU.add), R=[psr, nrr], W=[ncr])
                P.op("dve", lambda e, ps=ps, i=i: e.tensor_tensor(out=nrun[:, i + 1, :], in0=ps[:, 16:32],
                                                                  in1=nrun[:, i, :], op=ALU.add), R=[psr, nrr], W=[nrr])
            for i in range(NT):
                b0 = i * (i + 1) // 2
                P.op("dve", lambda e, i=i, b0=b0: e.tensor_tensor(
                    out=biasC[:, b0:b0 + i + 1, :], in0=ncum[:, 0:i + 1, :],
                    in1=nrun[:, i:i + 1, :].broadcast_to([128, i + 1, 16]), op=ALU.subtract),
                     R=[ncr, nrr], W=[bcr])
        sets = []
        for k in range(2):
            QT_, _ = self.alloc([128, 4, T], BF16, f"QT{k}")
            KT_, _ = self.alloc([128, 4, T], BF16, f"KT{k}")
            QRT_ = self.alloc([128, 4, T], BF16, f"QRT{k}")[0] if isA else None
            Va_, _ = self.alloc([128, NT, 4, 130], BF16, f"Vaug{k}")
            rs = [[Res(f"{nm}{k}_{i}") for i in range(NT)] for nm in ("QT", "KT", "QRT", "Va")]
            P.op("pool", lambda e, Va_=Va_: e.memset(Va_.rearrange("p a b c -> p (a b c)"), 1.0), W=rs[3])
            sets.append((QT_, rs[0], KT_, rs[1], QRT_, rs[2], Va_, rs[3]))
        SZ, SZr = self.alloc([128, NT, 512], BF16, "SZ")
        OG, _ = self.alloc([128, NT, 512], BF16, "OG")
        OGr = [Res(f"OG{i}") for i in range(NT)]
        OTs, OTsr = self.alloc([128, 4, T], BF16, "OTs")
        qw = 768 if isA else 512
        qtl = [self.alloc([128, qw], BF16, f"qtl{i}") for i in range(2)]
        ktl = [self.alloc([128, 512], BF16, f"ktl{i}") for i in range(2)]
        vtl = [self.alloc([128, 512], BF16, f"vtl{i}") for i in range(2)]
        PT = [(self.alloc([128, 512], BF16, f"PT{i}")[0], [Res(f"PT{i}_{k}") for k in range(4)]) for i in range(5)]
        rv = [self.alloc([128, 1], F32, f"rv{i}") for i in range(4)]
        ptc = [0]
        rvc = [0]
        sset = [0]
        ldc = [0]

        def load_ops(hg, S):
            QT, QTr, KT, KTr, QRT, QRTr, Va, Var = S
            if isA:
                qsrc, ksrc, vsrc = self.QA, self.KN, self.VA
                qo, ko, vo = hg * 768, hg * 512, hg * 512
            else:
                qsrc = ksrc = vsrc = self.PROJ
                qo, ko, vo = O_CQ2 + hg * 512, O_CK + hg * 512, O_CV + hg * 512

            def one(i):
                q_, qr = qtl[ldc[0] % 2]
                k_, kr = ktl[ldc[0] % 2]
                v_, vr = vtl[ldc[0] % 2]
                ldc[0] += 1
                rows = slice(i * 128, (i + 1) * 128)
                self.load(q_, qsrc[rows, qo:qo + qw], [qr])
                self.load(k_, ksrc[rows, ko:ko + 512], [kr])
                self.load(v_, vsrc[rows, vo:vo + 512], [vr])
                tcols = slice(i * 128, (i + 1) * 128)
                if isA:
                    self.transposes(q_, qr, [h * 192 for h in range(4)], lambda g0, n: QT[:, g0:g0 + n, tcols], QTr[i],
                                    "dve", banks=(0, 4))
                    self.transposes(q_, qr, [h * 192 + 128 for h in range(4)], lambda g0, n: QRT[0:64, g0:g0 + n, tcols],
                                    QRTr[i], "dve", rows=64, banks=(0, 4))
                else:
                    self.transposes(q_, qr, 4, lambda g0, n: QT[:, g0:g0 + n, tcols], QTr[i], "dve", banks=(0, 4))
                self.transposes(k_, kr, 4, lambda g0, n: KT[:, g0:g0 + n, tcols], KTr[i], "dve", banks=(0, 4))
                P.op("pool", lambda e, v_=v_, i=i, Va=Va: e.tensor_copy(
                    out=Va[:, i, :, 0:128], in_=v_.rearrange("p (h d) -> p h d", h=4, d=128)), R=[vr], W=[Var[i]])

            return [lambda i=i: one(i) for i in range(NT)]

        for f in load_ops(0, sets[0]):
            f()
        for hg in range(4):
            QT, QTr, KT, KTr, QRT, QRTr, Va, Var = sets[hg % 2]
            nxt = load_ops(hg + 1, sets[(hg + 1) % 2]) if hg < 3 else []
            zoff = O_AZ if isA else O_CZ
            self.load(SZ, self.PROJ[:, zoff + hg * 512: zoff + (hg + 1) * 512].rearrange("(n p) d -> p n d", p=128), [SZr])
            P.op("act", lambda e: e.activation(out=SZ, in_=SZ, func=AF.Silu), R=[SZr], W=[SZr])
            for hh in range(4):
                h = hg * 4 + hh
                for Qb in range(4):
                    st = sset[0]
                    sset[0] ^= 1

                    def oacc(t):
                        b = 4 + st * 2 + t // 2
                        return self.ps[b][:, (t % 2) * 256:(t % 2) * 256 + 129], self.psr[b], self.ps[b], (t % 2) * 256

                    def emit_scores(j):
                        r = max(0, j - 4 * Qb)
                        n0 = r * 128
                        ncol = 512 - n0
                        b = self.nb(0, 4)
                        ps, psr = self.ps[b], self.psr[b]
                        q0 = Qb * 512 + n0
                        qres = QTr[4 * Qb + r:4 * Qb + 4]
                        P.op("pe", lambda e, ps=ps, j=j, q0=q0, ncol=ncol, hh=hh, KT=KT, QT=QT: e.matmul(
                            ps[:, 0:ncol], lhsT=KT[:, hh, j * 128:(j + 1) * 128], rhs=QT[:, hh, q0:q0 + ncol],
                            start=True, stop=not isA), R=[KTr[j]] + qres, W=[psr])
                        if isA:
                            P.op("pe", lambda e, ps=ps, j=j, q0=q0, ncol=ncol, hh=hh, QRT=QRT: e.matmul(
                                ps[:, 0:ncol], lhsT=KRT[0:64, 0, j * 128:(j + 1) * 128],
                                rhs=QRT[0:64, hh, q0:q0 + ncol], start=False, stop=True),
                                 R=[self.KRTr] + QRTr[4 * Qb + r:4 * Qb + 4], W=[psr])
                        return ps, psr, r, ncol

                    def emit_rest(j, sc):
                        ps, psr, r, ncol = sc
                        p_, prs = PT[ptc[0] % 5]
                        ptc[0] += 1
                        if isA:
                            P.op("act", lambda e, ps=ps, p_=p_, ncol=ncol: e.activation(
                                out=p_[:, 0:ncol], in_=ps[:, 0:ncol], func=AF.Exp, scale=scale), R=[psr], W=prs[0:4 - r])
                        else:
                            for t in range(r, 4):
                                i = 4 * Qb + t
                                bi = i * (i + 1) // 2 + j
                                cc = (t - r) * 128
                                P.op("act", lambda e, ps=ps, p_=p_, cc=cc, bi=bi, h=h: e.activation(
                                    out=p_[:, cc:cc + 128], in_=ps[:, cc:cc + 128], func=AF.Exp, scale=scale,
                                    bias=biasC[:, bi, h:h + 1]), R=[psr, bcr], W=[prs[t - r]])
                        if j >= 4 * Qb:
                            P.op("pool", lambda e, p_=p_: e.tensor_tensor(out=p_[:, 0:128], in0=p_[:, 0:128],
                                                                          in1=self.maskT, op=ALU.mult),
                                 R=[prs[0], self.cres], W=[prs[0]])
                        for t in range(r, 4):
                            i = 4 * Qb + t
                            oa, oar, _, _ = oacc(t)
                            cc = (t - r) * 128
                            P.op("pe", lambda e, oa=oa, p_=p_, cc=cc, j=j, hh=hh, i=i, t=t, Va=Va: e.matmul(
                                oa, lhsT=p_[:, cc:cc + 128], rhs=Va[:, j, hh, 0:129], start=(j == 0 and t % 2 == 0),
                                stop=(j == i), skip_group_check=(t % 2 == 1)), R=[prs[t - r], Var[j]], W=[oar])

                    nj = 4 * Qb + 4
                    pend = [emit_scores(0), emit_scores(1)]
                    for j in range(nj):
                        if j + 2 < nj:
                            pend.append(emit_scores(j + 2))
                        emit_rest(j, pend.pop(0))
                    for t in range(4):
                        i = 4 * Qb + t
                        oa, oar, bank, c0 = oacc(t)
                        r_, rr = rv[rvc[0] % 4]
                        rvc[0] += 1
                        P.op("dve", lambda e, r_=r_, bank=bank, c0=c0: e.reciprocal(out=r_, in_=bank[:, c0 + 128:c0 + 129]),
                             R=[oar], W=[rr])
                        P.op("dve", lambda e, r_=r_, bank=bank, c0=c0, i=i, hh=hh: e.scalar_tensor_tensor(
                            out=OG[:, i, hh * 128:(hh + 1) * 128], in0=bank[:, c0:c0 + 128], scalar=r_[:, 0:1],
                            in1=SZ[:, i, hh * 128:(hh + 1) * 128], op0=ALU.mult, op1=ALU.mult),
                             R=[oar, rr, SZr], W=[OGr[i]])
                    if nxt:
                        nxt.pop(0)()
            for i in range(NT):
                tcols = slice(i * 128, (i + 1) * 128)
                self.transposes(OG[:, i, :], OGr[i], 4, lambda g0, n: OTs[:, g0:g0 + n, tcols], OTsr, "dve", banks=(0, 4))
            r0 = br * 2048 + hg * 512
            self.store(self.OT[r0:r0 + 512, :].rearrange("(h p) t -> p h t", p=128), OTs, [OTsr])

    def attn_swa(self, l):
        P, io = self.P, self.io
        self.new_phase(name=f"L{l}_swa")
        E, Er = self.alloc([128, 32, 2, 128], BF16, "E")
        esink, esr = self.alloc([128, 32], F32, "esink")
        self.load(esink, io["sink_bc"][l * 128:(l + 1) * 128, :], [esr])
        P.op("act", lambda e: e.activation(out=esink, in_=esink, func=AF.Exp), R=[esr], W=[esr])
        rb = [self.alloc([128, 8, 256], F32, f"rb{i}") for i in range(2)]
        bm = self.bmask.rearrange("p (o c) -> p o c", o=1, c=256).broadcast_to([128, 8, 256])
        for g in range(4):
            r_, rr = rb[g % 2]
            self.load(r_.rearrange("p a b -> p (a b)"), io["relb"][:, g * 2048:(g + 1) * 2048], [rr])
            P.op("act", lambda e, r_=r_: e.activation(out=r_, in_=r_, func=AF.Exp), R=[rr], W=[rr])
            P.op("dve", lambda e, r_=r_, g=g: e.tensor_tensor(
                out=E[:, g * 8:(g + 1) * 8, :, :].rearrange("p h a b -> p h (a b)"), in0=r_, in1=bm, op=ALU.mult),
                 R=[rr, self.cres], W=[Er])
        QT, _ = self.alloc([128, 4, T], BF16, "QTb")
        QTr = [Res(f"QTb{i}") for i in range(NT)]
        KT, _ = self.alloc([128, 1, T], BF16, "KTb")
        KTr = [Res(f"KTb{i}") for i in range(NT)]
        Vb, Vbr = self.alloc([128, NT, 66], BF16, "Vb")
        SZ, SZr = self.alloc([128, NT, 512], BF16, "SZb")
        OG, _ = self.alloc([128, NT, 512], BF16, "OGb")
        OGr = [Res(f"OGb{i}") for i in range(NT)]
        OTs, OTsr = self.alloc([128, 4, T], BF16, "OTsb")
        qtl = [self.alloc([128, 512], BF16, f"bq{i}") for i in range(2)]
        ktl = [self.alloc([128, 128], BF16, f"bk{i}") for i in range(2)]
        PTc = [self.alloc([128, 8, 128], BF16, f"PTc{i}") for i in range(2)]
        PTp = [self.alloc([128, 8, 128], BF16, f"PTp{i}") for i in range(2)]
        den = [self.alloc([128, 4, 1], F32, f"den{i}") for i in range(4)]
        tmp = [self.alloc([128, 4, 64], F32, f"tmpb{i}") for i in range(4)]
        P.op("pool", lambda e: e.memset(Vb.rearrange("p a b -> p (a b)"), 1.0), W=[Vbr])
        fc = [0]
        for kv in range(4):
            self.load(SZ, self.PROJ[:, O_BZ + kv * 512: O_BZ + (kv + 1) * 512].rearrange("(n p) d -> p n d", p=128), [SZr])
            P.op("act", lambda e: e.activation(out=SZ, in_=SZ, func=AF.Silu), R=[SZr], W=[SZr])
            self.load(Vb[:, :, 0:64], self.PROJ[:, O_BV + kv * 64: O_BV + (kv + 1) * 64].rearrange("(n p) d -> p n d", p=128),
                      [Vbr])
            for i in range(NT):
                q_, qr = qtl[i % 2]
                k_, kr = ktl[i % 2]
                rows = slice(i * 128, (i + 1) * 128)
                tcols = slice(i * 128, (i + 1) * 128)
                self.load(q_, self.PROJ[rows, O_BQ + kv * 512: O_BQ + (kv + 1) * 512], [qr])
                self.load(k_[:, 0:64], self.PROJ[rows, O_BK + kv * 64: O_BK + (kv + 1) * 64], [kr])
                self.load(k_[:, 64:128], self.PROJ[rows, O_BK + kv * 64: O_BK + (kv + 1) * 64], [kr])
                self.transposes(q_, qr, 4, lambda g0, n: QT[:, g0:g0 + n, tcols], QTr[i], "dve", banks=(6, 8))
                self.transposes(k_, kr, 1, lambda g0, n: KT[:, g0:g0 + n, tcols], KTr[i], "dve", banks=(6, 8))
            for n in range(NT):
                pc, pcr = PTc[n % 2]
                pp, ppr = PTp[n % 2]
                cols = slice(n * 128, (n + 1) * 128)
                pcols = slice((n - 1) * 128, n * 128)
                for which in ((1, 0) if n > 0 else (1,)):
                    bb = 0 if which == 1 else 2
                    kc_ = cols if which == 1 else pcols
                    kres = KTr[n] if which == 1 else KTr[n - 1]
                    dst, dstr = (pc, pcr) if which == 1 else (pp, ppr)
                    for hh in range(8):
                        m, r = hh // 2, hh % 2
                        ps, psr = self.ps[bb + r], self.psr[bb + r]
                        oc = m * 128
                        P.op("pe", lambda e, ps=ps, oc=oc, r=r, m=m, kc_=kc_, cols=cols: e.matmul(
                            ps[:, oc:oc + 128], lhsT=KT[64 * r:64 * r + 64, 0, kc_], rhs=QT[64 * r:64 * r + 64, m, cols],
                            start=True, stop=True), R=[kres, QTr[n]], W=[psr])
                    for r in range(2):
                        ps, psr = self.ps[bb + r], self.psr[bb + r]
                        P.op("act", lambda e, ps=ps, dst=dst, r=r: e.activation(
                            out=dst[:, r:8:2, :], in_=ps[:, 0:512].rearrange("p (a b) -> p a b", a=4, b=128),
                            func=AF.Exp, scale=0.125), R=[psr], W=[dstr])
                    P.op("dve", lambda e, dst=dst, which=which, kv=kv: e.tensor_tensor(
                        out=dst, in0=dst, in1=E[:, kv * 8:(kv + 1) * 8, which, :], op=ALU.mult), R=[dstr, Er], W=[dstr])
                for hh in range(8):
                    ob, obr = self.ps[4 + hh // 4], self.psr[4 + hh // 4]
                    oc = (hh % 4) * 66
                    if n > 0:
                        P.op("pe", lambda e, ob=ob, oc=oc, hh=hh, pp=pp, n=n: e.matmul(
                            ob[:, oc:oc + 65], lhsT=pp[:, hh, :], rhs=Vb[:, n - 1, 0:65], start=True, stop=False),
                             R=[ppr, Vbr], W=[obr])
                    P.op("pe", lambda e, ob=ob, oc=oc, hh=hh, pc=pc, n=n: e.matmul(
                        ob[:, oc:oc + 65], lhsT=pc[:, hh, :], rhs=Vb[:, n, 0:65], start=(n == 0), stop=True),
                         R=[pcr, Vbr], W=[obr])
                for half in range(2):
                    ob, obr = self.ps[4 + half], self.psr[4 + half]
                    d_, dr = den[fc[0] % 4]
                    t_, tr = tmp[fc[0] % 4]
                    fc[0] += 1
                    ov = ob[:, 0:264].rearrange("p (h d) -> p h d", h=4, d=66)
                    h0 = kv * 8 + half * 4
                    P.op("dve", lambda e, ov=ov, d_=d_, h0=h0: e.tensor_tensor(
                        out=d_, in0=ov[:, :, 64:65], in1=esink[:, h0:h0 + 4].rearrange("p (h o) -> p h o", h=4, o=1),
                        op=ALU.add), R=[obr, esr], W=[dr])
                    P.op("dve", lambda e, d_=d_: e.reciprocal(out=d_, in_=d_), R=[dr], W=[dr])
                    P.op("dve", lambda e, ov=ov, d_=d_, t_=t_: e.tensor_tensor(
                        out=t_, in0=ov[:, :, 0:64], in1=d_.broadcast_to([128, 4, 64]), op=ALU.mult), R=[obr, dr], W=[tr])
                    c0 = half * 256
                    P.op("pool", lambda e, t_=t_, c0=c0, n=n: e.tensor_tensor(
                        out=OG[:, n, c0:c0 + 256], in0=t_.rearrange("p h d -> p (h d)"), in1=SZ[:, n, c0:c0 + 256],
                        op=ALU.mult), R=[tr, SZr], W=[OGr[n]])
            for i in range(NT):
                tcols = slice(i * 128, (i + 1) * 128)
                self.transposes(OG[:, i, :], OGr[i], 4, lambda g0, n: OTs[:, g0:g0 + n, tcols], OTsr, "dve", banks=(6, 8))
            r0 = 2048 + kv * 512
            self.store(self.OT[r0:r0 + 512, :].rearrange("(h p) t -> p h t", p=128), OTs, [OTsr])

    def combine(self, l):
        P, io = self.P, self.io
        for half in range(2):
            self.new_phase(name=f"L{l}_combine")
            OTh, OThr = self.alloc([128, 48, 1024], BF16, "OTh")
            ov = self.OT[:, half * 1024:(half + 1) * 1024].rearrange("(c p) t -> p c t", p=128)
            for k0 in range(0, 48, 8):
                self.load(OTh[:, k0:k0 + 8, :], ov[:, k0:k0 + 8, :], [OThr])
            CW = 384
            NS = (D + CW - 1) // CW
            wb = [self.alloc([128, 48, CW], BF16, f"wc{i}") for i in range(2)]
            gt = [self.alloc([128, 3, CW], BF16, f"gt{i}") for i in range(3)]
            t1 = [self.alloc([128, CW], F32, f"t1_{i}") for i in range(2)]
            t2 = [self.alloc([128, CW], F32, f"t2_{i}") for i in range(2)]
            ys = [self.alloc([128, CW], BF16, f"ys{i}") for i in range(3)]
            pcs = [(self.alloc([128, 3, CW], F32, f"pc{i}")[0], [Res(f"pc{i}_{k}") for k in range(3)]) for i in range(3)]
            wsrc = [io[k][l * 2048:(l + 1) * 2048, :].rearrange("(c p) n -> p c n", p=128) for k in ("w_pa", "w_pb", "w_pc")]

            def issue(s):
                w_, wr = wb[s % 2]
                c0 = s * CW
                w = min(CW, D - c0)
                for b3 in range(3):
                    for k0 in (0, 8):
                        P.op("pool", lambda e, w_=w_, b3=b3, k0=k0, c0=c0, w=w: e.dma_start(
                            out=w_[:, b3 * 16 + k0:b3 * 16 + k0 + 8, 0:w], in_=wsrc[b3][:, k0:k0 + 8, c0:c0 + w]),
                             W=[wr], dma="wld")

            cnt = 0
            issue(0)
            for s in range(NS):
                if s + 1 < NS:
                    issue(s + 1)
                w_, wr = wb[s % 2]
                c0 = s * CW
                w = min(CW, D - c0)
                for ti in range(8):
                    i = half * 8 + ti
                    g_, gr = gt[cnt % 3]
                    a_, ar = t1[cnt % 2]
                    b_, br_ = t2[cnt % 2]
                    y_, yr = ys[cnt % 3]
                    cnt += 1
                    self.load(g_[:, :, 0:w], self.G[i * 128:(i + 1) * 128, :].rearrange("p (b f) -> p b f", b=3)[:, :, c0:c0 + w],
                              [gr])
                    pss = []
                    for b3 in range(3):
                        bk = self.nb(0, 6)
                        ps, psr = self.ps[bk], self.psr[bk]
                        pss.append((ps, psr))
                        for c in range(16):
                            P.op("pe", lambda e, ps=ps, w_=w_, b3=b3, c=c, ti=ti, w=w: e.matmul(
                                ps[:, 0:w], lhsT=OTh[:, b3 * 16 + c, ti * 128:(ti + 1) * 128], rhs=w_[:, b3 * 16 + c, 0:w],
                                start=(c == 0), stop=(c == 15)), R=[wr, OThr], W=[psr])
                    pc_, pcr = pcs[cnt % 3]
                    for b3 in range(3):
                        P.op("act", lambda e, pc_=pc_, b3=b3, ps=pss[b3][0], w=w: e.copy(out=pc_[:, b3, 0:w], in_=ps[:, 0:w]),
                             R=[pss[b3][1]], W=[pcr[b3]])
                    P.op("dve", lambda e, a_=a_, g_=g_, pc_=pc_, w=w: e.tensor_tensor(
                        out=a_[:, 0:w], in0=pc_[:, 0, 0:w], in1=g_[:, 0, 0:w], op=ALU.mult), R=[pcr[0], gr], W=[ar])
                    P.op("pool", lambda e, b_=b_, g_=g_, pc_=pc_, w=w: e.tensor_tensor(
                        out=b_[:, 0:w], in0=pc_[:, 1, 0:w], in1=g_[:, 1, 0:w], op=ALU.mult), R=[pcr[1], gr], W=[br_])
                    P.op("dve", lambda e, a_=a_, b_=b_, w=w: e.tensor_tensor(out=a_[:, 0:w], in0=a_[:, 0:w], in1=b_[:, 0:w],
                                                                           op=ALU.add), R=[ar, br_], W=[ar])
                    P.op("pool", lambda e, b_=b_, g_=g_, pc_=pc_, w=w: e.tensor_tensor(
                        out=b_[:, 0:w], in0=pc_[:, 2, 0:w], in1=g_[:, 2, 0:w], op=ALU.mult), R=[pcr[2], gr], W=[br_])
                    P.op("dve", lambda e, a_=a_, b_=b_, y_=y_, w=w: e.tensor_tensor(out=y_[:, 0:w], in0=a_[:, 0:w], in1=b_[:, 0:w],
                                                                                  op=ALU.add), R=[ar, br_], W=[yr])
                    self.store(self.Y[i * 128:(i + 1) * 128, c0:c0 + w], y_[:, 0:w], [yr])

    def out_proj(self, l):
        P, io = self.P, self.io
        self.new_phase(name=f"L{l}_yT")
        mark = self.base
        yT, _ = self.alloc([128, KC, T], BF16, "yT")
        yTr = [Res(f"yT{i}") for i in range(NT)]
        self.keep()
        yt = [self.alloc([128, D], BF16, f"yt{i}") for i in range(2)]
        for i in range(NT):
            y_, yr = yt[i % 2]
            self.load(y_, self.Y[i * 128:(i + 1) * 128, :], [yr])
            self.transposes(y_, yr, KC, lambda g0, n, i=i: yT[:, g0:g0 + n, i * 128:(i + 1) * 128], yTr[i],
                            "act" if i % 2 else "dve")
        self.new_phase(name=f"L{l}_w_o")
        ev = self.evac_store(self.Z, dt=F32, nbuf=2)
        self.gemm(yT, yTr, KC, io["w_o"][l * D:(l + 1) * D, :], D, 512, ev)
        self.new_phase(base=mark, name=f"L{l}_postnorm")

    def postnorm(self, l, xsrc, dst):
        P, io = self.P, self.io
        gbc, gr = self.alloc([128, D], F32, "gpost")
        self.load(gbc, io["post_bc"][l * 128:(l + 1) * 128, :], [gr])
        zs = [self.alloc([128, D], F32, f"zs{i}") for i in range(2)]
        xs = [self.alloc([128, D], F32, f"xr{i}") for i in range(2)]
        junk, jr = self.alloc([128, D], BF16, "junk")
        ss, ssr = self.alloc([128, 4], F32, "ss")
        for i in range(NT):
            z_, zr = zs[i % 2]
            x_, xr = xs[i % 2]
            rows = slice(i * 128, (i + 1) * 128)
            self.load(z_, self.Z[rows, :], [zr])
            self.load(x_, xsrc[rows, :], [xr])
            P.op("act", lambda e, z_=z_: e.activation(out=junk, in_=z_, func=AF.Square, accum_out=ss[:, 0:1]),
                 R=[zr], W=[jr, ssr])
            self.rstd_from_ss(ss, ssr, D, 0)
            P.op("dve", lambda e, z_=z_: e.scalar_tensor_tensor(out=z_, in0=z_, scalar=ss[:, 2:3], in1=gbc,
                                                                op0=ALU.mult, op1=ALU.mult), R=[zr, ssr, gr], W=[zr])
            P.op("pool", lambda e, z_=z_, x_=x_: e.tensor_tensor(out=z_, in0=z_, in1=x_, op=ALU.add), R=[zr, xr], W=[zr])
            self.store(dst[rows, :], z_, [zr])

    def build(self):
        io = self.io
        self.setup()
        xsrc = io["x"]
        for l in range(self.depth):
            hT, hTr = self.prenorm(l, xsrc)
            self.proj_in(l, hT, hTr)
            if self.stop == "proj":
                return
            self.gates(l, hT, hTr)
            self.new_phase(base=self.mark)
            if self.stop == "gates":
                return
            mark = self.mla(l)
            if self.stop == "mla":
                return
            self.attn_full(l, "A")
            self.new_phase(base=mark)
            if self.stop == "A":
                return
            self.attn_swa(l)
            if self.stop == "B":
                return
            self.attn_full(l, "C")
            if self.stop == "C":
                return
            self.combine(l)
            if self.stop == "Y":
                return
            self.out_proj(l)
            if self.stop == "Z":
                return
            dst = io["out"] if l == self.depth - 1 else self.X1
            self.postnorm(l, xsrc, dst)
            xsrc = dst


IN_SPECS = {
    "x": ([T, D], F32), "pos": ([128, NT], I32),
    "w_in": ([DEPTH * NSLAB_IN * 128, KC * 512], F32), "w_merge": ([DEPTH * D, 3 * D], F32), "w_o": ([DEPTH * D, D], F32),
    "w_uq": ([DEPTH * 1536, 3072], F32), "w_uk": ([DEPTH * 512, 2048], F32), "w_uv": ([DEPTH * 512, 2048], F32),
    "w_pa": ([DEPTH * 2048, D], F32), "w_pb": ([DEPTH * 2048, D], F32), "w_pc": ([DEPTH * 2048, D], F32),
    "pre_bc": ([DEPTH * 128, D], F32), "post_bc": ([DEPTH * 128, D], F32),
    "qn_bc": ([DEPTH * 128, 1536], F32), "kvn_bc": ([DEPTH * 128, 512], F32),
    "sink_bc": ([DEPTH * 128, 32], F32), "bf_bc": ([DEPTH * 128, 16], F32),
    "relb": ([128, 32 * 2 * 128], F32),
    "cstb": ([128, 512], BF16), "cst32": ([128, 288], F32),
}


def build_nc(depth=DEPTH, stop=None, dumps=(), scopes=False):
    nc = bass.Bass("TRN2", target_bir_lowering=False)
    io = {}
    for k, (shp, dt) in IN_SPECS.items():
        io[k] = nc.dram_tensor(k, list(shp), dt, kind="ExternalInput").ap()
    io["out"] = nc.dram_tensor("out", [T, D], F32, kind="ExternalOutput").ap()
    es = ExitStack()
    with es:
        B = Builder(nc, es, io, depth=depth, stop=stop, dumps=dumps)
        B.P.scopes = scopes
        B.build()
        B.P.final_wait()
        B.P.emit()
    return nc


def t5_bucket_np(dist):
    d = np.maximum(dist, 0)
    large = 16 + (np.log(np.maximum(d, 1).astype(np.float32) / np.float32(16)) / np.float32(math.log(128 / 16))
                  * np.float32(16)).astype(np.int32)
    large = np.minimum(large, 31)
    return np.where(d < 16, d, large)


def prep_shared(inputs):
    f = lambda a: np.ascontiguousarray(np.asarray(a, dtype=np.float32))
    sh = {}
    wi = np.zeros((DEPTH, KC, 128, NSLAB_IN * 512), np.float32)
    wi[:, :, :, :IN_DIM] = f(inputs["w_in"]).reshape(DEPTH, KC, 128, IN_DIM)
    sh["w_in"] = np.ascontiguousarray(
        wi.reshape(DEPTH, KC, 128, NSLAB_IN, 512).transpose(0, 3, 2, 1, 4)).reshape(DEPTH * NSLAB_IN * 128, KC * 512)
    del wi
    sh["w_merge"] = f(inputs["w_merge"]).reshape(DEPTH * D, 3 * D)
    sh["w_o"] = f(inputs["w_o"]).reshape(DEPTH * D, D)
    sh["w_uq"] = f(inputs["w_uq"]).reshape(DEPTH * 1536, 3072)
    sh["w_uk"] = f(inputs["w_uk"]).reshape(DEPTH * 512, 2048)
    sh["w_uv"] = f(inputs["w_uv"]).reshape(DEPTH * 512, 2048)
    sh["w_pa"] = f(inputs["w_proj_a"]).reshape(DEPTH * 2048, D)
    sh["w_pb"] = f(inputs["w_proj_b"]).reshape(DEPTH * 2048, D)
    sh["w_pc"] = f(inputs["w_proj_c"]).reshape(DEPTH * 2048, D)

    def bc(a):
        a = f(a)
        return np.ascontiguousarray(np.broadcast_to(a[:, None, :], (DEPTH, 128, a.shape[1]))).reshape(DEPTH * 128, -1)

    sh["pre_bc"] = bc(inputs["pre_norm"])
    sh["post_bc"] = bc(inputs["post_norm"])
    sh["qn_bc"] = bc(inputs["q_a_norm"])
    sh["kvn_bc"] = bc(inputs["kv_a_norm"])
    sh["sink_bc"] = bc(inputs["sinks"])
    sh["bf_bc"] = bc(inputs["b_f"])
    s_i = np.arange(128)[:, None]
    q_i = np.arange(128)[None, :]
    rt = f(inputs["rel_table"])
    d_cur = q_i - s_i
    d_prev = q_i - s_i + 128
    relb = np.zeros((128, 32, 2, 128), np.float32)
    relb[:, :, 0, :] = rt[t5_bucket_np(d_prev)].transpose(0, 2, 1)
    relb[:, :, 1, :] = rt[t5_bucket_np(d_cur)].transpose(0, 2, 1)
    sh["relb"] = relb.reshape(128, 32 * 2 * 128)
    cstb = np.zeros((128, 512), np.float32)
    cstb[:, 0:128] = np.eye(128)
    cstb[:, 128:256] = (s_i <= q_i)
    cstb[:, 256:384] = (s_i > q_i)
    cstb[:, 384:512] = (s_i <= q_i)
    sh["cstb"] = cstb.astype(ml_dtypes.bfloat16)
    c32 = np.zeros((128, 288), np.float32)
    c32[:, 0:128] = (s_i <= q_i)
    c32[:, 128:256] = 1.0
    half = 32
    c32[:, 256:288] = (np.float32(10000.0) ** (-np.arange(half, dtype=np.float32) / np.float32(half)))[None, :]
    sh["cst32"] = c32
    return sh


def core_inputs(inputs, sh, c):
    m = dict(sh)
    m["x"] = np.ascontiguousarray(np.asarray(inputs["x"][c], dtype=np.float32))
    p = np.asarray(inputs["positions"][c], dtype=np.int32)
    m["pos"] = np.ascontiguousarray(p.reshape(NT, 128).T)
    return m


def kernel(**inputs):
    sh = prep_shared(inputs)
    nc = build_nc()
    n = 8
    in_maps = [core_inputs(inputs, sh, c) for c in range(n)]
    res = run_bass_kernel_spmd(nc, in_maps, core_ids=list(range(n)))
    return np.stack([r["out"] for r in res.results], axis=0).astype(np.float32)
```

```python
import math
from contextlib import ExitStack
import numpy as np
import ml_dtypes
import concourse.bass as bass
import concourse.mybir as mybir
from concourse.bass_utils import run_bass_kernel_spmd

F32 = mybir.dt.float32
BF16 = mybir.dt.bfloat16
I32 = mybir.dt.int32
AF = mybir.ActivationFunctionType
ALU = mybir.AluOpType
AX = mybir.AxisListType

T = 2048
NT = 16
D = 4096
KC = 32
DEPTH = 2
EPS = 1e-6
IN_SIZES = (1536, 512, 64, 2048, 2048, 256, 256, 2048, 2048, 2048, 2048, 16, 2048)
IN_DIM = sum(IN_SIZES)
OFFS = [0]
for _n in IN_SIZES:
    OFFS.append(OFFS[-1] + _n)
(O_CQ, O_CKV, O_KR, O_AZ, O_BQ, O_BK, O_BV, O_BZ, O_CQ2, O_CK, O_CV, O_CF, O_CZ) = OFFS[:13]
SEM_LIMIT = 30000
NSLAB_IN = (IN_DIM + 511) // 512

ENGS = ["pe", "act", "dve", "pool", "sp"]
BLK = {"pe": "tensor", "act": "scalar", "dve": "vector", "pool": "gpsimd", "sp": "sync"}


class Res:
    __slots__ = ("name", "w", "r", "excl")

    def __init__(self, name, excl=False):
        self.name = name
        self.w = []
        self.r = {}
        self.excl = excl


class Op:
    __slots__ = ("eng", "fn", "deps", "signal", "val", "dma", "ep", "ph")


class Prog:
    NSLOT = 8

    def __init__(self, nc, es):
        self.nc = nc
        self.es = es
        self.ops = {e: [] for e in ENGS}
        self.sems = {}
        self.last = {}
        self.dmas = {}
        self.phase = "setup"
        self.scopes = False

    def _new(self, eng, fn, dma):
        o = Op()
        o.eng = eng
        o.fn = fn
        o.dma = dma
        o.signal = dma is not None
        o.val = None
        o.ep = 0
        o.deps = []
        o.ph = self.phase
        return o

    def op(self, eng, fn, R=(), W=(), dma=None):
        o = self._new(eng, fn, dma)
        deps = {}
        if dma is not None:
            lst = self.dmas.setdefault(dma, [])
            n = len(lst)
            o.dma = (dma, n % self.NSLOT)
            if n >= self.NSLOT:
                prev = lst[n - self.NSLOT]
                deps[id(prev)] = (prev, True)
            lst.append(o)
        key = o.dma if o.dma else eng
        for r in R:
            for wo in r.w:
                deps[id(wo)] = (wo, True)
            if r.excl:
                for k2, ro in r.r.items():
                    if k2 != key and id(ro) not in deps:
                        deps[id(ro)] = (ro, False)
        appendw = set()
        for w in W:
            if (o.dma is not None and w.w and not w.r
                    and all(x.dma is not None and x.dma[0] == o.dma[0] for x in w.w)):
                appendw.add(id(w))
                continue
            for wo in w.w:
                if id(wo) not in deps:
                    deps[id(wo)] = (wo, False)
            for ro in w.r.values():
                if id(ro) not in deps:
                    deps[id(ro)] = (ro, False)
        for d, raw in deps.values():
            if d is o:
                continue
            if d.dma is None and o.dma is None and d.eng == eng and eng == "pe":
                continue
            d.signal = True
            o.deps.append(d)
        for r in R:
            r.r[key] = o
        for w in W:
            if id(w) in appendw:
                w.w.append(o)
            else:
                w.w = [o]
                w.r = {}
        self.ops[eng].append(o)
        self.last[key] = o
        return o

    def barrier(self, engs=ENGS):
        lasts = list(self.last.values())
        for d in lasts:
            d.signal = True
        for e in engs:
            o = self._new(e, None, None)
            o.deps = list(lasts)
            self.ops[e].append(o)

    def final_wait(self, eng="sp"):
        self.barrier(engs=[eng])

    def emit(self):
        nc = self.nc
        cnt = {}
        for e in ENGS:
            for o in self.ops[e]:
                if not o.signal:
                    continue
                key = o.dma if o.dma else e
                inc = 16 if o.dma else 1
                ep, v = cnt.get(key, (0, 0))
                if v + inc > SEM_LIMIT:
                    ep, v = ep + 1, 0
                v += inc
                cnt[key] = (ep, v)
                o.ep, o.val = ep, v
        for key, (ep, v) in cnt.items():
            nm = key if isinstance(key, str) else f"{key[0]}{key[1]}"
            for k in range(ep + 1):
                self.sems[(key, k)] = self.es.enter_context(nc.semaphore(f"s_{nm}_{k}"))
        block = self.es.enter_context(nc.Block())
        for e in ENGS:
            ops = self.ops[e]

            def body(eng, ops=ops, e=e):
                seen = {}
                cur = None
                for o in ops:
                    need = {}
                    for d in o.deps:
                        k = (d.dma if d.dma else d.eng, d.ep)
                        if d.val > need.get(k, 0):
                            need[k] = d.val
                    for k, v in need.items():
                        if seen.get(k, 0) < v:
                            eng.wait_ge(self.sems[k], v)
                            seen[k] = v
                    if self.scopes and o.fn is not None and o.ph != cur:
                        if cur is not None:
                            nc.pop_named_scope(cur)
                        cur = o.ph
                        nc.push_named_scope(cur)
                    if o.fn is None:
                        continue
                    ins = o.fn(eng)
                    if o.signal:
                        key = o.dma if o.dma else e
                        ins.then_inc(self.sems[(key, o.ep)], 16 if o.dma else 1)
                if cur is not None:
                    nc.pop_named_scope(cur)

            getattr(block, BLK[e])(body)


ARENA_BYTES = 207 * 1024


class Builder:
    def __init__(self, nc, es, io, depth=DEPTH, stop=None, dumps=()):
        self.nc = nc
        self.es = es
        self.io = io
        self.depth = depth
        self.stop = stop
        self.dumps = set(dumps)
        self.P = Prog(nc, es)
        self.arena = es.enter_context(nc.sbuf_tensor("arena", [128, ARENA_BYTES // 2], BF16))
        self.base = 0
        self.off = 0
        self.ps = [es.enter_context(nc.psum_tensor(f"psb{i}", [128, 512], F32)) for i in range(8)]
        self.psr = [Res(f"psb{i}", excl=True) for i in range(8)]
        self.bankc = {}
        self.flip = 0
        self.scr = {}

    def alloc(self, shape, dt, name="t"):
        n = int(np.prod(shape[1:]))
        nb = n * (2 if dt == BF16 else 4)
        o = self.off
        assert o % 4 == 0
        self.off = o + (nb + 63) // 64 * 64
        assert self.off <= ARENA_BYTES, (name, self.off)
        a = self.arena[:, o // 2:(o + nb) // 2]
        if dt != BF16:
            a = a.bitcast(dt)
        if len(shape) == 3:
            a = a.rearrange("p (a b) -> p a b", a=shape[1], b=shape[2])
        elif len(shape) == 4:
            a = a.rearrange("p (a b c) -> p a b c", a=shape[1], b=shape[2], c=shape[3])
        if shape[0] != 128:
            a = a[0:shape[0]]
        return a, Res(name)

    def keep(self):
        self.base = self.off

    def new_phase(self, base=None, name=None):
        self.P.barrier()
        if name is not None:
            self.P.phase = name
        if base is not None:
            self.base = base
        self.off = self.base

    def dram(self, name, shape, dt):
        kind = "ExternalOutput" if name in self.dumps else "Internal"
        t = self.nc.dram_tensor(name, list(shape), dt, kind=kind).ap()
        self.scr[name] = t
        return t

    def nb(self, lo=0, hi=8):
        b = self.bankc.get((lo, hi), lo)
        self.bankc[(lo, hi)] = lo + (b + 1 - lo) % (hi - lo)
        return b

    def alt(self):
        self.flip ^= 1
        return "act" if self.flip else "dve"

    def copy(self, eng, out, in_, R, W):
        if eng == "act":
            self.P.op("act", lambda e: e.copy(out=out, in_=in_), R=R, W=W)
        else:
            self.P.op(eng, lambda e: e.tensor_copy(out=out, in_=in_), R=R, W=W)

    def load(self, out, in_, W, q="sp", st="ld"):
        self.P.op(q, lambda e: e.dma_start(out=out, in_=in_), W=W, dma=st)

    def store(self, out, in_, R, q="sp", st="st"):
        self.P.op(q, lambda e: e.dma_start(out=out, in_=in_), R=R, dma=st)

    def transposes(self, src, src_res, cols, dst_fn, dst_res, eng, rows=128, banks=(0, 8)):
        P = self.P
        if isinstance(cols, int):
            cols = [c * rows for c in range(cols)]
        nblk = len(cols)
        g0 = 0
        while g0 < nblk:
            n = min(4, nblk - g0)
            b = self.nb(*banks)
            ps, psr = self.ps[b], self.psr[b]
            for k in range(n):
                c0 = cols[g0 + k]
                P.op("pe", lambda e, ps=ps, k=k, c0=c0: e.matmul(
                    ps[0:rows, k * 128:(k + 1) * 128], lhsT=src[:, c0:c0 + rows], rhs=self.ident,
                    start=True, stop=True), R=[src_res, self.cres], W=[psr])
            dst = dst_fn(g0, n)
            pv = ps[0:rows, 0:n * 128].rearrange("p (a b) -> p a b", a=n, b=128)
            self.copy(eng, dst, pv, [psr], [dst_res])
            g0 += n

    def rstd_from_ss(self, ss, ssr, n, col):
        P = self.P
        P.op("dve", lambda e: e.tensor_scalar(out=ss[:, col + 1:col + 2], in0=ss[:, col:col + 1], scalar1=1.0 / n,
                                              scalar2=EPS, op0=ALU.mult, op1=ALU.add), R=[ssr], W=[ssr])
        P.op("act", lambda e: e.activation(out=ss[:, col + 1:col + 2], in_=ss[:, col + 1:col + 2], func=AF.Sqrt),
             R=[ssr], W=[ssr])
        P.op("dve", lambda e: e.reciprocal(out=ss[:, col + 2:col + 3], in_=ss[:, col + 1:col + 2]), R=[ssr], W=[ssr])

    def gemm(self, AT, ATres, kc, wsrc, ncols, slab_w, evac, tiles=range(NT), banks=(0, 8), wbufs=None, slab_src=None):
        P = self.P
        if wbufs is None:
            wbufs = [self.alloc([128, kc, slab_w], BF16, f"wb{i}") for i in range(2)]
        wv = wsrc.rearrange("(c p) n -> p c n", p=128) if slab_src is None else None
        nslab = (ncols + slab_w - 1) // slab_w

        def issue(s):
            c0 = s * slab_w
            w = min(slab_w, ncols - c0)
            wb, wr = wbufs[s % 2]
            step = 8 if kc >= 8 else kc
            for k0 in range(0, kc, step):
                k1 = min(kc, k0 + step)
                src = wv[:, k0:k1, c0:c0 + w] if slab_src is None else slab_src(s)[:, k0:k1, 0:w]
                P.op("pool", lambda e, wb=wb, k0=k0, k1=k1, w=w, src=src: e.dma_start(
                    out=wb[:, k0:k1, 0:w], in_=src), W=[wr], dma="wld")

        issue(0)
        for s in range(nslab):
            if s + 1 < nslab:
                issue(s + 1)
            c0 = s * slab_w
            w = min(slab_w, ncols - c0)
            wb, wr = wbufs[s % 2]
            for i in tiles:
                b = self.nb(*banks)
                ps, psr = self.ps[b], self.psr[b]
                for c in range(kc):
                    P.op("pe", lambda e, ps=ps, wb=wb, c=c, i=i, w=w: e.matmul(
                        ps[:, 0:w], lhsT=AT[:, c, i * 128:(i + 1) * 128], rhs=wb[:, c, 0:w],
                        start=(c == 0), stop=(c == kc - 1)), R=[wr, ATres[i]], W=[psr])
                evac(i, c0, w, ps, psr)

    def setup(self):
        P, io = self.P, self.io
        self.cres = Res("consts")
        cb, _ = self.alloc([128, 512], BF16, "cstb")
        c32, _ = self.alloc([128, 288], F32, "cst32")
        self.load(cb, io["cstb"], [self.cres])
        self.load(c32, io["cst32"], [self.cres])
        self.ident = cb[:, 0:128]
        self.maskT = cb[:, 128:256]
        self.bmask = cb[:, 256:512]
        self.tri = c32[:, 0:128]
        self.ones32 = c32[:, 128:256]
        invf = c32[:, 256:288]
        self.cos, _ = self.alloc([128, NT, 32], F32, "cos")
        self.sin, _ = self.alloc([128, NT, 32], F32, "sin")
        self.cf, self.cfr = self.alloc([128, NT, 16], F32, "cf")
        self.keep()
        posi, pr = self.alloc([128, NT], I32, "posi")
        posf, pfr = self.alloc([128, NT], F32, "posf")
        u, ur = self.alloc([128, NT, 32], F32, "u")
        ui, uir = self.alloc([128, NT, 32], I32, "ui")
        uk, ukr = self.alloc([128, NT, 32], F32, "uk")
        self.load(posi, io["pos"], [pr])
        P.op("dve", lambda e: e.tensor_copy(out=posf, in_=posi), R=[pr], W=[pfr])
        for i in range(NT):
            P.op("dve", lambda e, i=i: e.tensor_scalar(out=u[:, i, :], in0=invf, scalar1=posf[:, i:i + 1],
                                                      scalar2=1.0 / (2 * math.pi), op0=ALU.mult, op1=ALU.mult),
                 R=[pfr, self.cres], W=[ur])
        uf = u.rearrange("p a b -> p (a b)")
        uif = ui.rearrange("p a b -> p (a b)")
        ukf = uk.rearrange("p a b -> p (a b)")
        for dst, shift in ((self.sin, 0.0), (self.cos, 0.25)):
            df = dst.rearrange("p a b -> p (a b)")
            w, wr = self.alloc([128, NT * 32], F32, "w")
            P.op("dve", lambda e, w=w, shift=shift: e.tensor_scalar(out=w, in0=uf, scalar1=shift, scalar2=None,
                                                                    op0=ALU.add), R=[ur], W=[wr])
            P.op("dve", lambda e, w=w: e.tensor_copy(out=uif, in_=w), R=[wr], W=[uir])
            P.op("dve", lambda e: e.tensor_copy(out=ukf, in_=uif), R=[uir], W=[ukr])
            P.op("dve", lambda e, w=w: e.tensor_tensor(out=w, in0=w, in1=ukf, op=ALU.subtract), R=[wr, ukr], W=[wr])
            P.op("dve", lambda e, w=w: e.tensor_scalar(out=ukf, in0=w, scalar1=0.5, scalar2=None, op0=ALU.is_gt),
                 R=[wr], W=[ukr])
            P.op("dve", lambda e, w=w: e.tensor_tensor(out=w, in0=w, in1=ukf, op=ALU.subtract), R=[wr, ukr], W=[wr])
            P.op("dve", lambda e, w=w: e.tensor_scalar(out=ukf, in0=w, scalar1=-0.5, scalar2=None, op0=ALU.is_lt),
                 R=[wr], W=[ukr])
            P.op("dve", lambda e, w=w: e.tensor_tensor(out=w, in0=w, in1=ukf, op=ALU.add), R=[wr, ukr], W=[wr])
            P.op("act", lambda e, w=w, df=df: e.activation(out=df, in_=w, func=AF.Sin, scale=2 * math.pi),
                 R=[wr], W=[self.cres])
        self.PROJ = self.dram("PROJ", [T, 17408], BF16)
        self.G = self.dram("G", [T, 3 * D], BF16)
        self.QA = self.dram("QA", [T, 3072], BF16)
        self.KN = self.dram("KN", [T, 2048], BF16)
        self.VA = self.dram("VA", [T, 2048], BF16)
        self.OT = self.dram("OT", [3 * 2048, T], BF16)
        self.Y = self.dram("Y", [T, D], BF16)
        self.Z = self.dram("Z", [T, D], F32)
        self.X1 = self.dram("X1", [T, D], F32)

    def prenorm(self, l, xsrc):
        P, io = self.P, self.io
        self.new_phase(name=f"L{l}_prenorm")
        self.mark = self.base
        hT, _ = self.alloc([128, KC, T], BF16, "hT")
        self.keep()
        hTr = [Res(f"hT{i}") for i in range(NT)]
        gbc, gr = self.alloc([128, D], F32, "gbc")
        self.load(gbc, io["pre_bc"][l * 128:(l + 1) * 128, :], [gr])
        xs = [self.alloc([128, D], F32, f"xs{i}") for i in range(2)]
        hb, hbr = self.alloc([128, D], BF16, "hb")
        ss, ssr = self.alloc([128, 4], F32, "ss")
        for i in range(NT):
            x_, xr = xs[i % 2]
            self.load(x_, xsrc[i * 128:(i + 1) * 128, :], [xr])
            P.op("act", lambda e, x_=x_: e.activation(out=hb, in_=x_, func=AF.Square, accum_out=ss[:, 0:1]),
                 R=[xr], W=[hbr, ssr])
            self.rstd_from_ss(ss, ssr, D, 0)
            P.op("dve", lambda e, x_=x_: e.scalar_tensor_tensor(out=hb, in0=x_, scalar=ss[:, 2:3], in1=gbc,
                                                                op0=ALU.mult, op1=ALU.mult),
                 R=[xr, ssr, gr], W=[hbr])
            self.transposes(hb, hbr, KC, lambda g0, n, i=i: hT[:, g0:g0 + n, i * 128:(i + 1) * 128], hTr[i],
                            "act" if i % 2 else "dve")
        return hT, hTr

    def proj_in(self, l, hT, hTr):
        P, io = self.P, self.io
        self.new_phase(name=f"L{l}_proj_in")
        stg = [self.alloc([128, 512], BF16, f"stg{i}") for i in range(4)]
        cnt = [0]

        def evac(i, c0, w, ps, psr):
            s_, sr = stg[cnt[0] % 4]
            cnt[0] += 1
            self.copy("act", s_[:, 0:w], ps[:, 0:w], [psr], [sr])
            self.store(self.PROJ[i * 128:(i + 1) * 128, c0:c0 + w], s_[:, 0:w], [sr])
            if c0 <= O_CF < c0 + w:
                o = O_CF - c0
                P.op("act", lambda e: e.copy(out=self.cf[:, i, :], in_=ps[:, o:o + 16]), R=[psr], W=[self.cfr])

        def slab_src(s):
            r0 = (l * NSLAB_IN + s) * 128
            return io["w_in"][r0:r0 + 128, :].rearrange("p (c n) -> p c n", c=KC, n=512)

        wbufs = [self.alloc([128, KC, 512], BF16, f"wb{i}") for i in range(2)]
        self.gemm(hT, hTr, KC, None, IN_DIM, 512, evac, slab_src=slab_src, wbufs=wbufs)
        return stg, wbufs

    def gates(self, l, hT, hTr, stg, wbufs):
        P, io = self.P, self.io
        P.phase = f"L{l}_gates"
        cnt = [0]

        def evac(i, c0, w, ps, psr):
            s_, sr = stg[cnt[0] % 4]
            cnt[0] += 1
            P.op("act", lambda e: e.activation(out=s_[:, 0:w], in_=ps[:, 0:w], func=AF.Sigmoid), R=[psr], W=[sr])
            self.store(self.G[i * 128:(i + 1) * 128, c0:c0 + w], s_[:, 0:w], [sr])

        self.gemm(hT, hTr, KC, io["w_merge"][l * D:(l + 1) * D, :], 3 * D, 512, evac, wbufs=wbufs)

    def evac_store(self, dst, dt=BF16, nbuf=4):
        stg = [self.alloc([128, 512], dt, f"es{i}") for i in range(nbuf)]
        cnt = [0]

        def evac(i, c0, w, ps, psr):
            s_, sr = stg[cnt[0] % nbuf]
            cnt[0] += 1
            self.copy("act", s_[:, 0:w], ps[:, 0:w], [psr], [sr])
            self.store(dst[i * 128:(i + 1) * 128, c0:c0 + w], s_[:, 0:w], [sr])

        return evac

    def mla(self, l):
        P, io = self.P, self.io
        self.new_phase(name=f"L{l}_mla_prep")
        mark = self.base
        self.KRT, self.KRTr = self.alloc([128, 1, T], BF16, "KRT")
        krt_base = self.off
        cqT, _ = self.alloc([128, 12, T], BF16, "cqT")
        ckvT, _ = self.alloc([128, 4, T], BF16, "ckvT")
        cqTr = [Res(f"cqT{i}") for i in range(NT)]
        ckvTr = [Res(f"ckvT{i}") for i in range(NT)]
        self.keep()
        gq, gqr = self.alloc([128, 1536], F32, "gq")
        gkv, gkvr = self.alloc([128, 512], F32, "gkv")
        self.load(gq, io["qn_bc"][l * 128:(l + 1) * 128, :], [gqr])
        self.load(gkv, io["kvn_bc"][l * 128:(l + 1) * 128, :], [gkvr])
        tin = [self.alloc([128, 2112], BF16, f"tin{i}") for i in range(2)]
        junk, jr = self.alloc([128, 1536], BF16, "junk")
        cqn, cqnr = self.alloc([128, 1536], BF16, "cqn")
        ckvn, ckvnr = self.alloc([128, 512], BF16, "ckvn")
        ss, ssr = self.alloc([128, 8], F32, "ss")
        tt, ttr = self.alloc([128, 4, 32], F32, "tt")
        krr, krrr = self.alloc([128, 64], BF16, "krr")
        for i in range(NT):
            t_, tr = tin[i % 2]
            self.load(t_, self.PROJ[i * 128:(i + 1) * 128, 0:2112], [tr])
            P.op("act", lambda e, t_=t_: e.activation(out=junk, in_=t_[:, 0:1536], func=AF.Square,
                                                      accum_out=ss[:, 0:1]), R=[tr], W=[jr, ssr])
            P.op("act", lambda e, t_=t_: e.activation(out=junk[:, 0:512], in_=t_[:, 1536:2048], func=AF.Square,
                                                      accum_out=ss[:, 3:4]), R=[tr], W=[jr, ssr])
            self.rstd_from_ss(ss, ssr, 1536, 0)
            self.rstd_from_ss(ss, ssr, 512, 3)
            P.op("dve", lambda e, t_=t_: e.scalar_tensor_tensor(out=cqn, in0=t_[:, 0:1536], scalar=ss[:, 2:3], in1=gq,
                                                                op0=ALU.mult, op1=ALU.mult), R=[tr, ssr, gqr], W=[cqnr])
            P.op("dve", lambda e, t_=t_: e.scalar_tensor_tensor(out=ckvn, in0=t_[:, 1536:2048], scalar=ss[:, 5:6],
                                                                in1=gkv, op0=ALU.mult, op1=ALU.mult),
                 R=[tr, ssr, gkvr], W=[ckvnr])
            self.transposes(cqn, cqnr, 12, lambda g0, n, i=i: cqT[:, g0:g0 + n, i * 128:(i + 1) * 128], cqTr[i], "act")
            self.transposes(ckvn, ckvnr, 4, lambda g0, n, i=i: ckvT[:, g0:g0 + n, i * 128:(i + 1) * 128], ckvTr[i], "act")
            x1, x2 = t_[:, 2048:2080], t_[:, 2080:2112]
            cs, sn = self.cos[:, i, :], self.sin[:, i, :]
            for k, (a, b_) in enumerate(((x1, cs), (x2, sn), (x2, cs), (x1, sn))):
                P.op("pool", lambda e, k=k, a=a, b_=b_: e.tensor_tensor(out=tt[:, k, :], in0=a, in1=b_, op=ALU.mult),
                     R=[tr, self.cres], W=[ttr])
            P.op("pool", lambda e: e.tensor_tensor(out=krr[:, 0:32], in0=tt[:, 0, :], in1=tt[:, 1, :], op=ALU.subtract),
                 R=[ttr], W=[krrr])
            P.op("pool", lambda e: e.tensor_tensor(out=krr[:, 32:64], in0=tt[:, 2, :], in1=tt[:, 3, :], op=ALU.add),
                 R=[ttr], W=[krrr])
            self.transposes(krr, krrr, [0], lambda g0, n, i=i: self.KRT[0:64, 0:1, i * 128:(i + 1) * 128], self.KRTr,
                            "dve", rows=64)
        self.new_phase(name=f"L{l}_uq")
        qb = [self.alloc([128, 384], BF16, f"qb{i}") for i in range(3)]
        rt = [self.alloc([128, 4, 2, 32], F32, f"rt{i}") for i in range(2)]
        cnt = [0]

        def evac_q(i, c0, w, ps, psr):
            q_, qr = qb[cnt[0] % 3]
            t4, t4r = rt[cnt[0] % 2]
            cnt[0] += 1
            P.op("act", lambda e: e.copy(out=q_, in_=ps[:, 0:384]), R=[psr], W=[qr])
            pv = ps[:, 0:384].rearrange("p (h d) -> p h d", h=2, d=192)
            qv = q_.rearrange("p (h d) -> p h d", h=2, d=192)
            cs = self.cos[:, i:i + 1, :].broadcast_to([128, 2, 32])
            sn = self.sin[:, i:i + 1, :].broadcast_to([128, 2, 32])
            x1, x2 = pv[:, :, 128:160], pv[:, :, 160:192]
            for k, (a, b_) in enumerate(((x1, cs), (x2, sn), (x2, cs), (x1, sn))):
                P.op("dve", lambda e, k=k, a=a, b_=b_: e.tensor_tensor(out=t4[:, k, :, :], in0=a, in1=b_, op=ALU.mult),
                     R=[psr, self.cres], W=[t4r])
            P.op("dve", lambda e: e.tensor_tensor(out=qv[:, :, 128:160], in0=t4[:, 0, :, :], in1=t4[:, 1, :, :],
                                                  op=ALU.subtract), R=[t4r], W=[qr])
            P.op("dve", lambda e: e.tensor_tensor(out=qv[:, :, 160:192], in0=t4[:, 2, :, :], in1=t4[:, 3, :, :],
                                                  op=ALU.add), R=[t4r], W=[qr])
            self.store(self.QA[i * 128:(i + 1) * 128, c0:c0 + 384], q_, [qr])

        self.gemm(cqT, cqTr, 12, io["w_uq"][l * 1536:(l + 1) * 1536, :], 3072, 384, evac_q)
        self.new_phase(name=f"L{l}_ukv")
        wb = [self.alloc([128, 4, 512], BF16, f"wkv{i}") for i in range(2)]
        ev = self.evac_store(self.KN)
        self.gemm(ckvT, ckvTr, 4, io["w_uk"][l * 512:(l + 1) * 512, :], 2048, 512, ev, wbufs=wb)
        ev2 = self.evac_store(self.VA)
        self.gemm(ckvT, ckvTr, 4, io["w_uv"][l * 512:(l + 1) * 512, :], 2048, 512, ev2, wbufs=wb)
        self.new_phase(base=krt_base)
        return mark

    def attn_full(self, l, kind):
        P, io = self.P, self.io
        self.new_phase(name=f"L{l}_attn{kind}")
        isA = kind == "A"
        KRT = self.KRT if isA else None
        br = 0 if isA else 2
        scale = (192.0 if isA else 128.0) ** -0.5
        if not isA:
            bfb, bfr = self.alloc([128, 1, 16], F32, "bfb")
            self.load(bfb[:, 0, :], io["bf_bc"][l * 128:(l + 1) * 128, :], [bfr])
            lf, lfr = self.alloc([128, NT, 16], F32, "lf")
            ncum, ncr = self.alloc([128, NT, 16], F32, "ncum")
            nrun, nrr = self.alloc([128, NT + 1, 16], F32, "nrun")
            biasC, bcr = self.alloc([128, 136, 16], F32, "biasC")
            P.op("dve", lambda e: e.tensor_tensor(out=lf, in0=self.cf, in1=bfb.broadcast_to([128, NT, 16]), op=ALU.add),
                 R=[self.cfr, bfr], W=[lfr])
            P.op("act", lambda e: e.activation(out=lf, in_=lf, func=AF.Exp, scale=-1.0), R=[lfr], W=[lfr])
            P.op("act", lambda e: e.activation(out=lf, in_=lf, func=AF.Ln, bias=1.0, scale=1.0), R=[lfr], W=[lfr])
            P.op("pool", lambda e: e.memset(nrun[:, 0, :], 0.0), W=[nrr])
            for i in range(NT):
                b = self.nb(0, 4)
                ps, psr = self.ps[b], self.psr[b]
                P.op("pe", lambda e, ps=ps, i=i: e.matmul(ps[:, 0:16], lhsT=self.tri, rhs=lf[:, i, :], start=True,
                                                          stop=True), R=[lfr, self.cres], W=[psr])
                P.op("pe", lambda e, ps=ps, i=i: e.matmul(ps[:, 16:32], lhsT=self.ones32, rhs=lf[:, i, :], start=True,
                                                          stop=True), R=[lfr, self.cres], W=[psr])
                P.op("dve", lambda e, ps=ps, i=i: e.tensor_tensor(out=ncum[:, i, :], in0=ps[:, 0:16], in1=nrun[:, i, :],
                                                                  op=ALU.add), R=[psr, nrr], W=[ncr])
                P.op("dve", lambda e, ps=ps, i=i: e.tensor_tensor(out=nrun[:, i + 1, :], in0=ps[:, 16:32],
                                                                  in1=nrun[:, i, :], op=ALU.add), R=[psr, nrr], W=[nrr])
            nmid, nmr = self.alloc([128, NT, 16], F32, "nmid")
            P.op("dve", lambda e: e.tensor_tensor(out=nmid, in0=nrun[:, 0:NT, :], in1=nrun[:, 1:NT + 1, :], op=ALU.add),
                 R=[nrr], W=[nmr])
            P.op("dve", lambda e: e.tensor_scalar(out=nmid, in0=nmid, scalar1=0.5, scalar2=None, op0=ALU.mult),
                 R=[nmr], W=[nmr])
            for i in range(NT):
                b0 = i * (i + 1) // 2
                P.op("dve", lambda e, i=i, b0=b0: e.tensor_tensor(
                    out=biasC[:, b0:b0 + i + 1, :], in0=ncum[:, 0:i + 1, :],
                    in1=nmid[:, i:i + 1, :].broadcast_to([128, i + 1, 16]), op=ALU.subtract),
                     R=[ncr, nmr], W=[bcr])
        sets = []
        for k in range(2):
            QT_, _ = self.alloc([128, 4, T], BF16, f"QT{k}")
            KT_, _ = self.alloc([128, 4, T], BF16, f"KT{k}")
            QRT_ = self.alloc([128, 4, T], BF16, f"QRT{k}")[0] if isA else None
            Va_, _ = self.alloc([128, NT, 4, 130], BF16, f"Vaug{k}")
            rs = [[Res(f"{nm}{k}_{i}") for i in range(NT)] for nm in ("QT", "KT", "QRT", "Va")]
            P.op("pool", lambda e, Va_=Va_: e.memset(Va_.rearrange("p a b c -> p (a b c)"), 1.0), W=rs[3])
            sets.append((QT_, rs[0], KT_, rs[1], QRT_, rs[2], Va_, rs[3]))
        SZ, SZr = self.alloc([128, NT, 512], BF16, "SZ")
        OG, _ = self.alloc([128, NT, 512], BF16, "OG")
        OGr = [Res(f"OG{i}") for i in range(NT)]
        OTs, OTsr = self.alloc([128, 4, T], BF16, "OTs")
        qw = 768 if isA else 512
        qtl = [self.alloc([128, qw], BF16, f"qtl{i}") for i in range(2)]
        ktl = [self.alloc([128, 512], BF16, f"ktl{i}") for i in range(2)]
        vtl = [self.alloc([128, 512], BF16, f"vtl{i}") for i in range(2)]
        PT = [(self.alloc([128, 512], BF16, f"PT{i}")[0], [Res(f"PT{i}_{k}") for k in range(4)]) for i in range(5)]
        rv = [self.alloc([128, 1], F32, f"rv{i}") for i in range(4)]
        ptc = [0]
        rvc = [0]
        sset = [0]
        ldc = [0]

        def load_ops(hg, S):
            QT, QTr, KT, KTr, QRT, QRTr, Va, Var = S
            if isA:
                qsrc, ksrc, vsrc = self.QA, self.KN, self.VA
                qo, ko, vo = hg * 768, hg * 512, hg * 512
            else:
                qsrc = ksrc = vsrc = self.PROJ
                qo, ko, vo = O_CQ2 + hg * 512, O_CK + hg * 512, O_CV + hg * 512

            def one(i):
                q_, qr = qtl[ldc[0] % 2]
                k_, kr = ktl[ldc[0] % 2]
                v_, vr = vtl[ldc[0] % 2]
                ldc[0] += 1
                rows = slice(i * 128, (i + 1) * 128)
                self.load(q_, qsrc[rows, qo:qo + qw], [qr])
                self.load(k_, ksrc[rows, ko:ko + 512], [kr])
                self.load(v_, vsrc[rows, vo:vo + 512], [vr])
                tcols = slice(i * 128, (i + 1) * 128)
                if isA:
                    self.transposes(q_, qr, [h * 192 for h in range(4)], lambda g0, n: QT[:, g0:g0 + n, tcols], QTr[i],
                                    "dve", banks=(0, 4))
                    self.transposes(q_, qr, [h * 192 + 128 for h in range(4)], lambda g0, n: QRT[0:64, g0:g0 + n, tcols],
                                    QRTr[i], "dve", rows=64, banks=(0, 4))
                else:
                    self.transposes(q_, qr, 4, lambda g0, n: QT[:, g0:g0 + n, tcols], QTr[i], "dve", banks=(0, 4))
                self.transposes(k_, kr, 4, lambda g0, n: KT[:, g0:g0 + n, tcols], KTr[i], "dve", banks=(0, 4))
                P.op("pool", lambda e, v_=v_, i=i, Va=Va: e.tensor_copy(
                    out=Va[:, i, :, 0:128], in_=v_.rearrange("p (h d) -> p h d", h=4, d=128)), R=[vr], W=[Var[i]])

            return [lambda i=i: one(i) for i in range(NT)]

        for f in load_ops(0, sets[0]):
            f()
        for hg in range(4):
            QT, QTr, KT, KTr, QRT, QRTr, Va, Var = sets[hg % 2]
            nxt = load_ops(hg + 1, sets[(hg + 1) % 2]) if hg < 3 else []
            zoff = O_AZ if isA else O_CZ
            self.load(SZ, self.PROJ[:, zoff + hg * 512: zoff + (hg + 1) * 512].rearrange("(n p) d -> p n d", p=128), [SZr])
            P.op("act", lambda e: e.activation(out=SZ, in_=SZ, func=AF.Silu), R=[SZr], W=[SZr])
            for hh in range(4):
                h = hg * 4 + hh
                for Qb in range(4):
                    st = sset[0]
                    sset[0] ^= 1

                    def oacc(t):
                        b = 4 + st * 2 + t // 2
                        return self.ps[b][:, (t % 2) * 256:(t % 2) * 256 + 129], self.psr[b], self.ps[b], (t % 2) * 256

                    def emit_scores(j):
                        r = max(0, j - 4 * Qb)
                        n0 = r * 128
                        ncol = 512 - n0
                        b = self.nb(0, 4)
                        ps, psr = self.ps[b], self.psr[b]
                        q0 = Qb * 512 + n0
                        qres = QTr[4 * Qb + r:4 * Qb + 4]
                        P.op("pe", lambda e, ps=ps, j=j, q0=q0, ncol=ncol, hh=hh, KT=KT, QT=QT: e.matmul(
                            ps[:, 0:ncol], lhsT=KT[:, hh, j * 128:(j + 1) * 128], rhs=QT[:, hh, q0:q0 + ncol],
                            start=True, stop=not isA), R=[KTr[j]] + qres, W=[psr])
                        if isA:
                            P.op("pe", lambda e, ps=ps, j=j, q0=q0, ncol=ncol, hh=hh, QRT=QRT: e.matmul(
                                ps[:, 0:ncol], lhsT=KRT[0:64, 0, j * 128:(j + 1) * 128],
                                rhs=QRT[0:64, hh, q0:q0 + ncol], start=False, stop=True),
                                 R=[self.KRTr] + QRTr[4 * Qb + r:4 * Qb + 4], W=[psr])
                        return ps, psr, r, ncol

                    def emit_rest(j, sc):
                        ps, psr, r, ncol = sc
                        p_, prs = PT[ptc[0] % 5]
                        ptc[0] += 1
                        if isA:
                            P.op("act", lambda e, ps=ps, p_=p_, ncol=ncol: e.activation(
                                out=p_[:, 0:ncol], in_=ps[:, 0:ncol], func=AF.Exp, scale=scale), R=[psr], W=prs[0:4 - r])
                        else:
                            for t in range(r, 4):
                                i = 4 * Qb + t
                                bi = i * (i + 1) // 2 + j
                                cc = (t - r) * 128
                                P.op("act", lambda e, ps=ps, p_=p_, cc=cc, bi=bi, h=h: e.activation(
                                    out=p_[:, cc:cc + 128], in_=ps[:, cc:cc + 128], func=AF.Exp, scale=scale,
                                    bias=biasC[:, bi, h:h + 1]), R=[psr, bcr], W=[prs[t - r]])
                        if j >= 4 * Qb:
                            P.op("pool", lambda e, p_=p_: e.tensor_tensor(out=p_[:, 0:128], in0=p_[:, 0:128],
                                                                          in1=self.maskT, op=ALU.mult),
                                 R=[prs[0], self.cres], W=[prs[0]])
                        for t in range(r, 4):
                            i = 4 * Qb + t
                            oa, oar, _, _ = oacc(t)
                            cc = (t - r) * 128
                            P.op("pe", lambda e, oa=oa, p_=p_, cc=cc, j=j, hh=hh, i=i, t=t, Va=Va: e.matmul(
                                oa, lhsT=p_[:, cc:cc + 128], rhs=Va[:, j, hh, 0:129], start=(j == 0 and t % 2 == 0),
                                stop=(j == i), skip_group_check=(t % 2 == 1)), R=[prs[t - r], Var[j]], W=[oar])

                    nj = 4 * Qb + 4
                    pend = [emit_scores(0), emit_scores(1)]
                    for j in range(nj):
                        if j + 2 < nj:
                            pend.append(emit_scores(j + 2))
                        emit_rest(j, pend.pop(0))
                    for t in range(4):
                        i = 4 * Qb + t
                        oa, oar, bank, c0 = oacc(t)
                        r_, rr = rv[rvc[0] % 4]
                        rvc[0] += 1
                        P.op("dve", lambda e, r_=r_, bank=bank, c0=c0: e.reciprocal(out=r_, in_=bank[:, c0 + 128:c0 + 129]),
                             R=[oar], W=[rr])
                        P.op("dve", lambda e, r_=r_, bank=bank, c0=c0, i=i, hh=hh: e.scalar_tensor_tensor(
                            out=OG[:, i, hh * 128:(hh + 1) * 128], in0=bank[:, c0:c0 + 128], scalar=r_[:, 0:1],
                            in1=SZ[:, i, hh * 128:(hh + 1) * 128], op0=ALU.mult, op1=ALU.mult),
                             R=[oar, rr, SZr], W=[OGr[i]])
                    if nxt:
                        nxt.pop(0)()
            for i in range(NT):
                tcols = slice(i * 128, (i + 1) * 128)
                self.transposes(OG[:, i, :], OGr[i], 4, lambda g0, n: OTs[:, g0:g0 + n, tcols], OTsr, "dve", banks=(0, 4))
            r0 = br * 2048 + hg * 512
            self.store(self.OT[r0:r0 + 512, :].rearrange("(h p) t -> p h t", p=128), OTs, [OTsr])

    def attn_swa(self, l):
        P, io = self.P, self.io
        self.new_phase(name=f"L{l}_swa")
        E, Er = self.alloc([128, 32, 2, 128], BF16, "E")
        esink, esr = self.alloc([128, 32], F32, "esink")
        self.load(esink, io["sink_bc"][l * 128:(l + 1) * 128, :], [esr])
        P.op("act", lambda e: e.activation(out=esink, in_=esink, func=AF.Exp), R=[esr], W=[esr])
        rb = [self.alloc([128, 8, 256], F32, f"rb{i}") for i in range(2)]
        bm = self.bmask.rearrange("p (o c) -> p o c", o=1, c=256).broadcast_to([128, 8, 256])
        for g in range(4):
            r_, rr = rb[g % 2]
            self.load(r_.rearrange("p a b -> p (a b)"), io["relb"][:, g * 2048:(g + 1) * 2048], [rr])
            P.op("act", lambda e, r_=r_: e.activation(out=r_, in_=r_, func=AF.Exp), R=[rr], W=[rr])
            P.op("dve", lambda e, r_=r_, g=g: e.tensor_tensor(
                out=E[:, g * 8:(g + 1) * 8, :, :].rearrange("p h a b -> p h (a b)"), in0=r_, in1=bm, op=ALU.mult),
                 R=[rr, self.cres], W=[Er])
        QT, _ = self.alloc([128, 4, T], BF16, "QTb")
        QTr = [Res(f"QTb{i}") for i in range(NT)]
        KT, _ = self.alloc([128, 1, T], BF16, "KTb")
        KTr = [Res(f"KTb{i}") for i in range(NT)]
        Vb, Vbr = self.alloc([128, NT, 66], BF16, "Vb")
        SZ, SZr = self.alloc([128, NT, 512], BF16, "SZb")
        OG, _ = self.alloc([128, NT, 512], BF16, "OGb")
        OGr = [Res(f"OGb{i}") for i in range(NT)]
        OTs, OTsr = self.alloc([128, 4, T], BF16, "OTsb")
        qtl = [self.alloc([128, 512], BF16, f"bq{i}") for i in range(2)]
        ktl = [self.alloc([128, 128], BF16, f"bk{i}") for i in range(2)]
        PTc = [self.alloc([128, 8, 128], BF16, f"PTc{i}") for i in range(2)]
        PTp = [self.alloc([128, 8, 128], BF16, f"PTp{i}") for i in range(2)]
        den = [self.alloc([128, 4, 1], F32, f"den{i}") for i in range(4)]
        tmp = [self.alloc([128, 4, 64], F32, f"tmpb{i}") for i in range(4)]
        P.op("pool", lambda e: e.memset(Vb.rearrange("p a b -> p (a b)"), 1.0), W=[Vbr])
        fc = [0]
        for kv in range(4):
            self.load(SZ, self.PROJ[:, O_BZ + kv * 512: O_BZ + (kv + 1) * 512].rearrange("(n p) d -> p n d", p=128), [SZr])
            P.op("act", lambda e: e.activation(out=SZ, in_=SZ, func=AF.Silu), R=[SZr], W=[SZr])
            self.load(Vb[:, :, 0:64], self.PROJ[:, O_BV + kv * 64: O_BV + (kv + 1) * 64].rearrange("(n p) d -> p n d", p=128),
                      [Vbr])
            for i in range(NT):
                q_, qr = qtl[i % 2]
                k_, kr = ktl[i % 2]
                rows = slice(i * 128, (i + 1) * 128)
                tcols = slice(i * 128, (i + 1) * 128)
                self.load(q_, self.PROJ[rows, O_BQ + kv * 512: O_BQ + (kv + 1) * 512], [qr])
                self.load(k_[:, 0:64], self.PROJ[rows, O_BK + kv * 64: O_BK + (kv + 1) * 64], [kr])
                self.load(k_[:, 64:128], self.PROJ[rows, O_BK + kv * 64: O_BK + (kv + 1) * 64], [kr])
                self.transposes(q_, qr, 4, lambda g0, n: QT[:, g0:g0 + n, tcols], QTr[i], "dve", banks=(6, 8))
                self.transposes(k_, kr, 1, lambda g0, n: KT[:, g0:g0 + n, tcols], KTr[i], "dve", banks=(6, 8))
            for n in range(NT):
                pc, pcr = PTc[n % 2]
                pp, ppr = PTp[n % 2]
                cols = slice(n * 128, (n + 1) * 128)
                pcols = slice((n - 1) * 128, n * 128)
                for which in ((1, 0) if n > 0 else (1,)):
                    bb = 0 if which == 1 else 2
                    kc_ = cols if which == 1 else pcols
                    kres = KTr[n] if which == 1 else KTr[n - 1]
                    dst, dstr = (pc, pcr) if which == 1 else (pp, ppr)
                    for hh in range(8):
                        m, r = hh // 2, hh % 2
                        ps, psr = self.ps[bb + r], self.psr[bb + r]
                        oc = m * 128
                        P.op("pe", lambda e, ps=ps, oc=oc, r=r, m=m, kc_=kc_, cols=cols: e.matmul(
                            ps[:, oc:oc + 128], lhsT=KT[64 * r:64 * r + 64, 0, kc_], rhs=QT[64 * r:64 * r + 64, m, cols],
                            start=True, stop=True), R=[kres, QTr[n]], W=[psr])
                    for r in range(2):
                        ps, psr = self.ps[bb + r], self.psr[bb + r]
                        P.op("act", lambda e, ps=ps, dst=dst, r=r: e.activation(
                            out=dst[:, r:8:2, :], in_=ps[:, 0:512].rearrange("p (a b) -> p a b", a=4, b=128),
                            func=AF.Exp, scale=0.125), R=[psr], W=[dstr])
                    P.op("dve", lambda e, dst=dst, which=which, kv=kv: e.tensor_tensor(
                        out=dst, in0=dst, in1=E[:, kv * 8:(kv + 1) * 8, which, :], op=ALU.mult), R=[dstr, Er], W=[dstr])
                for hh in range(8):
                    ob, obr = self.ps[4 + hh // 4], self.psr[4 + hh // 4]
                    oc = (hh % 4) * 66
                    if n > 0:
                        P.op("pe", lambda e, ob=ob, oc=oc, hh=hh, pp=pp, n=n: e.matmul(
                            ob[:, oc:oc + 65], lhsT=pp[:, hh, :], rhs=Vb[:, n - 1, 0:65], start=True, stop=False),
                             R=[ppr, Vbr], W=[obr])
                    P.op("pe", lambda e, ob=ob, oc=oc, hh=hh, pc=pc, n=n: e.matmul(
                        ob[:, oc:oc + 65], lhsT=pc[:, hh, :], rhs=Vb[:, n, 0:65], start=(n == 0), stop=True),
                         R=[pcr, Vbr], W=[obr])
                for half in range(2):
                    ob, obr = self.ps[4 + half], self.psr[4 + half]
                    d_, dr = den[fc[0] % 4]
                    t_, tr = tmp[fc[0] % 4]
                    fc[0] += 1
                    ov = ob[:, 0:264].rearrange("p (h d) -> p h d", h=4, d=66)
                    h0 = kv * 8 + half * 4
                    P.op("dve", lambda e, ov=ov, d_=d_, h0=h0: e.tensor_tensor(
                        out=d_, in0=ov[:, :, 64:65], in1=esink[:, h0:h0 + 4].rearrange("p (h o) -> p h o", h=4, o=1),
                        op=ALU.add), R=[obr, esr], W=[dr])
                    P.op("dve", lambda e, d_=d_: e.reciprocal(out=d_, in_=d_), R=[dr], W=[dr])
                    P.op("dve", lambda e, ov=ov, d_=d_, t_=t_: e.tensor_tensor(
                        out=t_, in0=ov[:, :, 0:64], in1=d_.broadcast_to([128, 4, 64]), op=ALU.mult), R=[obr, dr], W=[tr])
                    c0 = half * 256
                    P.op("pool", lambda e, t_=t_, c0=c0, n=n: e.tensor_tensor(
                        out=OG[:, n, c0:c0 + 256], in0=t_.rearrange("p h d -> p (h d)"), in1=SZ[:, n, c0:c0 + 256],
                        op=ALU.mult), R=[tr, SZr], W=[OGr[n]])
            for i in range(NT):
                tcols = slice(i * 128, (i + 1) * 128)
                self.transposes(OG[:, i, :], OGr[i], 4, lambda g0, n: OTs[:, g0:g0 + n, tcols], OTsr, "dve", banks=(6, 8))
            r0 = 2048 + kv * 512
            self.store(self.OT[r0:r0 + 512, :].rearrange("(h p) t -> p h t", p=128), OTs, [OTsr])

    def combine(self, l):
        P, io = self.P, self.io
        for half in range(2):
            self.new_phase(name=f"L{l}_combine")
            OTh, OThr = self.alloc([128, 48, 1024], BF16, "OTh")
            ov = self.OT[:, half * 1024:(half + 1) * 1024].rearrange("(c p) t -> p c t", p=128)
            for k0 in range(0, 48, 8):
                self.load(OTh[:, k0:k0 + 8, :], ov[:, k0:k0 + 8, :], [OThr])
            CW = 384
            NS = (D + CW - 1) // CW
            wb = [self.alloc([128, 48, CW], BF16, f"wc{i}") for i in range(2)]
            gt = [self.alloc([128, 3, CW], BF16, f"gt{i}") for i in range(3)]
            t1 = [self.alloc([128, CW], F32, f"t1_{i}") for i in range(2)]
            t2 = [self.alloc([128, CW], F32, f"t2_{i}") for i in range(2)]
            ys = [self.alloc([128, CW], BF16, f"ys{i}") for i in range(3)]
            pcs = [(self.alloc([128, 3, CW], F32, f"pc{i}")[0], [Res(f"pc{i}_{k}") for k in range(3)]) for i in range(3)]
            wsrc = [io[k][l * 2048:(l + 1) * 2048, :].rearrange("(c p) n -> p c n", p=128) for k in ("w_pa", "w_pb", "w_pc")]

            def issue(s):
                w_, wr = wb[s % 2]
                c0 = s * CW
                w = min(CW, D - c0)
                for b3 in range(3):
                    for k0 in (0, 8):
                        P.op("pool", lambda e, w_=w_, b3=b3, k0=k0, c0=c0, w=w: e.dma_start(
                            out=w_[:, b3 * 16 + k0:b3 * 16 + k0 + 8, 0:w], in_=wsrc[b3][:, k0:k0 + 8, c0:c0 + w]),
                             W=[wr], dma="wld")

            cnt = 0
            issue(0)
            for s in range(NS):
                if s + 1 < NS:
                    issue(s + 1)
                w_, wr = wb[s % 2]
                c0 = s * CW
                w = min(CW, D - c0)
                for ti in range(8):
                    i = half * 8 + ti
                    g_, gr = gt[cnt % 3]
                    a_, ar = t1[cnt % 2]
                    b_, br_ = t2[cnt % 2]
                    y_, yr = ys[cnt % 3]
                    cnt += 1
                    self.load(g_[:, :, 0:w], self.G[i * 128:(i + 1) * 128, :].rearrange("p (b f) -> p b f", b=3)[:, :, c0:c0 + w],
                              [gr])
                    pss = []
                    for b3 in range(3):
                        bk = self.nb(0, 6)
                        ps, psr = self.ps[bk], self.psr[bk]
                        pss.append((ps, psr))
                        for c in range(16):
                            P.op("pe", lambda e, ps=ps, w_=w_, b3=b3, c=c, ti=ti, w=w: e.matmul(
                                ps[:, 0:w], lhsT=OTh[:, b3 * 16 + c, ti * 128:(ti + 1) * 128], rhs=w_[:, b3 * 16 + c, 0:w],
                                start=(c == 0), stop=(c == 15)), R=[wr, OThr], W=[psr])
                    pc_, pcr = pcs[cnt % 3]
                    for b3 in range(3):
                        P.op("act", lambda e, pc_=pc_, b3=b3, ps=pss[b3][0], w=w: e.copy(out=pc_[:, b3, 0:w], in_=ps[:, 0:w]),
                             R=[pss[b3][1]], W=[pcr[b3]])
                    P.op("dve", lambda e, a_=a_, g_=g_, pc_=pc_, w=w: e.tensor_tensor(
                        out=a_[:, 0:w], in0=pc_[:, 0, 0:w], in1=g_[:, 0, 0:w], op=ALU.mult), R=[pcr[0], gr], W=[ar])
                    P.op("pool", lambda e, b_=b_, g_=g_, pc_=pc_, w=w: e.tensor_tensor(
                        out=b_[:, 0:w], in0=pc_[:, 1, 0:w], in1=g_[:, 1, 0:w], op=ALU.mult), R=[pcr[1], gr], W=[br_])
                    P.op("dve", lambda e, a_=a_, b_=b_, w=w: e.tensor_tensor(out=a_[:, 0:w], in0=a_[:, 0:w], in1=b_[:, 0:w],
                                                                           op=ALU.add), R=[ar, br_], W=[ar])
                    P.op("pool", lambda e, b_=b_, g_=g_, pc_=pc_, w=w: e.tensor_tensor(
                        out=b_[:, 0:w], in0=pc_[:, 2, 0:w], in1=g_[:, 2, 0:w], op=ALU.mult), R=[pcr[2], gr], W=[br_])
                    P.op("dve", lambda e, a_=a_, b_=b_, y_=y_, w=w: e.tensor_tensor(out=y_[:, 0:w], in0=a_[:, 0:w], in1=b_[:, 0:w],
                                                                                  op=ALU.add), R=[ar, br_], W=[yr])
                    self.store(self.Y[i * 128:(i + 1) * 128, c0:c0 + w], y_[:, 0:w], [yr])

    def out_proj(self, l):
        P, io = self.P, self.io
        self.new_phase(name=f"L{l}_yT")
        mark = self.base
        yT, _ = self.alloc([128, KC, T], BF16, "yT")
        yTr = [Res(f"yT{i}") for i in range(NT)]
        self.keep()
        yt = [self.alloc([128, D], BF16, f"yt{i}") for i in range(2)]
        for i in range(NT):
            y_, yr = yt[i % 2]
            self.load(y_, self.Y[i * 128:(i + 1) * 128, :], [yr])
            self.transposes(y_, yr, KC, lambda g0, n, i=i: yT[:, g0:g0 + n, i * 128:(i + 1) * 128], yTr[i],
                            "act" if i % 2 else "dve")
        self.new_phase(name=f"L{l}_w_o")
        ev = self.evac_store(self.Z, dt=F32, nbuf=2)
        self.gemm(yT, yTr, KC, io["w_o"][l * D:(l + 1) * D, :], D, 512, ev)
        self.new_phase(base=mark, name=f"L{l}_postnorm")

    def postnorm(self, l, xsrc, dst):
        P, io = self.P, self.io
        gbc, gr = self.alloc([128, D], F32, "gpost")
        self.load(gbc, io["post_bc"][l * 128:(l + 1) * 128, :], [gr])
        zs = [self.alloc([128, D], F32, f"zs{i}") for i in range(2)]
        xs = [self.alloc([128, D], F32, f"xr{i}") for i in range(2)]
        junk, jr = self.alloc([128, D], BF16, "junk")
        ss, ssr = self.alloc([128, 4], F32, "ss")
        for i in range(NT):
            z_, zr = zs[i % 2]
            x_, xr = xs[i % 2]
            rows = slice(i * 128, (i + 1) * 128)
            self.load(z_, self.Z[rows, :], [zr])
            self.load(x_, xsrc[rows, :], [xr])
            P.op("act", lambda e, z_=z_: e.activation(out=junk, in_=z_, func=AF.Square, accum_out=ss[:, 0:1]),
                 R=[zr], W=[jr, ssr])
            self.rstd_from_ss(ss, ssr, D, 0)
            P.op("dve", lambda e, z_=z_: e.scalar_tensor_tensor(out=z_, in0=z_, scalar=ss[:, 2:3], in1=gbc,
                                                                op0=ALU.mult, op1=ALU.mult), R=[zr, ssr, gr], W=[zr])
            P.op("pool", lambda e, z_=z_, x_=x_: e.tensor_tensor(out=z_, in0=z_, in1=x_, op=ALU.add), R=[zr, xr], W=[zr])
            self.store(dst[rows, :], z_, [zr])

    def build(self):
        io = self.io
        self.setup()
        xsrc = io["x"]
        for l in range(self.depth):
            hT, hTr = self.prenorm(l, xsrc)
            stg, wbufs = self.proj_in(l, hT, hTr)
            if self.stop == "proj":
                return
            self.gates(l, hT, hTr, stg, wbufs)
            self.new_phase(base=self.mark)
            if self.stop == "gates":
                return
            mark = self.mla(l)
            if self.stop == "mla":
                return
            self.attn_full(l, "A")
            self.new_phase(base=mark)
            if self.stop == "A":
                return
            self.attn_swa(l)
            if self.stop == "B":
                return
            self.attn_full(l, "C")
            if self.stop == "C":
                return
            self.combine(l)
            if self.stop == "Y":
                return
            self.out_proj(l)
            if self.stop == "Z":
                return
            dst = io["out"] if l == self.depth - 1 else self.X1
            self.postnorm(l, xsrc, dst)
            xsrc = dst


IN_SPECS = {
    "x": ([T, D], F32), "pos": ([128, NT], I32),
    "w_in": ([DEPTH * NSLAB_IN * 128, KC * 512], F32), "w_merge": ([DEPTH * D, 3 * D], F32), "w_o": ([DEPTH * D, D], F32),
    "w_uq": ([DEPTH * 1536, 3072], F32), "w_uk": ([DEPTH * 512, 2048], F32), "w_uv": ([DEPTH * 512, 2048], F32),
    "w_pa": ([DEPTH * 2048, D], F32), "w_pb": ([DEPTH * 2048, D], F32), "w_pc": ([DEPTH * 2048, D], F32),
    "pre_bc": ([DEPTH * 128, D], F32), "post_bc": ([DEPTH * 128, D], F32),
    "qn_bc": ([DEPTH * 128, 1536], F32), "kvn_bc": ([DEPTH * 128, 512], F32),
    "sink_bc": ([DEPTH * 128, 32], F32), "bf_bc": ([DEPTH * 128, 16], F32),
    "relb": ([128, 32 * 2 * 128], F32),
    "cstb": ([128, 512], BF16), "cst32": ([128, 288], F32),
}


def build_nc(depth=DEPTH, stop=None, dumps=(), scopes=False):
    nc = bass.Bass("TRN2", target_bir_lowering=False)
    io = {}
    for k, (shp, dt) in IN_SPECS.items():
        io[k] = nc.dram_tensor(k, list(shp), dt, kind="ExternalInput").ap()
    io["out"] = nc.dram_tensor("out", [T, D], F32, kind="ExternalOutput").ap()
    es = ExitStack()
    with es:
        B = Builder(nc, es, io, depth=depth, stop=stop, dumps=dumps)
        B.P.scopes = scopes
        B.build()
        B.P.final_wait()
        B.P.emit()
    return nc


def t5_bucket_np(dist):
    d = np.maximum(dist, 0)
    large = 16 + (np.log(np.maximum(d, 1).astype(np.float32) / np.float32(16)) / np.float32(math.log(128 / 16))
                  * np.float32(16)).astype(np.int32)
    large = np.minimum(large, 31)
    return np.where(d < 16, d, large)


def prep_shared(inputs):
    f = lambda a: np.ascontiguousarray(np.asarray(a, dtype=np.float32))
    sh = {}
    wi = np.zeros((DEPTH, KC, 128, NSLAB_IN * 512), np.float32)
    wi[:, :, :, :IN_DIM] = f(inputs["w_in"]).reshape(DEPTH, KC, 128, IN_DIM)
    sh["w_in"] = np.ascontiguousarray(
        wi.reshape(DEPTH, KC, 128, NSLAB_IN, 512).transpose(0, 3, 2, 1, 4)).reshape(DEPTH * NSLAB_IN * 128, KC * 512)
    del wi
    sh["w_merge"] = f(inputs["w_merge"]).reshape(DEPTH * D, 3 * D)
    sh["w_o"] = f(inputs["w_o"]).reshape(DEPTH * D, D)
    sh["w_uq"] = f(inputs["w_uq"]).reshape(DEPTH * 1536, 3072)
    sh["w_uk"] = f(inputs["w_uk"]).reshape(DEPTH * 512, 2048)
    sh["w_uv"] = f(inputs["w_uv"]).reshape(DEPTH * 512, 2048)
    sh["w_pa"] = f(inputs["w_proj_a"]).reshape(DEPTH * 2048, D)
    sh["w_pb"] = f(inputs["w_proj_b"]).reshape(DEPTH * 2048, D)
    sh["w_pc"] = f(inputs["w_proj_c"]).reshape(DEPTH * 2048, D)

    def bc(a):
        a = f(a)
        return np.ascontiguousarray(np.broadcast_to(a[:, None, :], (DEPTH, 128, a.shape[1]))).reshape(DEPTH * 128, -1)

    sh["pre_bc"] = bc(inputs["pre_norm"])
    sh["post_bc"] = bc(inputs["post_norm"])
    sh["qn_bc"] = bc(inputs["q_a_norm"])
    sh["kvn_bc"] = bc(inputs["kv_a_norm"])
    sh["sink_bc"] = bc(inputs["sinks"])
    sh["bf_bc"] = bc(inputs["b_f"])
    s_i = np.arange(128)[:, None]
    q_i = np.arange(128)[None, :]
    rt = f(inputs["rel_table"])
    d_cur = q_i - s_i
    d_prev = q_i - s_i + 128
    relb = np.zeros((128, 32, 2, 128), np.float32)
    relb[:, :, 0, :] = rt[t5_bucket_np(d_prev)].transpose(0, 2, 1)
    relb[:, :, 1, :] = rt[t5_bucket_np(d_cur)].transpose(0, 2, 1)
    sh["relb"] = relb.reshape(128, 32 * 2 * 128)
    cstb = np.zeros((128, 512), np.float32)
    cstb[:, 0:128] = np.eye(128)
    cstb[:, 128:256] = (s_i <= q_i)
    cstb[:, 256:384] = (s_i > q_i)
    cstb[:, 384:512] = (s_i <= q_i)
    sh["cstb"] = cstb.astype(ml_dtypes.bfloat16)
    c32 = np.zeros((128, 288), np.float32)
    c32[:, 0:128] = (s_i <= q_i)
    c32[:, 128:256] = 1.0
    half = 32
    c32[:, 256:288] = (np.float32(10000.0) ** (-np.arange(half, dtype=np.float32) / np.float32(half)))[None, :]
    sh["cst32"] = c32
    return sh


def core_inputs(inputs, sh, c):
    m = dict(sh)
    m["x"] = np.ascontiguousarray(np.asarray(inputs["x"][c], dtype=np.float32))
    p = np.asarray(inputs["positions"][c], dtype=np.int32)
    m["pos"] = np.ascontiguousarray(p.reshape(NT, 128).T)
    return m


def kernel(**inputs):
    sh = prep_shared(inputs)
    nc = build_nc()
    n = 8
    in_maps = [core_inputs(inputs, sh, c) for c in range(n)]
    res = run_bass_kernel_spmd(nc, in_maps, core_ids=list(range(n)))
    return np.stack([r["out"] for r in res.results], axis=0).astype(np.float32)
```

```python
import math
from contextlib import ExitStack
import numpy as np
import ml_dtypes
import concourse.bass as bass
import concourse.mybir as mybir
from concourse.bass_utils import run_bass_kernel_spmd

F32 = mybir.dt.float32
BF16 = mybir.dt.bfloat16
I32 = mybir.dt.int32
AF = mybir.ActivationFunctionType
ALU = mybir.AluOpType
AX = mybir.AxisListType

T = 2048
NT = 16
D = 4096
KC = 32
DEPTH = 2
EPS = 1e-6
IN_SIZES = (1536, 512, 64, 2048, 2048, 256, 256, 2048, 2048, 2048, 2048, 16, 2048)
IN_DIM = sum(IN_SIZES)
OFFS = [0]
for _n in IN_SIZES:
    OFFS.append(OFFS[-1] + _n)
(O_CQ, O_CKV, O_KR, O_AZ, O_BQ, O_BK, O_BV, O_BZ, O_CQ2, O_CK, O_CV, O_CF, O_CZ) = OFFS[:13]
SEM_LIMIT = 30000
NSLAB_IN = (IN_DIM + 511) // 512

ENGS = ["pe", "act", "dve", "pool", "sp"]
BLK = {"pe": "tensor", "act": "scalar", "dve": "vector", "pool": "gpsimd", "sp": "sync"}


class Res:
    __slots__ = ("name", "w", "r", "excl")

    def __init__(self, name, excl=False):
        self.name = name
        self.w = []
        self.r = {}
        self.excl = excl


class Op:
    __slots__ = ("eng", "fn", "deps", "signal", "val", "dma", "ep", "ph")


class Prog:
    NSLOT = 8

    def __init__(self, nc, es):
        self.nc = nc
        self.es = es
        self.ops = {e: [] for e in ENGS}
        self.sems = {}
        self.last = {}
        self.dmas = {}
        self.phase = "setup"
        self.scopes = False

    def _new(self, eng, fn, dma):
        o = Op()
        o.eng = eng
        o.fn = fn
        o.dma = dma
        o.signal = dma is not None
        o.val = None
        o.ep = 0
        o.deps = []
        o.ph = self.phase
        return o

    def op(self, eng, fn, R=(), W=(), dma=None):
        o = self._new(eng, fn, dma)
        deps = {}
        if dma is not None:
            lst = self.dmas.setdefault(dma, [])
            n = len(lst)
            o.dma = (dma, n % self.NSLOT)
            if n >= self.NSLOT:
                prev = lst[n - self.NSLOT]
                deps[id(prev)] = (prev, True)
            lst.append(o)
        key = o.dma if o.dma else eng
        for r in R:
            for wo in r.w:
                deps[id(wo)] = (wo, True)
            if r.excl:
                for k2, ro in r.r.items():
                    if k2 != key and id(ro) not in deps:
                        deps[id(ro)] = (ro, False)
        appendw = set()
        for w in W:
            if (o.dma is not None and w.w and not w.r
                    and all(x.dma is not None and x.dma[0] == o.dma[0] for x in w.w)):
                appendw.add(id(w))
                continue
            for wo in w.w:
                if id(wo) not in deps:
                    deps[id(wo)] = (wo, False)
            for ro in w.r.values():
                if id(ro) not in deps:
                    deps[id(ro)] = (ro, False)
        for d, raw in deps.values():
            if d is o:
                continue
            if d.dma is None and o.dma is None and d.eng == eng and eng == "pe":
                continue
            d.signal = True
            o.deps.append(d)
        for r in R:
            r.r[key] = o
        for w in W:
            if id(w) in appendw:
                w.w.append(o)
            else:
                w.w = [o]
                w.r = {}
        self.ops[eng].append(o)
        self.last[key] = o
        return o

    def barrier(self, engs=ENGS):
        lasts = list(self.last.values())
        for d in lasts:
            d.signal = True
        for e in engs:
            o = self._new(e, None, None)
            o.deps = list(lasts)
            self.ops[e].append(o)

    def final_wait(self, eng="sp"):
        self.barrier(engs=[eng])

    def emit(self):
        nc = self.nc
        cnt = {}
        for e in ENGS:
            for o in self.ops[e]:
                if not o.signal:
                    continue
                key = o.dma if o.dma else e
                inc = 16 if o.dma else 1
                ep, v = cnt.get(key, (0, 0))
                if v + inc > SEM_LIMIT:
                    ep, v = ep + 1, 0
                v += inc
                cnt[key] = (ep, v)
                o.ep, o.val = ep, v
        for key, (ep, v) in cnt.items():
            nm = key if isinstance(key, str) else f"{key[0]}{key[1]}"
            for k in range(ep + 1):
                self.sems[(key, k)] = self.es.enter_context(nc.semaphore(f"s_{nm}_{k}"))
        block = self.es.enter_context(nc.Block())
        for e in ENGS:
            ops = self.ops[e]

            def body(eng, ops=ops, e=e):
                seen = {}
                cur = None
                for o in ops:
                    need = {}
                    for d in o.deps:
                        k = (d.dma if d.dma else d.eng, d.ep)
                        if d.val > need.get(k, 0):
                            need[k] = d.val
                    for k, v in need.items():
                        if seen.get(k, 0) < v:
                            eng.wait_ge(self.sems[k], v)
                            seen[k] = v
                    if self.scopes and o.fn is not None and o.ph != cur:
                        if cur is not None:
                            nc.pop_named_scope(cur)
                        cur = o.ph
                        nc.push_named_scope(cur)
                    if o.fn is None:
                        continue
                    ins = o.fn(eng)
                    if o.signal:
                        key = o.dma if o.dma else e
                        ins.then_inc(self.sems[(key, o.ep)], 16 if o.dma else 1)
                if cur is not None:
                    nc.pop_named_scope(cur)

            getattr(block, BLK[e])(body)


ARENA_BYTES = 207 * 1024


class Builder:
    def __init__(self, nc, es, io, depth=DEPTH, stop=None, dumps=()):
        self.nc = nc
        self.es = es
        self.io = io
        self.depth = depth
        self.stop = stop
        self.dumps = set(dumps)
        self.P = Prog(nc, es)
        self.arena = es.enter_context(nc.sbuf_tensor("arena", [128, ARENA_BYTES // 2], BF16))
        self.base = 0
        self.off = 0
        self.ps = [es.enter_context(nc.psum_tensor(f"psb{i}", [128, 512], F32)) for i in range(8)]
        self.psr = [Res(f"psb{i}", excl=True) for i in range(8)]
        self.bankc = {}
        self.flip = 0
        self.scr = {}

    def alloc(self, shape, dt, name="t"):
        n = int(np.prod(shape[1:]))
        nb = n * (2 if dt == BF16 else 4)
        o = self.off
        assert o % 4 == 0
        self.off = o + (nb + 63) // 64 * 64
        assert self.off <= ARENA_BYTES, (name, self.off)
        a = self.arena[:, o // 2:(o + nb) // 2]
        if dt != BF16:
            a = a.bitcast(dt)
        if len(shape) == 3:
            a = a.rearrange("p (a b) -> p a b", a=shape[1], b=shape[2])
        elif len(shape) == 4:
            a = a.rearrange("p (a b c) -> p a b c", a=shape[1], b=shape[2], c=shape[3])
        if shape[0] != 128:
            a = a[0:shape[0]]
        return a, Res(name)

    def keep(self):
        self.base = self.off

    def new_phase(self, base=None, name=None):
        self.P.barrier()
        if name is not None:
            self.P.phase = name
        if base is not None:
            self.base = base
        self.off = self.base

    def dram(self, name, shape, dt):
        kind = "ExternalOutput" if name in self.dumps else "Internal"
        t = self.nc.dram_tensor(name, list(shape), dt, kind=kind).ap()
        self.scr[name] = t
        return t

    def nb(self, lo=0, hi=8):
        b = self.bankc.get((lo, hi), lo)
        self.bankc[(lo, hi)] = lo + (b + 1 - lo) % (hi - lo)
        return b

    def alt(self):
        self.flip ^= 1
        return "act" if self.flip else "dve"

    def copy(self, eng, out, in_, R, W):
        if eng == "act":
            self.P.op("act", lambda e: e.copy(out=out, in_=in_), R=R, W=W)
        else:
            self.P.op(eng, lambda e: e.tensor_copy(out=out, in_=in_), R=R, W=W)

    def load(self, out, in_, W, q="sp", st="ld"):
        self.P.op(q, lambda e: e.dma_start(out=out, in_=in_), W=W, dma=st)

    def store(self, out, in_, R, q="sp", st="st"):
        self.P.op(q, lambda e: e.dma_start(out=out, in_=in_), R=R, dma=st)

    def transposes(self, src, src_res, cols, dst_fn, dst_res, eng, rows=128, banks=(0, 8)):
        P = self.P
        if isinstance(cols, int):
            cols = [c * rows for c in range(cols)]
        nblk = len(cols)
        g0 = 0
        while g0 < nblk:
            n = min(4, nblk - g0)
            b = self.nb(*banks)
            ps, psr = self.ps[b], self.psr[b]
            for k in range(n):
                c0 = cols[g0 + k]
                P.op("pe", lambda e, ps=ps, k=k, c0=c0: e.matmul(
                    ps[0:rows, k * 128:(k + 1) * 128], lhsT=src[:, c0:c0 + rows], rhs=self.ident,
                    start=True, stop=True), R=[src_res, self.cres], W=[psr])
            dst = dst_fn(g0, n)
            pv = ps[0:rows, 0:n * 128].rearrange("p (a b) -> p a b", a=n, b=128)
            self.copy(eng, dst, pv, [psr], [dst_res])
            g0 += n

    def rstd_from_ss(self, ss, ssr, n, col):
        P = self.P
        P.op("dve", lambda e: e.tensor_scalar(out=ss[:, col + 1:col + 2], in0=ss[:, col:col + 1], scalar1=1.0 / n,
                                              scalar2=EPS, op0=ALU.mult, op1=ALU.add), R=[ssr], W=[ssr])
        P.op("act", lambda e: e.activation(out=ss[:, col + 1:col + 2], in_=ss[:, col + 1:col + 2], func=AF.Sqrt),
             R=[ssr], W=[ssr])
        P.op("dve", lambda e: e.reciprocal(out=ss[:, col + 2:col + 3], in_=ss[:, col + 1:col + 2]), R=[ssr], W=[ssr])

    def gemm(self, AT, ATres, kc, wsrc, ncols, slab_w, evac, tiles=range(NT), banks=(0, 8), wbufs=None, slab_src=None):
        P = self.P
        if wbufs is None:
            wbufs = [self.alloc([128, kc, slab_w], BF16, f"wb{i}") for i in range(2)]
        wv = wsrc.rearrange("(c p) n -> p c n", p=128) if slab_src is None else None
        nslab = (ncols + slab_w - 1) // slab_w

        def issue(s):
            c0 = s * slab_w
            w = min(slab_w, ncols - c0)
            wb, wr = wbufs[s % 2]
            step = 8 if kc >= 8 else kc
            for k0 in range(0, kc, step):
                k1 = min(kc, k0 + step)
                src = wv[:, k0:k1, c0:c0 + w] if slab_src is None else slab_src(s)[:, k0:k1, 0:w]
                P.op("pool", lambda e, wb=wb, k0=k0, k1=k1, w=w, src=src: e.dma_start(
                    out=wb[:, k0:k1, 0:w], in_=src), W=[wr], dma="wld")

        issue(0)
        for s in range(nslab):
            if s + 1 < nslab:
                issue(s + 1)
            c0 = s * slab_w
            w = min(slab_w, ncols - c0)
            wb, wr = wbufs[s % 2]
            for i in tiles:
                b = self.nb(*banks)
                ps, psr = self.ps[b], self.psr[b]
                for c in range(kc):
                    P.op("pe", lambda e, ps=ps, wb=wb, c=c, i=i, w=w: e.matmul(
                        ps[:, 0:w], lhsT=AT[:, c, i * 128:(i + 1) * 128], rhs=wb[:, c, 0:w],
                        start=(c == 0), stop=(c == kc - 1)), R=[wr, ATres[i]], W=[psr])
                evac(i, c0, w, ps, psr)

    def setup(self):
        P, io = self.P, self.io
        self.cres = Res("consts")
        cb, _ = self.alloc([128, 512], BF16, "cstb")
        c32, _ = self.alloc([128, 288], F32, "cst32")
        self.load(cb, io["cstb"], [self.cres])
        self.load(c32, io["cst32"], [self.cres])
        self.ident = cb[:, 0:128]
        self.maskT = cb[:, 128:256]
        self.bmask = cb[:, 256:512]
        self.tri = c32[:, 0:128]
        self.ones32 = c32[:, 128:256]
        invf = c32[:, 256:288]
        self.cos, _ = self.alloc([128, NT, 32], F32, "cos")
        self.sin, _ = self.alloc([128, NT, 32], F32, "sin")
        self.cf, self.cfr = self.alloc([128, NT, 16], F32, "cf")
        self.keep()
        posi, pr = self.alloc([128, NT], I32, "posi")
        posf, pfr = self.alloc([128, NT], F32, "posf")
        u, ur = self.alloc([128, NT, 32], F32, "u")
        ui, uir = self.alloc([128, NT, 32], I32, "ui")
        uk, ukr = self.alloc([128, NT, 32], F32, "uk")
        self.load(posi, io["pos"], [pr])
        P.op("dve", lambda e: e.tensor_copy(out=posf, in_=posi), R=[pr], W=[pfr])
        for i in range(NT):
            P.op("dve", lambda e, i=i: e.tensor_scalar(out=u[:, i, :], in0=invf, scalar1=posf[:, i:i + 1],
                                                      scalar2=1.0 / (2 * math.pi), op0=ALU.mult, op1=ALU.mult),
                 R=[pfr, self.cres], W=[ur])
        uf = u.rearrange("p a b -> p (a b)")
        uif = ui.rearrange("p a b -> p (a b)")
        ukf = uk.rearrange("p a b -> p (a b)")
        for dst, shift in ((self.sin, 0.0), (self.cos, 0.25)):
            df = dst.rearrange("p a b -> p (a b)")
            w, wr = self.alloc([128, NT * 32], F32, "w")
            P.op("dve", lambda e, w=w, shift=shift: e.tensor_scalar(out=w, in0=uf, scalar1=shift, scalar2=None,
                                                                    op0=ALU.add), R=[ur], W=[wr])
            P.op("dve", lambda e, w=w: e.tensor_copy(out=uif, in_=w), R=[wr], W=[uir])
            P.op("dve", lambda e: e.tensor_copy(out=ukf, in_=uif), R=[uir], W=[ukr])
            P.op("dve", lambda e, w=w: e.tensor_tensor(out=w, in0=w, in1=ukf, op=ALU.subtract), R=[wr, ukr], W=[wr])
            P.op("dve", lambda e, w=w: e.tensor_scalar(out=ukf, in0=w, scalar1=0.5, scalar2=None, op0=ALU.is_gt),
                 R=[wr], W=[ukr])
            P.op("dve", lambda e, w=w: e.tensor_tensor(out=w, in0=w, in1=ukf, op=ALU.subtract), R=[wr, ukr], W=[wr])
            P.op("dve", lambda e, w=w: e.tensor_scalar(out=ukf, in0=w, scalar1=-0.5, scalar2=None, op0=ALU.is_lt),
                 R=[wr], W=[ukr])
            P.op("dve", lambda e, w=w: e.tensor_tensor(out=w, in0=w, in1=ukf, op=ALU.add), R=[wr, ukr], W=[wr])
            P.op("act", lambda e, w=w, df=df: e.activation(out=df, in_=w, func=AF.Sin, scale=2 * math.pi),
                 R=[wr], W=[self.cres])
        self.PROJ = self.dram("PROJ", [T, 17408], BF16)
        self.G = self.dram("G", [T, 3 * D], BF16)
        self.QA = self.dram("QA", [T, 3072], BF16)
        self.KN = self.dram("KN", [T, 2048], BF16)
        self.VA = self.dram("VA", [T, 2048], BF16)
        self.OT = self.dram("OT", [3 * 2048, T], BF16)
        self.Y = self.dram("Y", [T, D], BF16)
        self.Z = self.dram("Z", [T, D], F32)
        self.X1 = self.dram("X1", [T, D], F32)

    def prenorm(self, l, xsrc):
        P, io = self.P, self.io
        self.new_phase(name=f"L{l}_prenorm")
        self.mark = self.base
        hT, _ = self.alloc([128, KC, T], BF16, "hT")
        self.keep()
        hTr = [Res(f"hT{i}") for i in range(NT)]
        gbc, gr = self.alloc([128, D], F32, "gbc")
        self.load(gbc, io["pre_bc"][l * 128:(l + 1) * 128, :], [gr])
        xs = [self.alloc([128, D], F32, f"xs{i}") for i in range(2)]
        hb, hbr = self.alloc([128, D], BF16, "hb")
        ss, ssr = self.alloc([128, 4], F32, "ss")
        for i in range(NT):
            x_, xr = xs[i % 2]
            self.load(x_, xsrc[i * 128:(i + 1) * 128, :], [xr])
            P.op("act", lambda e, x_=x_: e.activation(out=hb, in_=x_, func=AF.Square, accum_out=ss[:, 0:1]),
                 R=[xr], W=[hbr, ssr])
            self.rstd_from_ss(ss, ssr, D, 0)
            P.op("dve", lambda e, x_=x_: e.scalar_tensor_tensor(out=hb, in0=x_, scalar=ss[:, 2:3], in1=gbc,
                                                                op0=ALU.mult, op1=ALU.mult),
                 R=[xr, ssr, gr], W=[hbr])
            self.transposes(hb, hbr, KC, lambda g0, n, i=i: hT[:, g0:g0 + n, i * 128:(i + 1) * 128], hTr[i],
                            "act" if i % 2 else "dve")
        return hT, hTr

    def proj_in(self, l, hT, hTr):
        P, io = self.P, self.io
        self.new_phase(name=f"L{l}_proj_in")
        stg = [self.alloc([128, 512], BF16, f"stg{i}") for i in range(4)]
        cnt = [0]

        def evac(i, c0, w, ps, psr):
            s_, sr = stg[cnt[0] % 4]
            cnt[0] += 1
            self.copy("act", s_[:, 0:w], ps[:, 0:w], [psr], [sr])
            self.store(self.PROJ[i * 128:(i + 1) * 128, c0:c0 + w], s_[:, 0:w], [sr])
            if c0 <= O_CF < c0 + w:
                o = O_CF - c0
                P.op("act", lambda e: e.copy(out=self.cf[:, i, :], in_=ps[:, o:o + 16]), R=[psr], W=[self.cfr])

        def slab_src(s):
            r0 = (l * NSLAB_IN + s) * 128
            return io["w_in"][r0:r0 + 128, :].rearrange("p (c n) -> p c n", c=KC, n=512)

        wbufs = [self.alloc([128, KC, 512], BF16, f"wb{i}") for i in range(2)]
        self.gemm(hT, hTr, KC, None, IN_DIM, 512, evac, slab_src=slab_src, wbufs=wbufs)
        return stg, wbufs

    def gates(self, l, hT, hTr, stg, wbufs):
        P, io = self.P, self.io
        P.phase = f"L{l}_gates"
        cnt = [0]

        def evac(i, c0, w, ps, psr):
            s_, sr = stg[cnt[0] % 4]
            cnt[0] += 1
            P.op("act", lambda e: e.activation(out=s_[:, 0:w], in_=ps[:, 0:w], func=AF.Sigmoid), R=[psr], W=[sr])
            self.store(self.G[i * 128:(i + 1) * 128, c0:c0 + w], s_[:, 0:w], [sr])

        self.gemm(hT, hTr, KC, io["w_merge"][l * D:(l + 1) * D, :], 3 * D, 512, evac, wbufs=wbufs)

    def evac_store(self, dst, dt=BF16, nbuf=4):
        stg = [self.alloc([128, 512], dt, f"es{i}") for i in range(nbuf)]
        cnt = [0]

        def evac(i, c0, w, ps, psr):
            s_, sr = stg[cnt[0] % nbuf]
            cnt[0] += 1
            self.copy("act", s_[:, 0:w], ps[:, 0:w], [psr], [sr])
            self.store(dst[i * 128:(i + 1) * 128, c0:c0 + w], s_[:, 0:w], [sr])

        return evac

    def mla(self, l):
        P, io = self.P, self.io
        self.new_phase(name=f"L{l}_mla_prep")
        mark = self.base
        self.KRT, self.KRTr = self.alloc([128, 1, T], BF16, "KRT")
        krt_base = self.off
        cqT, _ = self.alloc([128, 12, T], BF16, "cqT")
        ckvT, _ = self.alloc([128, 4, T], BF16, "ckvT")
        cqTr = [Res(f"cqT{i}") for i in range(NT)]
        ckvTr = [Res(f"ckvT{i}") for i in range(NT)]
        self.keep()
        gq, gqr = self.alloc([128, 1536], F32, "gq")
        gkv, gkvr = self.alloc([128, 512], F32, "gkv")
        self.load(gq, io["qn_bc"][l * 128:(l + 1) * 128, :], [gqr])
        self.load(gkv, io["kvn_bc"][l * 128:(l + 1) * 128, :], [gkvr])
        tin = [self.alloc([128, 2112], BF16, f"tin{i}") for i in range(2)]
        junk, jr = self.alloc([128, 1536], BF16, "junk")
        cqn, cqnr = self.alloc([128, 1536], BF16, "cqn")
        ckvn, ckvnr = self.alloc([128, 512], BF16, "ckvn")
        ss, ssr = self.alloc([128, 8], F32, "ss")
        tt, ttr = self.alloc([128, 4, 32], F32, "tt")
        krr, krrr = self.alloc([128, 64], BF16, "krr")
        for i in range(NT):
            t_, tr = tin[i % 2]
            self.load(t_, self.PROJ[i * 128:(i + 1) * 128, 0:2112], [tr])
            P.op("act", lambda e, t_=t_: e.activation(out=junk, in_=t_[:, 0:1536], func=AF.Square,
                                                      accum_out=ss[:, 0:1]), R=[tr], W=[jr, ssr])
            P.op("act", lambda e, t_=t_: e.activation(out=junk[:, 0:512], in_=t_[:, 1536:2048], func=AF.Square,
                                                      accum_out=ss[:, 3:4]), R=[tr], W=[jr, ssr])
            self.rstd_from_ss(ss, ssr, 1536, 0)
            self.rstd_from_ss(ss, ssr, 512, 3)
            P.op("dve", lambda e, t_=t_: e.scalar_tensor_tensor(out=cqn, in0=t_[:, 0:1536], scalar=ss[:, 2:3], in1=gq,
                                                                op0=ALU.mult, op1=ALU.mult), R=[tr, ssr, gqr], W=[cqnr])
            P.op("dve", lambda e, t_=t_: e.scalar_tensor_tensor(out=ckvn, in0=t_[:, 1536:2048], scalar=ss[:, 5:6],
                                                                in1=gkv, op0=ALU.mult, op1=ALU.mult),
                 R=[tr, ssr, gkvr], W=[ckvnr])
            self.transposes(cqn, cqnr, 12, lambda g0, n, i=i: cqT[:, g0:g0 + n, i * 128:(i + 1) * 128], cqTr[i], "act")
            self.transposes(ckvn, ckvnr, 4, lambda g0, n, i=i: ckvT[:, g0:g0 + n, i * 128:(i + 1) * 128], ckvTr[i], "act")
            x1, x2 = t_[:, 2048:2080], t_[:, 2080:2112]
            cs, sn = self.cos[:, i, :], self.sin[:, i, :]
            for k, (a, b_) in enumerate(((x1, cs), (x2, sn), (x2, cs), (x1, sn))):
                P.op("pool", lambda e, k=k, a=a, b_=b_: e.tensor_tensor(out=tt[:, k, :], in0=a, in1=b_, op=ALU.mult),
                     R=[tr, self.cres], W=[ttr])
            P.op("pool", lambda e: e.tensor_tensor(out=krr[:, 0:32], in0=tt[:, 0, :], in1=tt[:, 1, :], op=ALU.subtract),
                 R=[ttr], W=[krrr])
            P.op("pool", lambda e: e.tensor_tensor(out=krr[:, 32:64], in0=tt[:, 2, :], in1=tt[:, 3, :], op=ALU.add),
                 R=[ttr], W=[krrr])
            self.transposes(krr, krrr, [0], lambda g0, n, i=i: self.KRT[0:64, 0:1, i * 128:(i + 1) * 128], self.KRTr,
                            "dve", rows=64)
        self.new_phase(name=f"L{l}_uq")
        qb = [self.alloc([128, 384], BF16, f"qb{i}") for i in range(3)]
        rt = [self.alloc([128, 4, 2, 32], F32, f"rt{i}") for i in range(2)]
        cnt = [0]

        def evac_q(i, c0, w, ps, psr):
            q_, qr = qb[cnt[0] % 3]
            t4, t4r = rt[cnt[0] % 2]
            cnt[0] += 1
            P.op("act", lambda e: e.copy(out=q_, in_=ps[:, 0:384]), R=[psr], W=[qr])
            pv = ps[:, 0:384].rearrange("p (h d) -> p h d", h=2, d=192)
            qv = q_.rearrange("p (h d) -> p h d", h=2, d=192)
            cs = self.cos[:, i:i + 1, :].broadcast_to([128, 2, 32])
            sn = self.sin[:, i:i + 1, :].broadcast_to([128, 2, 32])
            x1, x2 = pv[:, :, 128:160], pv[:, :, 160:192]
            for k, (a, b_) in enumerate(((x1, cs), (x2, sn), (x2, cs), (x1, sn))):
                P.op("dve", lambda e, k=k, a=a, b_=b_: e.tensor_tensor(out=t4[:, k, :, :], in0=a, in1=b_, op=ALU.mult),
                     R=[psr, self.cres], W=[t4r])
            P.op("dve", lambda e: e.tensor_tensor(out=qv[:, :, 128:160], in0=t4[:, 0, :, :], in1=t4[:, 1, :, :],
                                                  op=ALU.subtract), R=[t4r], W=[qr])
            P.op("dve", lambda e: e.tensor_tensor(out=qv[:, :, 160:192], in0=t4[:, 2, :, :], in1=t4[:, 3, :, :],
                                                  op=ALU.add), R=[t4r], W=[qr])
            self.store(self.QA[i * 128:(i + 1) * 128, c0:c0 + 384], q_, [qr])

        self.gemm(cqT, cqTr, 12, io["w_uq"][l * 1536:(l + 1) * 1536, :], 3072, 384, evac_q)
        P.phase = f"L{l}_ukv"
        wb = [self.alloc([128, 4, 512], BF16, f"wkv{i}") for i in range(2)]
        ev = self.evac_store(self.KN)
        self.gemm(ckvT, ckvTr, 4, io["w_uk"][l * 512:(l + 1) * 512, :], 2048, 512, ev, wbufs=wb)
        ev2 = self.evac_store(self.VA)
        self.gemm(ckvT, ckvTr, 4, io["w_uv"][l * 512:(l + 1) * 512, :], 2048, 512, ev2, wbufs=wb)
        self.new_phase(base=krt_base)
        return mark

    def attn_full(self, l, kind):
        P, io = self.P, self.io
        self.new_phase(name=f"L{l}_attn{kind}")
        isA = kind == "A"
        KRT = self.KRT if isA else None
        br = 0 if isA else 2
        scale = (192.0 if isA else 128.0) ** -0.5
        if not isA:
            bfb, bfr = self.alloc([128, 1, 16], F32, "bfb")
            self.load(bfb[:, 0, :], io["bf_bc"][l * 128:(l + 1) * 128, :], [bfr])
            lf, lfr = self.alloc([128, NT, 16], F32, "lf")
            ncum, ncr = self.alloc([128, NT, 16], F32, "ncum")
            nrun, nrr = self.alloc([128, NT + 1, 16], F32, "nrun")
            biasC, bcr = self.alloc([128, 136, 16], F32, "biasC")
            P.op("dve", lambda e: e.tensor_tensor(out=lf, in0=self.cf, in1=bfb.broadcast_to([128, NT, 16]), op=ALU.add),
                 R=[self.cfr, bfr], W=[lfr])
            P.op("act", lambda e: e.activation(out=lf, in_=lf, func=AF.Exp, scale=-1.0), R=[lfr], W=[lfr])
            P.op("act", lambda e: e.activation(out=lf, in_=lf, func=AF.Ln, bias=1.0, scale=1.0), R=[lfr], W=[lfr])
            P.op("pool", lambda e: e.memset(nrun[:, 0, :], 0.0), W=[nrr])
            for i in range(NT):
                b = self.nb(0, 4)
                ps, psr = self.ps[b], self.psr[b]
                P.op("pe", lambda e, ps=ps, i=i: e.matmul(ps[:, 0:16], lhsT=self.tri, rhs=lf[:, i, :], start=True,
                                                          stop=True), R=[lfr, self.cres], W=[psr])
                P.op("pe", lambda e, ps=ps, i=i: e.matmul(ps[:, 16:32], lhsT=self.ones32, rhs=lf[:, i, :], start=True,
                                                          stop=True), R=[lfr, self.cres], W=[psr])
                P.op("dve", lambda e, ps=ps, i=i: e.tensor_tensor(out=ncum[:, i, :], in0=ps[:, 0:16], in1=nrun[:, i, :],
                                                                  op=ALU.add), R=[psr, nrr], W=[ncr])
                P.op("dve", lambda e, ps=ps, i=i: e.tensor_tensor(out=nrun[:, i + 1, :], in0=ps[:, 16:32],
                                                                  in1=nrun[:, i, :], op=ALU.add), R=[psr, nrr], W=[nrr])
            nmid, nmr = self.alloc([128, NT, 16], F32, "nmid")
            P.op("dve", lambda e: e.tensor_tensor(out=nmid, in0=nrun[:, 0:NT, :], in1=nrun[:, 1:NT + 1, :], op=ALU.add),
                 R=[nrr], W=[nmr])
            P.op("dve", lambda e: e.tensor_scalar(out=nmid, in0=nmid, scalar1=0.5, scalar2=None, op0=ALU.mult),
                 R=[nmr], W=[nmr])
            for i in range(NT):
                b0 = i * (i + 1) // 2
                P.op("dve", lambda e, i=i, b0=b0: e.tensor_tensor(
                    out=biasC[:, b0:b0 + i + 1, :], in0=ncum[:, 0:i + 1, :],
                    in1=nmid[:, i:i + 1, :].broadcast_to([128, i + 1, 16]), op=ALU.subtract),
                     R=[ncr, nmr], W=[bcr])
        sets = []
        for k in range(2):
            QT_, _ = self.alloc([128, 4, T], BF16, f"QT{k}")
            KT_, _ = self.alloc([128, 4, T], BF16, f"KT{k}")
            QRT_ = self.alloc([128, 4, T], BF16, f"QRT{k}")[0] if isA else None
            Va_, _ = self.alloc([128, NT, 4, 130], BF16, f"Vaug{k}")
            rs = [[Res(f"{nm}{k}_{i}") for i in range(NT)] for nm in ("QT", "KT", "QRT", "Va")]
            P.op("pool", lambda e, Va_=Va_: e.memset(Va_.rearrange("p a b c -> p (a b c)"), 1.0), W=rs[3])
            sets.append((QT_, rs[0], KT_, rs[1], QRT_, rs[2], Va_, rs[3]))
        SZ, SZr = self.alloc([128, NT, 512], BF16, "SZ")
        OG, _ = self.alloc([128, NT, 512], BF16, "OG")
        OGr = [Res(f"OG{i}") for i in range(NT)]
        OTs, OTsr = self.alloc([128, 4, T], BF16, "OTs")
        qw = 768 if isA else 512
        qtl = [self.alloc([128, qw], BF16, f"qtl{i}") for i in range(2)]
        ktl = [self.alloc([128, 512], BF16, f"ktl{i}") for i in range(2)]
        vtl = [self.alloc([128, 512], BF16, f"vtl{i}") for i in range(2)]
        PT = [(self.alloc([128, 512], BF16, f"PT{i}")[0], [Res(f"PT{i}_{k}") for k in range(4)]) for i in range(5)]
        rv = [self.alloc([128, 1], F32, f"rv{i}") for i in range(4)]
        ptc = [0]
        rvc = [0]
        sset = [0]
        ldc = [0]

        def load_ops(hg, S):
            QT, QTr, KT, KTr, QRT, QRTr, Va, Var = S
            if isA:
                qsrc, ksrc, vsrc = self.QA, self.KN, self.VA
                qo, ko, vo = hg * 768, hg * 512, hg * 512
            else:
                qsrc = ksrc = vsrc = self.PROJ
                qo, ko, vo = O_CQ2 + hg * 512, O_CK + hg * 512, O_CV + hg * 512

            def one(i):
                q_, qr = qtl[ldc[0] % 2]
                k_, kr = ktl[ldc[0] % 2]
                v_, vr = vtl[ldc[0] % 2]
                ldc[0] += 1
                rows = slice(i * 128, (i + 1) * 128)
                self.load(q_, qsrc[rows, qo:qo + qw], [qr])
                self.load(k_, ksrc[rows, ko:ko + 512], [kr])
                self.load(v_, vsrc[rows, vo:vo + 512], [vr])
                tcols = slice(i * 128, (i + 1) * 128)
                if isA:
                    self.transposes(q_, qr, [h * 192 for h in range(4)], lambda g0, n: QT[:, g0:g0 + n, tcols], QTr[i],
                                    "dve", banks=(0, 4))
                    self.transposes(q_, qr, [h * 192 + 128 for h in range(4)], lambda g0, n: QRT[0:64, g0:g0 + n, tcols],
                                    QRTr[i], "dve", rows=64, banks=(0, 4))
                else:
                    self.transposes(q_, qr, 4, lambda g0, n: QT[:, g0:g0 + n, tcols], QTr[i], "dve", banks=(0, 4))
                self.transposes(k_, kr, 4, lambda g0, n: KT[:, g0:g0 + n, tcols], KTr[i], "dve", banks=(0, 4))
                P.op("pool", lambda e, v_=v_, i=i, Va=Va: e.tensor_copy(
                    out=Va[:, i, :, 0:128], in_=v_.rearrange("p (h d) -> p h d", h=4, d=128)), R=[vr], W=[Var[i]])

            return [lambda i=i: one(i) for i in range(NT)]

        for f in load_ops(0, sets[0]):
            f()
        for hg in range(4):
            QT, QTr, KT, KTr, QRT, QRTr, Va, Var = sets[hg % 2]
            nxt = load_ops(hg + 1, sets[(hg + 1) % 2]) if hg < 3 else []
            zoff = O_AZ if isA else O_CZ
            self.load(SZ, self.PROJ[:, zoff + hg * 512: zoff + (hg + 1) * 512].rearrange("(n p) d -> p n d", p=128), [SZr])
            P.op("act", lambda e: e.activation(out=SZ, in_=SZ, func=AF.Silu), R=[SZr], W=[SZr])
            for hh in range(4):
                h = hg * 4 + hh
                for Qb in range(4):
                    st = sset[0]
                    sset[0] ^= 1

                    def oacc(t):
                        b = 4 + st * 2 + t // 2
                        return self.ps[b][:, (t % 2) * 256:(t % 2) * 256 + 129], self.psr[b], self.ps[b], (t % 2) * 256

                    def emit_scores(j):
                        r = max(0, j - 4 * Qb)
                        n0 = r * 128
                        ncol = 512 - n0
                        b = self.nb(0, 4)
                        ps, psr = self.ps[b], self.psr[b]
                        q0 = Qb * 512 + n0
                        qres = QTr[4 * Qb + r:4 * Qb + 4]
                        P.op("pe", lambda e, ps=ps, j=j, q0=q0, ncol=ncol, hh=hh, KT=KT, QT=QT: e.matmul(
                            ps[:, 0:ncol], lhsT=KT[:, hh, j * 128:(j + 1) * 128], rhs=QT[:, hh, q0:q0 + ncol],
                            start=True, stop=not isA), R=[KTr[j]] + qres, W=[psr])
                        if isA:
                            P.op("pe", lambda e, ps=ps, j=j, q0=q0, ncol=ncol, hh=hh, QRT=QRT: e.matmul(
                                ps[:, 0:ncol], lhsT=KRT[0:64, 0, j * 128:(j + 1) * 128],
                                rhs=QRT[0:64, hh, q0:q0 + ncol], start=False, stop=True),
                                 R=[self.KRTr] + QRTr[4 * Qb + r:4 * Qb + 4], W=[psr])
                        return ps, psr, r, ncol

                    def emit_rest(j, sc):
                        ps, psr, r, ncol = sc
                        p_, prs = PT[ptc[0] % 5]
                        ptc[0] += 1
                        if isA:
                            P.op("act", lambda e, ps=ps, p_=p_, ncol=ncol: e.activation(
                                out=p_[:, 0:ncol], in_=ps[:, 0:ncol], func=AF.Exp, scale=scale), R=[psr], W=prs[0:4 - r])
                        else:
                            for t in range(r, 4):
                                i = 4 * Qb + t
                                bi = i * (i + 1) // 2 + j
                                cc = (t - r) * 128
                                P.op("act", lambda e, ps=ps, p_=p_, cc=cc, bi=bi, h=h: e.activation(
                                    out=p_[:, cc:cc + 128], in_=ps[:, cc:cc + 128], func=AF.Exp, scale=scale,
                                    bias=biasC[:, bi, h:h + 1]), R=[psr, bcr], W=[prs[t - r]])
                        if j >= 4 * Qb:
                            P.op("pool", lambda e, p_=p_: e.tensor_tensor(out=p_[:, 0:128], in0=p_[:, 0:128],
                                                                          in1=self.maskT, op=ALU.mult),
                                 R=[prs[0], self.cres], W=[prs[0]])
                        for t in range(r, 4):
                            i = 4 * Qb + t
                            oa, oar, _, _ = oacc(t)
                            cc = (t - r) * 128
                            P.op("pe", lambda e, oa=oa, p_=p_, cc=cc, j=j, hh=hh, i=i, t=t, Va=Va: e.matmul(
                                oa, lhsT=p_[:, cc:cc + 128], rhs=Va[:, j, hh, 0:129], start=(j == 0 and t % 2 == 0),
                                stop=(j == i), skip_group_check=(t % 2 == 1)), R=[prs[t - r], Var[j]], W=[oar])

                    nj = 4 * Qb + 4
                    pend = [emit_scores(0), emit_scores(1)]
                    for j in range(nj):
                        if j + 2 < nj:
                            pend.append(emit_scores(j + 2))
                        emit_rest(j, pend.pop(0))
                    for t in range(4):
                        i = 4 * Qb + t
                        oa, oar, bank, c0 = oacc(t)
                        r_, rr = rv[rvc[0] % 4]
                        rvc[0] += 1
                        P.op("dve", lambda e, r_=r_, bank=bank, c0=c0: e.reciprocal(out=r_, in_=bank[:, c0 + 128:c0 + 129]),
                             R=[oar], W=[rr])
                        P.op("dve", lambda e, r_=r_, bank=bank, c0=c0, i=i, hh=hh: e.scalar_tensor_tensor(
                            out=OG[:, i, hh * 128:(hh + 1) * 128], in0=bank[:, c0:c0 + 128], scalar=r_[:, 0:1],
                            in1=SZ[:, i, hh * 128:(hh + 1) * 128], op0=ALU.mult, op1=ALU.mult),
                             R=[oar, rr, SZr], W=[OGr[i]])
                    if nxt:
                        nxt.pop(0)()
            for i in range(NT):
                tcols = slice(i * 128, (i + 1) * 128)
                self.transposes(OG[:, i, :], OGr[i], 4, lambda g0, n: OTs[:, g0:g0 + n, tcols], OTsr, "dve", banks=(0, 4))
            r0 = br * 2048 + hg * 512
            self.store(self.OT[r0:r0 + 512, :].rearrange("(h p) t -> p h t", p=128), OTs, [OTsr])

    def attn_swa(self, l):
        P, io = self.P, self.io
        self.new_phase(name=f"L{l}_swa")
        E, Er = self.alloc([128, 32, 2, 128], BF16, "E")
        esink, esr = self.alloc([128, 32], F32, "esink")
        self.load(esink, io["sink_bc"][l * 128:(l + 1) * 128, :], [esr])
        P.op("act", lambda e: e.activation(out=esink, in_=esink, func=AF.Exp), R=[esr], W=[esr])
        rb = [self.alloc([128, 8, 256], F32, f"rb{i}") for i in range(2)]
        bm = self.bmask.rearrange("p (o c) -> p o c", o=1, c=256).broadcast_to([128, 8, 256])
        for g in range(4):
            r_, rr = rb[g % 2]
            self.load(r_.rearrange("p a b -> p (a b)"), io["relb"][:, g * 2048:(g + 1) * 2048], [rr])
            P.op("act", lambda e, r_=r_: e.activation(out=r_, in_=r_, func=AF.Exp), R=[rr], W=[rr])
            P.op("dve", lambda e, r_=r_, g=g: e.tensor_tensor(
                out=E[:, g * 8:(g + 1) * 8, :, :].rearrange("p h a b -> p h (a b)"), in0=r_, in1=bm, op=ALU.mult),
                 R=[rr, self.cres], W=[Er])
        QT, _ = self.alloc([128, 4, T], BF16, "QTb")
        QTr = [Res(f"QTb{i}") for i in range(NT)]
        KT, _ = self.alloc([128, 1, T], BF16, "KTb")
        KTr = [Res(f"KTb{i}") for i in range(NT)]
        Vb, Vbr = self.alloc([128, NT, 66], BF16, "Vb")
        SZ, SZr = self.alloc([128, NT, 512], BF16, "SZb")
        OG, _ = self.alloc([128, NT, 512], BF16, "OGb")
        OGr = [Res(f"OGb{i}") for i in range(NT)]
        OTs, OTsr = self.alloc([128, 4, T], BF16, "OTsb")
        qtl = [self.alloc([128, 512], BF16, f"bq{i}") for i in range(2)]
        ktl = [self.alloc([128, 128], BF16, f"bk{i}") for i in range(2)]
        PTc = [self.alloc([128, 8, 128], BF16, f"PTc{i}") for i in range(2)]
        PTp = [self.alloc([128, 8, 128], BF16, f"PTp{i}") for i in range(2)]
        den = [self.alloc([128, 4, 1], F32, f"den{i}") for i in range(4)]
        tmp = [self.alloc([128, 4, 64], F32, f"tmpb{i}") for i in range(4)]
        P.op("pool", lambda e: e.memset(Vb.rearrange("p a b -> p (a b)"), 1.0), W=[Vbr])
        fc = [0]
        for kv in range(4):
            self.load(SZ, self.PROJ[:, O_BZ + kv * 512: O_BZ + (kv + 1) * 512].rearrange("(n p) d -> p n d", p=128), [SZr])
            P.op("act", lambda e: e.activation(out=SZ, in_=SZ, func=AF.Silu), R=[SZr], W=[SZr])
            self.load(Vb[:, :, 0:64], self.PROJ[:, O_BV + kv * 64: O_BV + (kv + 1) * 64].rearrange("(n p) d -> p n d", p=128),
                      [Vbr])
            for i in range(NT):
                q_, qr = qtl[i % 2]
                k_, kr = ktl[i % 2]
                rows = slice(i * 128, (i + 1) * 128)
                tcols = slice(i * 128, (i + 1) * 128)
                self.load(q_, self.PROJ[rows, O_BQ + kv * 512: O_BQ + (kv + 1) * 512], [qr])
                self.load(k_[:, 0:64], self.PROJ[rows, O_BK + kv * 64: O_BK + (kv + 1) * 64], [kr])
                self.load(k_[:, 64:128], self.PROJ[rows, O_BK + kv * 64: O_BK + (kv + 1) * 64], [kr])
                self.transposes(q_, qr, 4, lambda g0, n: QT[:, g0:g0 + n, tcols], QTr[i], "dve", banks=(6, 8))
                self.transposes(k_, kr, 1, lambda g0, n: KT[:, g0:g0 + n, tcols], KTr[i], "dve", banks=(6, 8))
            for n in range(NT):
                pc, pcr = PTc[n % 2]
                pp, ppr = PTp[n % 2]
                cols = slice(n * 128, (n + 1) * 128)
                pcols = slice((n - 1) * 128, n * 128)
                for which in ((1, 0) if n > 0 else (1,)):
                    bb = 0 if which == 1 else 2
                    kc_ = cols if which == 1 else pcols
                    kres = KTr[n] if which == 1 else KTr[n - 1]
                    dst, dstr = (pc, pcr) if which == 1 else (pp, ppr)
                    for hh in range(8):
                        m, r = hh // 2, hh % 2
                        ps, psr = self.ps[bb + r], self.psr[bb + r]
                        oc = m * 128
                        P.op("pe", lambda e, ps=ps, oc=oc, r=r, m=m, kc_=kc_, cols=cols: e.matmul(
                            ps[:, oc:oc + 128], lhsT=KT[64 * r:64 * r + 64, 0, kc_], rhs=QT[64 * r:64 * r + 64, m, cols],
                            start=True, stop=True), R=[kres, QTr[n]], W=[psr])
                    for r in range(2):
                        ps, psr = self.ps[bb + r], self.psr[bb + r]
                        P.op("act", lambda e, ps=ps, dst=dst, r=r: e.activation(
                            out=dst[:, r:8:2, :], in_=ps[:, 0:512].rearrange("p (a b) -> p a b", a=4, b=128),
                            func=AF.Exp, scale=0.125), R=[psr], W=[dstr])
                    P.op("dve", lambda e, dst=dst, which=which, kv=kv: e.tensor_tensor(
                        out=dst, in0=dst, in1=E[:, kv * 8:(kv + 1) * 8, which, :], op=ALU.mult), R=[dstr, Er], W=[dstr])
                for hh in range(8):
                    ob, obr = self.ps[4 + hh // 4], self.psr[4 + hh // 4]
                    oc = (hh % 4) * 66
                    if n > 0:
                        P.op("pe", lambda e, ob=ob, oc=oc, hh=hh, pp=pp, n=n: e.matmul(
                            ob[:, oc:oc + 65], lhsT=pp[:, hh, :], rhs=Vb[:, n - 1, 0:65], start=True, stop=False),
                             R=[ppr, Vbr], W=[obr])
                    P.op("pe", lambda e, ob=ob, oc=oc, hh=hh, pc=pc, n=n: e.matmul(
                        ob[:, oc:oc + 65], lhsT=pc[:, hh, :], rhs=Vb[:, n, 0:65], start=(n == 0), stop=True),
                         R=[pcr, Vbr], W=[obr])
                for half in range(2):
                    ob, obr = self.ps[4 + half], self.psr[4 + half]
                    d_, dr = den[fc[0] % 4]
                    t_, tr = tmp[fc[0] % 4]
                    fc[0] += 1
                    ov = ob[:, 0:264].rearrange("p (h d) -> p h d", h=4, d=66)
                    h0 = kv * 8 + half * 4
                    P.op("dve", lambda e, ov=ov, d_=d_, h0=h0: e.tensor_tensor(
                        out=d_, in0=ov[:, :, 64:65], in1=esink[:, h0:h0 + 4].rearrange("p (h o) -> p h o", h=4, o=1),
                        op=ALU.add), R=[obr, esr], W=[dr])
                    P.op("dve", lambda e, d_=d_: e.reciprocal(out=d_, in_=d_), R=[dr], W=[dr])
                    P.op("dve", lambda e, ov=ov, d_=d_, t_=t_: e.tensor_tensor(
                        out=t_, in0=ov[:, :, 0:64], in1=d_.broadcast_to([128, 4, 64]), op=ALU.mult), R=[obr, dr], W=[tr])
                    c0 = half * 256
                    P.op("pool", lambda e, t_=t_, c0=c0, n=n: e.tensor_tensor(
                        out=OG[:, n, c0:c0 + 256], in0=t_.rearrange("p h d -> p (h d)"), in1=SZ[:, n, c0:c0 + 256],
                        op=ALU.mult), R=[tr, SZr], W=[OGr[n]])
            for i in range(NT):
                tcols = slice(i * 128, (i + 1) * 128)
                self.transposes(OG[:, i, :], OGr[i], 4, lambda g0, n: OTs[:, g0:g0 + n, tcols], OTsr, "dve", banks=(6, 8))
            r0 = 2048 + kv * 512
            self.store(self.OT[r0:r0 + 512, :].rearrange("(h p) t -> p h t", p=128), OTs, [OTsr])

    def combine(self, l):
        P, io = self.P, self.io
        for half in range(2):
            self.new_phase(name=f"L{l}_combine")
            OTh, OThr = self.alloc([128, 48, 1024], BF16, "OTh")
            ov = self.OT[:, half * 1024:(half + 1) * 1024].rearrange("(c p) t -> p c t", p=128)
            for k0 in range(0, 48, 8):
                self.load(OTh[:, k0:k0 + 8, :], ov[:, k0:k0 + 8, :], [OThr])
            CW = 384
            NS = (D + CW - 1) // CW
            wb = [self.alloc([128, 48, CW], BF16, f"wc{i}") for i in range(2)]
            gt = [self.alloc([128, 3, CW], BF16, f"gt{i}") for i in range(3)]
            t1 = [self.alloc([128, CW], F32, f"t1_{i}") for i in range(2)]
            t2 = [self.alloc([128, CW], F32, f"t2_{i}") for i in range(2)]
            ys = [self.alloc([128, CW], BF16, f"ys{i}") for i in range(3)]
            pcs = [(self.alloc([128, 3, CW], F32, f"pc{i}")[0], [Res(f"pc{i}_{k}") for k in range(3)]) for i in range(3)]
            wsrc = [io[k][l * 2048:(l + 1) * 2048, :].rearrange("(c p) n -> p c n", p=128) for k in ("w_pa", "w_pb", "w_pc")]

            def issue(s):
                w_, wr = wb[s % 2]
                c0 = s * CW
                w = min(CW, D - c0)
                for b3 in range(3):
                    for k0 in (0, 8):
                        P.op("pool", lambda e, w_=w_, b3=b3, k0=k0, c0=c0, w=w: e.dma_start(
                            out=w_[:, b3 * 16 + k0:b3 * 16 + k0 + 8, 0:w], in_=wsrc[b3][:, k0:k0 + 8, c0:c0 + w]),
                             W=[wr], dma="wld")

            cnt = 0
            issue(0)
            for s in range(NS):
                if s + 1 < NS:
                    issue(s + 1)
                w_, wr = wb[s % 2]
                c0 = s * CW
                w = min(CW, D - c0)
                for ti in range(8):
                    i = half * 8 + ti
                    g_, gr = gt[cnt % 3]
                    a_, ar = t1[cnt % 2]
                    b_, br_ = t2[cnt % 2]
                    y_, yr = ys[cnt % 3]
                    cnt += 1
                    self.load(g_[:, :, 0:w], self.G[i * 128:(i + 1) * 128, :].rearrange("p (b f) -> p b f", b=3)[:, :, c0:c0 + w],
                              [gr])
                    pss = []
                    for b3 in range(3):
                        bk = self.nb(0, 6)
                        ps, psr = self.ps[bk], self.psr[bk]
                        pss.append((ps, psr))
                        for c in range(16):
                            P.op("pe", lambda e, ps=ps, w_=w_, b3=b3, c=c, ti=ti, w=w: e.matmul(
                                ps[:, 0:w], lhsT=OTh[:, b3 * 16 + c, ti * 128:(ti + 1) * 128], rhs=w_[:, b3 * 16 + c, 0:w],
                                start=(c == 0), stop=(c == 15)), R=[wr, OThr], W=[psr])
                    pc_, pcr = pcs[cnt % 3]
                    for b3 in range(3):
                        P.op("act", lambda e, pc_=pc_, b3=b3, ps=pss[b3][0], w=w: e.copy(out=pc_[:, b3, 0:w], in_=ps[:, 0:w]),
                             R=[pss[b3][1]], W=[pcr[b3]])
                    P.op("dve", lambda e, a_=a_, g_=g_, pc_=pc_, w=w: e.tensor_tensor(
                        out=a_[:, 0:w], in0=pc_[:, 0, 0:w], in1=g_[:, 0, 0:w], op=ALU.mult), R=[pcr[0], gr], W=[ar])
                    P.op("pool", lambda e, b_=b_, g_=g_, pc_=pc_, w=w: e.tensor_tensor(
                        out=b_[:, 0:w], in0=pc_[:, 1, 0:w], in1=g_[:, 1, 0:w], op=ALU.mult), R=[pcr[1], gr], W=[br_])
                    P.op("dve", lambda e, a_=a_, b_=b_, w=w: e.tensor_tensor(out=a_[:, 0:w], in0=a_[:, 0:w], in1=b_[:, 0:w],
                                                                           op=ALU.add), R=[ar, br_], W=[ar])
                    P.op("pool", lambda e, b_=b_, g_=g_, pc_=pc_, w=w: e.tensor_tensor(
                        out=b_[:, 0:w], in0=pc_[:, 2, 0:w], in1=g_[:, 2, 0:w], op=ALU.mult), R=[pcr[2], gr], W=[br_])
                    P.op("dve", lambda e, a_=a_, b_=b_, y_=y_, w=w: e.tensor_tensor(out=y_[:, 0:w], in0=a_[:, 0:w], in1=b_[:, 0:w],
                                                                                  op=ALU.add), R=[ar, br_], W=[yr])
                    self.store(self.Y[i * 128:(i + 1) * 128, c0:c0 + w], y_[:, 0:w], [yr])

    def out_proj(self, l):
        P, io = self.P, self.io
        self.new_phase(name=f"L{l}_yT")
        mark = self.base
        yT, _ = self.alloc([128, KC, T], BF16, "yT")
        yTr = [Res(f"yT{i}") for i in range(NT)]
        self.keep()
        yt = [self.alloc([128, D], BF16, f"yt{i}") for i in range(2)]
        for i in range(NT):
            y_, yr = yt[i % 2]
            self.load(y_, self.Y[i * 128:(i + 1) * 128, :], [yr])
            self.transposes(y_, yr, KC, lambda g0, n, i=i: yT[:, g0:g0 + n, i * 128:(i + 1) * 128], yTr[i],
                            "act" if i % 2 else "dve")
        self.new_phase(name=f"L{l}_w_o")
        ev = self.evac_store(self.Z, dt=F32, nbuf=2)
        self.gemm(yT, yTr, KC, io["w_o"][l * D:(l + 1) * D, :], D, 512, ev)
        self.new_phase(base=mark, name=f"L{l}_postnorm")

    def postnorm(self, l, xsrc, dst):
        P, io = self.P, self.io
        gbc, gr = self.alloc([128, D], F32, "gpost")
        self.load(gbc, io["post_bc"][l * 128:(l + 1) * 128, :], [gr])
        zs = [self.alloc([128, D], F32, f"zs{i}") for i in range(2)]
        xs = [self.alloc([128, D], F32, f"xr{i}") for i in range(2)]
        junk, jr = self.alloc([128, D], BF16, "junk")
        ss, ssr = self.alloc([128, 4], F32, "ss")
        for i in range(NT):
            z_, zr = zs[i % 2]
            x_, xr = xs[i % 2]
            rows = slice(i * 128, (i + 1) * 128)
            self.load(z_, self.Z[rows, :], [zr])
            self.load(x_, xsrc[rows, :], [xr])
            P.op("act", lambda e, z_=z_: e.activation(out=junk, in_=z_, func=AF.Square, accum_out=ss[:, 0:1]),
                 R=[zr], W=[jr, ssr])
            self.rstd_from_ss(ss, ssr, D, 0)
            P.op("dve", lambda e, z_=z_: e.scalar_tensor_tensor(out=z_, in0=z_, scalar=ss[:, 2:3], in1=gbc,
                                                                op0=ALU.mult, op1=ALU.mult), R=[zr, ssr, gr], W=[zr])
            P.op("pool", lambda e, z_=z_, x_=x_: e.tensor_tensor(out=z_, in0=z_, in1=x_, op=ALU.add), R=[zr, xr], W=[zr])
            self.store(dst[rows, :], z_, [zr])

    def build(self):
        io = self.io
        self.setup()
        xsrc = io["x"]
        for l in range(self.depth):
            hT, hTr = self.prenorm(l, xsrc)
            stg, wbufs = self.proj_in(l, hT, hTr)
            if self.stop == "proj":
                return
            self.gates(l, hT, hTr, stg, wbufs)
            self.new_phase(base=self.mark)
            if self.stop == "gates":
                return
            mark = self.mla(l)
            if self.stop == "mla":
                return
            self.attn_full(l, "A")
            self.new_phase(base=mark)
            if self.stop == "A":
                return
            self.attn_swa(l)
            if self.stop == "B":
                return
            self.attn_full(l, "C")
            if self.stop == "C":
                return
            self.combine(l)
            if self.stop == "Y":
                return
            self.out_proj(l)
            if self.stop == "Z":
                return
            dst = io["out"] if l == self.depth - 1 else self.X1
            self.postnorm(l, xsrc, dst)
            xsrc = dst


IN_SPECS = {
    "x": ([T, D], F32), "pos": ([128, NT], I32),
    "w_in": ([DEPTH * NSLAB_IN * 128, KC * 512], F32), "w_merge": ([DEPTH * D, 3 * D], F32), "w_o": ([DEPTH * D, D], F32),
    "w_uq": ([DEPTH * 1536, 3072], F32), "w_uk": ([DEPTH * 512, 2048], F32), "w_uv": ([DEPTH * 512, 2048], F32),
    "w_pa": ([DEPTH * 2048, D], F32), "w_pb": ([DEPTH * 2048, D], F32), "w_pc": ([DEPTH * 2048, D], F32),
    "pre_bc": ([DEPTH * 128, D], F32), "post_bc": ([DEPTH * 128, D], F32),
    "qn_bc": ([DEPTH * 128, 1536], F32), "kvn_bc": ([DEPTH * 128, 512], F32),
    "sink_bc": ([DEPTH * 128, 32], F32), "bf_bc": ([DEPTH * 128, 16], F32),
    "relb": ([128, 32 * 2 * 128], F32),
    "cstb": ([128, 512], BF16), "cst32": ([128, 288], F32),
}


def build_nc(depth=DEPTH, stop=None, dumps=(), scopes=False):
    nc = bass.Bass("TRN2", target_bir_lowering=False)
    io = {}
    for k, (shp, dt) in IN_SPECS.items():
        io[k] = nc.dram_tensor(k, list(shp), dt, kind="ExternalInput").ap()
    io["out"] = nc.dram_tensor("out", [T, D], F32, kind="ExternalOutput").ap()
    es = ExitStack()
    with es:
        B = Builder(nc, es, io, depth=depth, stop=stop, dumps=dumps)
        B.P.scopes = scopes
        B.build()
        B.P.final_wait()
        B.P.emit()
    return nc


def t5_bucket_np(dist):
    d = np.maximum(dist, 0)
    large = 16 + (np.log(np.maximum(d, 1).astype(np.float32) / np.float32(16)) / np.float32(math.log(128 / 16))
                  * np.float32(16)).astype(np.int32)
    large = np.minimum(large, 31)
    return np.where(d < 16, d, large)


def prep_shared(inputs):
    f = lambda a: np.ascontiguousarray(np.asarray(a, dtype=np.float32))
    sh = {}
    wi = np.zeros((DEPTH, KC, 128, NSLAB_IN * 512), np.float32)
    wi[:, :, :, :IN_DIM] = f(inputs["w_in"]).reshape(DEPTH, KC, 128, IN_DIM)
    sh["w_in"] = np.ascontiguousarray(
        wi.reshape(DEPTH, KC, 128, NSLAB_IN, 512).transpose(0, 3, 2, 1, 4)).reshape(DEPTH * NSLAB_IN * 128, KC * 512)
    del wi
    sh["w_merge"] = f(inputs["w_merge"]).reshape(DEPTH * D, 3 * D)
    sh["w_o"] = f(inputs["w_o"]).reshape(DEPTH * D, D)
    sh["w_uq"] = f(inputs["w_uq"]).reshape(DEPTH * 1536, 3072)
    sh["w_uk"] = f(inputs["w_uk"]).reshape(DEPTH * 512, 2048)
    sh["w_uv"] = f(inputs["w_uv"]).reshape(DEPTH * 512, 2048)
    sh["w_pa"] = f(inputs["w_proj_a"]).reshape(DEPTH * 2048, D)
    sh["w_pb"] = f(inputs["w_proj_b"]).reshape(DEPTH * 2048, D)
    sh["w_pc"] = f(inputs["w_proj_c"]).reshape(DEPTH * 2048, D)

    def bc(a):
        a = f(a)
        return np.ascontiguousarray(np.broadcast_to(a[:, None, :], (DEPTH, 128, a.shape[1]))).reshape(DEPTH * 128, -1)

    sh["pre_bc"] = bc(inputs["pre_norm"])
    sh["post_bc"] = bc(inputs["post_norm"])
    sh["qn_bc"] = bc(inputs["q_a_norm"])
    sh["kvn_bc"] = bc(inputs["kv_a_norm"])
    sh["sink_bc"] = bc(inputs["sinks"])
    sh["bf_bc"] = bc(inputs["b_f"])
    s_i = np.arange(128)[:, None]
    q_i = np.arange(128)[None, :]
    rt = f(inputs["rel_table"])
    d_cur = q_i - s_i
    d_prev = q_i - s_i + 128
    relb = np.zeros((128, 32, 2, 128), np.float32)
    relb[:, :, 0, :] = rt[t5_bucket_np(d_prev)].transpose(0, 2, 1)
    relb[:, :, 1, :] = rt[t5_bucket_np(d_cur)].transpose(0, 2, 1)
    sh["relb"] = relb.reshape(128, 32 * 2 * 128)
    cstb = np.zeros((128, 512), np.float32)
    cstb[:, 0:128] = np.eye(128)
    cstb[:, 128:256] = (s_i <= q_i)
    cstb[:, 256:384] = (s_i > q_i)
    cstb[:, 384:512] = (s_i <= q_i)
    sh["cstb"] = cstb.astype(ml_dtypes.bfloat16)
    c32 = np.zeros((128, 288), np.float32)
    c32[:, 0:128] = (s_i <= q_i)
    c32[:, 128:256] = 1.0
    half = 32
    c32[:, 256:288] = (np.float32(10000.0) ** (-np.arange(half, dtype=np.float32) / np.float32(half)))[None, :]
    sh["cst32"] = c32
    return sh


def core_inputs(inputs, sh, c):
    m = dict(sh)
    m["x"] = np.ascontiguousarray(np.asarray(inputs["x"][c], dtype=np.float32))
    p = np.asarray(inputs["positions"][c], dtype=np.int32)
    m["pos"] = np.ascontiguousarray(p.reshape(NT, 128).T)
    return m


def kernel(**inputs):
    sh = prep_shared(inputs)
    nc = build_nc()
    n = 8
    in_maps = [core_inputs(inputs, sh, c) for c in range(n)]
    res = run_bass_kernel_spmd(nc, in_maps, core_ids=list(range(n)))
    return np.stack([r["out"] for r in res.results], axis=0).astype(np.float32)
```
